# Optimizing a Trainium2 kernel written in Bass

```python
import functools
import jax, jax.numpy as jnp
from jax import lax
import numpy as np

D_MODEL = 1024
BATCH = 4
SEQ = 4096
DEPTH = 2
DEC_BATCH = 128
DEC_SEQ = 8
PAST_LEN = 8192
PAGE_SIZE = 128

N_EVEN = (DEPTH + 1) // 2
N_ODD = DEPTH // 2
POOL_WINDOWS = (2, 4, 8, 16)
POOL_GROUPS = len(POOL_WINDOWS)
POOL_GROUP_DIM = 128
POOL_DIM = POOL_GROUPS * POOL_GROUP_DIM
POOL_HIST = max(POOL_WINDOWS) - 1
N_HEADS = 8
QK_NOPE_DIM = 64
QK_ROPE_DIM = 32
QK_HEAD_DIM = QK_NOPE_DIM + QK_ROPE_DIM
V_HEAD_DIM = 64
KV_RANK = 256
Q_DIM = N_HEADS * QK_HEAD_DIM
ATTN_OUT_DIM = N_HEADS * V_HEAD_DIM
ROPE_BASE = 10000.0
SOFTMAX_SCALE = QK_HEAD_DIM ** -0.5
Q_BLOCK = 128
IN_EVEN_DIM = POOL_DIM + Q_DIM + KV_RANK + QK_ROPE_DIM
MIX_DIM = POOL_DIM + ATTN_OUT_DIM
RNN_DIM = D_MODEL
RNN_BLOCKS = 8
RNN_BLOCK_DIM = RNN_DIM // RNN_BLOCKS
CONV_WIDTH = 4
LRU_C = 8.0
D_FF = 4 * D_MODEL
NORM_EPS = 1e-6

kernel_name = 'hybrid_pool_mla_rglru_decode_step'


def rms_norm(x, g):
    xf = x.astype(jnp.float32)
    y = xf * lax.rsqrt(jnp.mean(xf * xf, axis=-1, keepdims=True) + NORM_EPS)
    return (y * g.astype(jnp.float32)).astype(x.dtype)


def rope(x, pos):
    half = QK_ROPE_DIM // 2
    inv = ROPE_BASE ** (-jnp.arange(half, dtype=jnp.float32) / half)
    ang = pos.astype(jnp.float32)[:, None] * inv[None, :]
    ang = ang.reshape((1, ang.shape[0]) + (1,) * (x.ndim - 3) + (half,))
    cos, sin = jnp.cos(ang), jnp.sin(ang)
    xf = x.astype(jnp.float32)
    x1, x2 = xf[..., :half], xf[..., half:]
    return jnp.concatenate([x1 * cos - x2 * sin, x1 * sin + x2 * cos], axis=-1).astype(x.dtype)


def pool_mix(u_ext, pos, w_pool, pool_scale):
    T = pos.shape[0]
    B = u_ext.shape[0]
    uf = u_ext.astype(jnp.float32)
    cs = jnp.cumsum(uf, axis=1)
    cs = jnp.concatenate([jnp.zeros_like(cs[:, :1]), cs], axis=1)
    end = cs[:, POOL_HIST + 1:]
    u_new = uf[:, POOL_HIST:]
    groups = []
    for g, w in enumerate(POOL_WINDOWS):
        sl = slice(g * POOL_GROUP_DIM, (g + 1) * POOL_GROUP_DIM)
        start = cs[:, POOL_HIST + 1 - w: POOL_HIST + 1 - w + T, sl]
        cnt = jnp.minimum(pos + 1, w).astype(jnp.float32)[None, :, None]
        groups.append((end[..., sl] - start) / cnt - u_new[..., sl])
    d = jnp.stack(groups, axis=2)
    y = jnp.einsum('btgc,gcd->btgd', d, w_pool.astype(jnp.float32)).reshape(B, T, POOL_DIM)
    return (y * pool_scale.astype(jnp.float32)).astype(u_ext.dtype)


def k_nope_from_latent(c, w_uk, g_k_nope):
    return rms_norm(jnp.einsum('bkc,chd->bkhd', c, w_uk), g_k_nope)


def mla_core(qn, qr, k_nope, k_rope, c, mask, w_uv):
    b, tq = qn.shape[:2]
    s = (jnp.einsum('bqhd,bkhd->bhqk', qn, k_nope)
         + jnp.einsum('bqhr,bkr->bhqk', qr, k_rope)).astype(jnp.float32) * SOFTMAX_SCALE
    s = jnp.where(mask[None, None], s, -jnp.inf)
    p = jax.nn.softmax(s, axis=-1)
    lat = jnp.einsum('bhqk,bkc->bqhc', p.astype(c.dtype), c)
    return jnp.einsum('bqhc,chv->bqhv', lat, w_uv).reshape(b, tq, ATTN_OUT_DIM)


def mla_attend_prompt(qn, qr, c, kr, w_uk, g_k_nope, w_uv):
    B, S = qn.shape[:2]
    k_nope = k_nope_from_latent(c, w_uk, g_k_nope)
    key_pos = jnp.arange(S)

    def block(i):
        q0 = i * Q_BLOCK
        qn_b = lax.dynamic_slice_in_dim(qn, q0, Q_BLOCK, axis=1)
        qr_b = lax.dynamic_slice_in_dim(qr, q0, Q_BLOCK, axis=1)
        mask = (q0 + jnp.arange(Q_BLOCK))[:, None] >= key_pos[None, :]
        return mla_core(qn_b, qr_b, k_nope, kr, c, mask, w_uv)

    out = lax.map(block, jnp.arange(S // Q_BLOCK))
    return out.transpose(1, 0, 2, 3).reshape(B, S, ATTN_OUT_DIM)


def mla_attend_sample(qn, qr, c, kr, ckv_pool, krope_pool, layer, page_table, w_uk, g_k_nope, w_uv):
    T = qn.shape[1]
    n_past = page_table.shape[1] * PAGE_SIZE
    key_pos = jnp.arange(n_past + T)
    q_pos = n_past + jnp.arange(T)
    mask = key_pos[None, :] <= q_pos[:, None]

    def one(args):
        qn_b, qr_b, c_b, kr_b, pt_b = args
        c_all = jnp.concatenate([ckv_pool[layer, pt_b].reshape(n_past, KV_RANK).astype(c_b.dtype), c_b], axis=0)[None]
        kr_all = jnp.concatenate([krope_pool[layer, pt_b].reshape(n_past, QK_ROPE_DIM).astype(kr_b.dtype), kr_b], axis=0)[None]
        k_nope = k_nope_from_latent(c_all, w_uk, g_k_nope)
        return mla_core(qn_b[None], qr_b[None], k_nope, kr_all, c_all, mask, w_uv)[0]

    return lax.map(one, (qn, qr, c, kr, page_table))


def even_mixer(x, pos, pool_hist, attend, g_mix, w_in, g_q_nope, g_q_rope, g_ckv, g_k_rope, w_pool, pool_scale, w_out):
    B, T, _ = x.shape
    z = rms_norm(x, g_mix) @ w_in
    u = z[..., :POOL_DIM]
    q = z[..., POOL_DIM:POOL_DIM + Q_DIM].reshape(B, T, N_HEADS, QK_HEAD_DIM)
    c = z[..., POOL_DIM + Q_DIM:POOL_DIM + Q_DIM + KV_RANK]
    kr = z[..., POOL_DIM + Q_DIM + KV_RANK:]
    qn = rms_norm(q[..., :QK_NOPE_DIM], g_q_nope)
    qr = rope(rms_norm(q[..., QK_NOPE_DIM:], g_q_rope), pos)
    c = rms_norm(c, g_ckv)
    kr = rope(rms_norm(kr, g_k_rope), pos)
    u_ext = jnp.concatenate([pool_hist.astype(u.dtype), u], axis=1)
    pool_out = pool_mix(u_ext, pos, w_pool, pool_scale)
    attn_out = attend(qn, qr, c, kr)
    out = jnp.concatenate([pool_out, attn_out], axis=-1) @ w_out
    return out, c, kr, u_ext[:, -POOL_HIST:]


def linear_scan(a, b, h0):
    b = b.at[:, 0].add(a[:, 0] * h0)

    def comb(l, r):
        al, bl = l
        ar, br = r
        return al * ar, ar * bl + br

    _, h = lax.associative_scan(comb, (a, b), axis=1)
    return h


def odd_mixer(x, pos, conv_hist, h0, g_mix, w_in, conv_w, conv_b, w_ga, b_ga, w_gx, b_gx, lam, w_out):
    B, T, _ = x.shape
    z = rms_norm(x, g_mix) @ w_in
    gate, u = z[..., :RNN_DIM], z[..., RNN_DIM:]
    u_ext = jnp.concatenate([conv_hist.astype(u.dtype), u], axis=1)
    v = conv_b
    for k in range(CONV_WIDTH):
        v = v + u_ext[:, k:k + T] * conv_w[k]
    vb = v.reshape(B, T, RNN_BLOCKS, RNN_BLOCK_DIM)
    r = jax.nn.sigmoid((jnp.einsum('btni,nij->btnj', vb, w_ga).reshape(B, T, RNN_DIM) + b_ga).astype(jnp.float32))
    i = jax.nn.sigmoid((jnp.einsum('btni,nij->btnj', vb, w_gx).reshape(B, T, RNN_DIM) + b_gx).astype(jnp.float32))
    log_a = -LRU_C * r * jax.nn.softplus(-lam.astype(jnp.float32))
    a = jnp.exp(log_a)
    mult = jnp.where((pos == 0)[None, :, None], 1.0, jnp.sqrt(-jnp.expm1(2.0 * log_a)))
    h = linear_scan(a, mult * i * v.astype(jnp.float32), h0.astype(jnp.float32))
    y = (jax.nn.gelu(gate, approximate=True).astype(jnp.float32) * h).astype(x.dtype)
    return y @ w_out, u_ext[:, -(CONV_WIDTH - 1):], h[:, -1].astype(x.dtype)


def sqrelu_mlp(x, g, w_up, w_down):
    return jnp.square(jax.nn.relu(rms_norm(x, g) @ w_up)) @ w_down


def setup_inputs(seed: int = 0) -> dict:
    key = jax.random.key(seed)
    keys = iter(jax.random.split(key, 48))
    f32 = jnp.float32

    def normal(shape, scale=1.0):
        return scale * jax.random.normal(next(keys), shape, f32)

    def gain(shape):
        return 1.0 + 0.05 * jax.random.normal(next(keys), shape, f32)

    n_pages = PAST_LEN // PAGE_SIZE
    n_used = DEC_BATCH * n_pages
    n_pool = n_used + n_used // 4
    page_table = jax.random.permutation(next(keys), n_pool)[:n_used].reshape(DEC_BATCH, n_pages).astype(jnp.int32)
    a_c = jax.random.uniform(next(keys), (N_ODD, RNN_DIM), f32, 0.9, 0.999)
    s = a_c ** (1.0 / LRU_C)
    lru_lambda = jnp.log(s) - jnp.log1p(-s)
    return {
        'x_prompt': normal((BATCH, SEQ, D_MODEL)),
        'x_sample': normal((DEC_BATCH, DEC_SEQ, D_MODEL)),
        'cache_ckv': normal((N_EVEN, n_pool, PAGE_SIZE, KV_RANK)),
        'cache_krope': normal((N_EVEN, n_pool, PAGE_SIZE, QK_ROPE_DIM)),
        'state_pool': normal((N_EVEN, DEC_BATCH, POOL_HIST, POOL_DIM)),
        'state_conv': normal((N_ODD, DEC_BATCH, CONV_WIDTH - 1, RNN_DIM)),
        'state_lru': normal((N_ODD, DEC_BATCH, RNN_DIM), 0.5),
        'page_table': page_table,
        'norm_mix': gain((DEPTH, D_MODEL)),
        'w_in_even': normal((N_EVEN, D_MODEL, IN_EVEN_DIM), D_MODEL ** -0.5),
        'g_q_nope': gain((N_EVEN, QK_NOPE_DIM)),
        'g_q_rope': gain((N_EVEN, QK_ROPE_DIM)),
        'g_ckv': gain((N_EVEN, KV_RANK)),
        'g_k_rope': gain((N_EVEN, QK_ROPE_DIM)),
        'g_k_nope': gain((N_EVEN, QK_NOPE_DIM)),
        'w_uk': normal((N_EVEN, KV_RANK, N_HEADS, QK_NOPE_DIM), KV_RANK ** -0.5),
        'w_uv': normal((N_EVEN, KV_RANK, N_HEADS, V_HEAD_DIM), KV_RANK ** -0.5),
        'w_pool': normal((N_EVEN, POOL_GROUPS, POOL_GROUP_DIM, POOL_GROUP_DIM), POOL_GROUP_DIM ** -0.5),
        'pool_scale': gain((N_EVEN, POOL_DIM)),
        'w_out_even': normal((N_EVEN, MIX_DIM, D_MODEL), MIX_DIM ** -0.5),
        'w_in_rnn': normal((N_ODD, D_MODEL, 2 * RNN_DIM), D_MODEL ** -0.5),
        'conv_w': normal((N_ODD, CONV_WIDTH, RNN_DIM), CONV_WIDTH ** -0.5),
        'conv_b': normal((N_ODD, RNN_DIM), 0.01),
        'w_gate_a': normal((N_ODD, RNN_BLOCKS, RNN_BLOCK_DIM, RNN_BLOCK_DIM), RNN_BLOCK_DIM ** -0.5),
        'b_gate_a': normal((N_ODD, RNN_DIM), 0.01),
        'w_gate_x': normal((N_ODD, RNN_BLOCKS, RNN_BLOCK_DIM, RNN_BLOCK_DIM), RNN_BLOCK_DIM ** -0.5),
        'b_gate_x': normal((N_ODD, RNN_DIM), 0.01),
        'lru_lambda': lru_lambda,
        'w_out_rnn': normal((N_ODD, RNN_DIM, D_MODEL), RNN_DIM ** -0.5),
        'norm_ffn': gain((DEPTH, D_MODEL)),
        'w_up': normal((DEPTH, D_MODEL, D_FF), D_MODEL ** -0.5),
        'w_down': normal((DEPTH, D_FF, D_MODEL), D_FF ** -0.5),
    }


def reference(x_prompt, x_sample, cache_ckv, cache_krope, state_pool, state_conv, state_lru, page_table,
              norm_mix, w_in_even, g_q_nope, g_q_rope, g_ckv, g_k_rope, g_k_nope, w_uk, w_uv, w_pool, pool_scale,
              w_out_even, w_in_rnn, conv_w, conv_b, w_gate_a, b_gate_a, w_gate_x, b_gate_x, lru_lambda, w_out_rnn,
              norm_ffn, w_up, w_down):
    B, S = x_prompt.shape[:2]
    DB, T = x_sample.shape[:2]
    n_past = page_table.shape[1] * PAGE_SIZE
    pos_p = jnp.arange(S, dtype=jnp.int32)
    pos_s = n_past + jnp.arange(T, dtype=jnp.int32)
    yp, ys = x_prompt, x_sample
    ckv_p, kr_p, pool_p, conv_p, lru_p = [], [], [], [], []
    ckv_s, kr_s, pool_s, conv_s, lru_s = [], [], [], [], []
    for layer in range(DEPTH):
        j = layer // 2
        if layer % 2 == 0:
            ew = (norm_mix[layer], w_in_even[j], g_q_nope[j], g_q_rope[j], g_ckv[j], g_k_rope[j],
                  w_pool[j], pool_scale[j], w_out_even[j])
            attend_p = functools.partial(mla_attend_prompt, w_uk=w_uk[j], g_k_nope=g_k_nope[j], w_uv=w_uv[j])
            attend_s = functools.partial(mla_attend_sample, ckv_pool=cache_ckv, krope_pool=cache_krope, layer=j,
                                         page_table=page_table, w_uk=w_uk[j], g_k_nope=g_k_nope[j], w_uv=w_uv[j])
            hist0 = jnp.zeros((B, POOL_HIST, POOL_DIM), yp.dtype)
            mp, c, kr, ph = even_mixer(yp, pos_p, hist0, attend_p, *ew)
            ms, cs_, krs, phs = even_mixer(ys, pos_s, state_pool[j], attend_s, *ew)
            ckv_p.append(c); kr_p.append(kr); pool_p.append(ph)
            ckv_s.append(cs_); kr_s.append(krs); pool_s.append(phs)
        else:
            ow = (norm_mix[layer], w_in_rnn[j], conv_w[j], conv_b[j], w_gate_a[j], b_gate_a[j],
                  w_gate_x[j], b_gate_x[j], lru_lambda[j], w_out_rnn[j])
            conv0 = jnp.zeros((B, CONV_WIDTH - 1, RNN_DIM), yp.dtype)
            h0 = jnp.zeros((B, RNN_DIM), yp.dtype)
            mp, cvp, hp = odd_mixer(yp, pos_p, conv0, h0, *ow)
            ms, cvs, hs = odd_mixer(ys, pos_s, state_conv[j], state_lru[j], *ow)
            conv_p.append(cvp); lru_p.append(hp)
            conv_s.append(cvs); lru_s.append(hs)
        yp = yp + mp
        ys = ys + ms
        yp = yp + sqrelu_mlp(yp, norm_ffn[layer], w_up[layer], w_down[layer])
        ys = ys + sqrelu_mlp(ys, norm_ffn[layer], w_up[layer], w_down[layer])
    return (yp, ys,
            jnp.stack(ckv_p), jnp.stack(kr_p), jnp.stack(pool_p), jnp.stack(conv_p), jnp.stack(lru_p),
            jnp.stack(ckv_s), jnp.stack(kr_s), jnp.stack(pool_s), jnp.stack(conv_s), jnp.stack(lru_s))
```

```python
import numpy as np
import concourse.bass as bass
import concourse.mybir as mybir
from concourse.bass_utils import run_bass_kernel_spmd
from contextlib import ExitStack

F32 = mybir.dt.float32
BF16 = mybir.dt.bfloat16
I32 = mybir.dt.int32
AF = mybir.ActivationFunctionType
ALU = mybir.AluOpType
AX = mybir.AxisListType

ENGS = ("pe", "act", "dve", "pool", "sp")
SEM_ROT = 30000
SAME_ENGINE_SYNC = True
import os as _os0
CUT = int(_os0.environ.get('KDBG_CUT', '100000000'))
SKIP = set(int(x) for x in _os0.environ.get('KDBG_SKIP', '').split(',') if x)


class Buf:
    def __init__(self, name, t=None):
        self.name = name
        self.t = t
        self.w = None
        self.r = []
        self.dsem = None
        self.excl = False


class Sched:
    def __init__(self, nc, stack, nsem=96):
        self.nc = nc
        self.sems = [stack.enter_context(nc.semaphore(f"s{i}")) for i in range(nsem)]
        self.free = list(range(nsem))
        self.total = [0] * nsem
        self.cur = {}
        self.own = {e: set() for e in ENGS}
        for e in ENGS[:4]:
            self.cur[e] = self.free.pop(0)
            self.own[e].add(self.cur[e])
        self.waited = {e: {} for e in ENGS}
        self.streams = {e: [] for e in ENGS}
        self.stage_dsems = set()
        self.stage_bufs = []
        self.nops = 0

    def _ev_resolve(self, ev):
        s, v = ev
        if v is None:
            v = self.total[s]
        return s, v

    def _wait(self, eng, evs):
        need = {}
        for ev in evs:
            if ev is None:
                continue
            s, v = self._ev_resolve(ev)
            if v <= 0:
                continue
            if need.get(s, 0) < v:
                need[s] = v
        for s, v in need.items():
            if self.waited[eng].get(s, 0) >= v:
                continue
            if (not SAME_ENGINE_SYNC or eng == "pe") and s in self.own[eng]:
                continue
            self.waited[eng][s] = v
            sem = self.sems[s]
            self.streams[eng].append(lambda e, sem=sem, v=v: e.wait_ge(sem, v))

    def _deps(self, reads, writes, eng=None):
        evs = []
        for b in reads:
            evs.append(b.w)
            if b.excl:
                evs.extend(ev for ev in b.r if ev[0] not in self.own.get(eng, ()))
        for b in writes:
            evs.append(b.w)
            evs.extend(b.r)
        return evs

    def op(self, eng, fn, reads=(), writes=()):
        self.nops += 1
        if self.nops > CUT or self.nops in SKIP:
            return
        self._wait(eng, self._deps(reads, writes, eng))
        s = self.cur[eng]
        self.total[s] += 1
        v = self.total[s]
        sem = self.sems[s]
        self.streams[eng].append(lambda e, fn=fn, sem=sem: fn(e).then_inc(sem, 1))
        ev = (s, v)
        for b in reads:
            b.r.append(ev)
        for b in writes:
            b.w = ev
            b.r = []
        if v >= SEM_ROT:
            self.cur[eng] = self.free.pop(0)
            self.own[eng].add(self.cur[eng])

    def _dsem(self, b):
        if b.dsem is None:
            b.dsem = self.free.pop(0)
            self.stage_bufs.append(b)
        self.stage_dsems.add(b.dsem)
        return b.dsem

    def dma(self, q, fn, sb, load, extra_reads=()):
        self.nops += 1
        if self.nops > CUT or self.nops in SKIP:
            return
        if load:
            self._wait(q, self._deps(extra_reads, [sb]))
        else:
            self._wait(q, self._deps([sb] + list(extra_reads), []))
        s = self._dsem(sb)
        self.total[s] += 16
        sem = self.sems[s]
        self.streams[q].append(lambda e, fn=fn, sem=sem: fn(e).then_inc(sem, 16))
        ev = (s, None)
        if load:
            sb.w = ev
            sb.r = []
        else:
            sb.r.append(ev)

    def release(self, bufs):
        for b in bufs:
            if b.dsem is not None:
                self.free.append(b.dsem)
                b.dsem = None

    def end_stage(self, final=False):
        nc = self.nc
        for s in sorted(self.stage_dsems):
            v = self.total[s]
            if self.waited["sp"].get(s, 0) < v:
                self.waited["sp"][s] = v
                sem = self.sems[s]
                self.streams["sp"].append(lambda e, sem=sem, v=v: e.wait_ge(sem, v))
        streams = self.streams
        with nc.Block() as block:
            @block.tensor
            def _(e):
                for f in streams["pe"]:
                    f(e)

            @block.scalar
            def _(e):
                for f in streams["act"]:
                    f(e)

            @block.vector
            def _(e):
                for f in streams["dve"]:
                    f(e)

            @block.gpsimd
            def _(e):
                for f in streams["pool"]:
                    f(e)

            @block.sync
            def _(e):
                for f in streams["sp"]:
                    f(e)
        self.streams = {e: [] for e in ENGS}
        self.stage_dsems = set()
        for e in ENGS:
            for s in range(len(self.total)):
                self.waited[e][s] = self.total[s]


EPS = 1e-6
SCALE = 96 ** -0.5
import os as _os
NT = int(_os.environ.get('KDBG_NT', '32'))
NB = 16
import ml_dtypes
import os
NPBF = ml_dtypes.bfloat16


class TL:
    def __init__(self, t, name):
        self.t = t
        self.b = Buf(name)

    def __getitem__(self, k):
        return self.t[k]


def build(upto=9, small_cache=False):
    NPOOL = 2 if small_cache else 10240
    nc = bass.Bass("TRN2", target_bir_lowering=False)

    def DI(name, shape, dt=F32):
        return nc.dram_tensor(name, shape, dt, kind="ExternalInput").ap()

    def DO(name, shape, dt=F32):
        return nc.dram_tensor(name, shape, dt, kind="ExternalOutput").ap()

    xp = DI("xp", [4096, 1024]); xs = DI("xs", [128, 1024])
    ckv = DI("ckv", [NPOOL * 128, 256]); krp = DI("krp", [NPOOL * 128, 32])
    ptab = DI("ptab", [128, NB * 64], I32)
    spool = DI("spool", [NB * 15, 512]); sconv = DI("sconv", [NB * 3, 1024]); slru = DI("slru", [NB, 1024])
    w_in = DI("w_in", [1024, 1568]); w_uk = DI("w_uk", [256, 512]); w_ukT = DI("w_ukT", [64, 8 * 256])
    w_uv = DI("w_uv", [256, 512]); w_pool = DI("w_pool", [4, 128, 128]); w_out = DI("w_out", [1024, 1024])
    w_inr = DI("w_inr", [1024, 2048]); w_ga = DI("w_ga", [8, 128, 128]); w_gx = DI("w_gx", [8, 128, 128])
    w_outr = DI("w_outr", [1024, 1024]); w_up = DI("w_up", [2, 1024, 4096]); w_dn = DI("w_dn", [2, 4096, 1024])
    vecs = DI("vecs", [128, 100]); rowv = DI("rowv", [128, 448])
    c_ident = DI("c_ident", [128, 128]); c_cs = DI("c_cs", [4096 + 128, 32])
    c_band = DI("c_band", [128, 20 * 128]); c_bandh = DI("c_bandh", [128, 8 * 128])
    c_mask = DI("c_mask", [128, 128]); c_mnew = DI("c_mnew", [128, NB * 64]); c_pcol = DI("c_pcol", [128, 1])

    yp = DO("yp", [4096, 1024]); ys = DO("ys", [128, 1024])
    o_ckvp = DO("o_ckvp", [4096, 256]); o_krp = DO("o_krp", [4096, 32]); o_poolp = DO("o_poolp", [15, 512])
    o_convp = DO("o_convp", [3, 1024]); o_lrup = DO("o_lrup", [1024])
    o_ckvs = DO("o_ckvs", [128, 256]); o_krs = DO("o_krs", [128, 32]); o_pools = DO("o_pools", [NB, 15, 512])
    o_convs = DO("o_convs", [NB, 3, 1024]); o_lrus = DO("o_lrus", [NB, 1024])
    X1 = nc.dram_tensor("X1", [4224, 1024], F32).ap()
    X2 = nc.dram_tensor("X2", [4224, 1024], F32).ap()
    X3 = nc.dram_tensor("X3", [4224, 1024], F32).ap()

    def rows(i):
        return slice(i * 128, (i + 1) * 128)

    with ExitStack() as gst:
        S = Sched(nc, gst)

        uid = [0]

        def mk(stack, name, shape, dt):
            uid[0] += 1
            name = f"{name}_{uid[0]}"
            return TL(stack.enter_context(nc.sbuf_tensor(name, shape, dt)), name)

        def mkps(stack, name):
            t = TL(stack.enter_context(nc.psum_tensor(name, [128, 512], F32)), name)
            t.b.excl = True
            return t

        def bs(xs_):
            return [x.b for x in xs_]

        def P(fn, r=(), w=()): S.op("pe", fn, bs(r), bs(w))
        def A(fn, r=(), w=()): S.op("act", fn, bs(r), bs(w))
        def V(fn, r=(), w=()): S.op("dve", fn, bs(r), bs(w))
        def G(fn, r=(), w=()): S.op("pool", fn, bs(r), bs(w))
        def LD(tl, out_ap, in_ap, q="sp", **kw): S.dma(q, lambda e: e.dma_start(out=out_ap, in_=in_ap, **kw), tl.b, True)
        def ST(tl, out_ap, in_ap, q="sp", **kw): S.dma(q, lambda e: e.dma_start(out=out_ap, in_=in_ap, **kw), tl.b, False)

        ident32 = mk(gst, "ident32", [128, 128], F32)
        ident = mk(gst, "ident", [128, 128], BF16)
        vec = mk(gst, "vec", [128, 100], F32)
        row = mk(gst, "row", [128, 448], F32)
        PS = [mkps(gst, f"B{i}") for i in range(8)]
        LD(ident32, ident32[:], c_ident[:, :])
        LD(vec, vec[:], vecs[:, :])
        LD(row, row[:], rowv[:, :])
        G(lambda e: e.tensor_copy(out=ident[:], in_=ident32[:]), [ident32], [ident])
        GM0, GF0, GM1, GF1, PSC, CW, CB, BGA, BGX, LAM = 0, 8, 16, 24, 32, 36, 68, 76, 84, 92
        RQN, RQR, RCKV, RKR, RKN = 0, 64, 96, 352, 384

        def bank_bf(p):
            return p.t[:].bitcast(BF16)

        def load_w(stack, name, src_ap, K, N, gcol=None, q="sp", stg=None):
            wt = mk(stack, name, [128, K, N], BF16)
            wt.kb = [Buf(f"{name}_{k}") for k in range(K)]
            CH = 1024
            n = 0
            for k in range(K):
                for c0 in range(0, N, CH):
                    c1 = min(N, c0 + CH)
                    s = stg[n % 2]
                    LD(s, s[:, 0:c1 - c0], src_ap[k * 128:(k + 1) * 128, c0:c1], q=("sp" if n % 2 == 0 else "act"))
                    if gcol is None:
                        if n % 2 == 0:
                            S.op("pool", lambda e, s=s, k=k, c0=c0, c1=c1: e.tensor_copy(out=wt[:, k, c0:c1], in_=s[:, 0:c1 - c0]), [s.b], [wt.b])
                        else:
                            S.op("dve", lambda e, s=s, k=k, c0=c0, c1=c1: e.tensor_copy(out=wt[:, k, c0:c1], in_=s[:, 0:c1 - c0]), [s.b], [wt.b])
                    else:
                        eng = "pool" if n % 2 == 0 else "dve"
                        S.op(eng, lambda e, s=s, k=k, c0=c0, c1=c1: e.tensor_scalar(out=wt[:, k, c0:c1], in0=s[:, 0:c1 - c0], scalar1=vec[:, gcol + k:gcol + k + 1], scalar2=None, op0=ALU.mult), [s.b, vec.b], [wt.b])
                    n += 1
            return wt

        def rmsnorm(x, xn, junk, ss):
            A(lambda e: e.activation(out=junk[:], in_=x[:], func=AF.Square, accum_out=ss[:, 0:1]), [x], [junk, ss])
            V(lambda e: e.tensor_scalar(out=ss[:, 1:2], in0=ss[:, 0:1], scalar1=1.0 / 1024, scalar2=EPS, op0=ALU.mult, op1=ALU.add), [ss], [ss])
            A(lambda e: e.activation(out=ss[:, 1:2], in_=ss[:, 1:2], func=AF.Sqrt), [ss], [ss])
            V(lambda e: e.reciprocal(out=ss[:, 1:2], in_=ss[:, 1:2]), [ss], [ss])
            V(lambda e: e.tensor_scalar(out=xn[:], in0=x[:], scalar1=ss[:, 1:2], scalar2=None, op0=ALU.mult), [x, ss], [xn])

        def transpose8(xn, xnT, pb):
            pbb = bank_bf(pb)
            for k in range(8):
                P(lambda e, k=k: e.transpose(pbb[:, k * 128:(k + 1) * 128], xn[:, k * 128:(k + 1) * 128], ident[:]), [xn, ident], [pb])
            V(lambda e: e.tensor_copy(out=xnT[:].rearrange("p k t -> p (k t)"), in_=pbb[:, :]), [pb], [xnT])

        def rstd_cols(ss, c0, c1, n):
            V(lambda e: e.tensor_scalar(out=ss[:, c0:c1], in0=ss[:, c0:c1], scalar1=1.0 / n, scalar2=EPS, op0=ALU.mult, op1=ALU.add), [ss], [ss])

        def sqrt_recip(ss, c0, c1):
            A(lambda e: e.activation(out=ss[:, c0:c1], in_=ss[:, c0:c1], func=AF.Sqrt), [ss], [ss])
            V(lambda e: e.reciprocal(out=ss[:, c0:c1], in_=ss[:, c0:c1]), [ss], [ss])

        def rope(dst, src, cs, H, tmp):
            cosb = cs[:, 0:16].unsqueeze(1).broadcast_to([128, H, 16])
            sinb = cs[:, 16:32].unsqueeze(1).broadcast_to([128, H, 16])
            x1 = src(0, 16); x2 = src(16, 32)
            t = tmp.t[:, 0:H, :]
            V(lambda e: e.tensor_tensor(out=t[:, :, 0:16], in0=x2, in1=sinb, op=ALU.mult), [src.tl, cs], [tmp])
            V(lambda e: e.tensor_tensor(out=t[:, :, 16:32], in0=x2, in1=cosb, op=ALU.mult), [src.tl, cs], [tmp])
            V(lambda e: e.tensor_tensor(out=dst(0, 16), in0=x1, in1=cosb, op=ALU.mult), [src.tl, cs], [dst.tl])
            V(lambda e: e.tensor_tensor(out=dst(16, 32), in0=x1, in1=sinb, op=ALU.mult), [src.tl, cs], [dst.tl])
            V(lambda e: e.tensor_tensor(out=dst(0, 16), in0=dst(0, 16), in1=t[:, :, 0:16], op=ALU.subtract), [dst.tl, tmp], [dst.tl])
            V(lambda e: e.tensor_tensor(out=dst(16, 32), in0=dst(16, 32), in1=t[:, :, 16:32], op=ALU.add), [dst.tl, tmp], [dst.tl])

        class APM:
            def __init__(self, tl, f):
                self.tl = tl; self.f = f
            def __call__(self, lo, hi):
                return self.f(lo, hi)

        S.end_stage()
        if upto <= 0:
            return nc

        def l0_stage(SAMPLE):
          with ExitStack() as st:
            stg = [mk(st, f"stg{j}", [128, 1024], F32) for j in range(2)]
            win = load_w(st, "win", w_in, 8, 1568, GM0, stg=stg)
            wout = load_w(st, "wout", w_out, 8, 1024, stg=stg)
            wuk = load_w(st, "wuk", w_uk, 2, 512, stg=stg)
            wuv = load_w(st, "wuv", w_uv, 2, 512, stg=stg)
            wpl = mk(st, "wpl", [128, 4, 128], BF16)
            s32 = stg[0]
            LD(s32, s32[:, 0:512].rearrange("p (g n) -> p g n", g=4), w_pool.rearrange("g p n -> p g n"))
            G(lambda e: e.tensor_copy(out=wpl[:].rearrange("p g n -> p (g n)"), in_=s32[:, 0:512]), [s32], [wpl])
            band = mk(st, "band", [128, 20, 128], BF16)
            for q4 in range(5):
                LD(s32, s32[:, 0:512], c_band[:, q4 * 512:(q4 + 1) * 512])
                G(lambda e, q4=q4: e.tensor_copy(out=band[:, q4 * 4:q4 * 4 + 4, :].rearrange("p a n -> p (a n)"), in_=s32[:, 0:512]), [s32], [band])
            maskc = mk(st, "maskc", [128, 128], BF16)
            LD(s32, s32[:, 0:128], c_mask[:, :])
            G(lambda e: e.tensor_copy(out=maskc[:], in_=s32[:, 0:128]), [s32], [maskc])
            if SAMPLE:
                wukT = mk(st, "wukT", [64, 8, 256], BF16)
                for q2 in range(2):
                    LD(s32, s32[0:64, :], w_ukT[:, q2 * 1024:(q2 + 1) * 1024])
                    G(lambda e, q2=q2: e.tensor_copy(out=wukT[:, q2 * 4:q2 * 4 + 4, :].rearrange("p h n -> p (h n)"), in_=s32[0:64, :]), [s32], [wukT])
                bandh = mk(st, "bandh", [128, 8, 128], BF16)
                mnew = mk(st, "mnew", [128, NB, 64], BF16)
                LD(s32, s32[:, 0:1024], c_bandh[:, :])
                G(lambda e: e.tensor_copy(out=bandh[:].rearrange("p a n -> p (a n)"), in_=s32[:, 0:1024]), [s32], [bandh])
                LD(s32, s32[:, 0:1024], c_mnew[:, :])
                G(lambda e: e.tensor_copy(out=mnew[:].rearrange("p a n -> p (a n)"), in_=s32[:, 0:1024]), [s32], [mnew])
            else:
                bandh = band
                KT = mk(st, "KT", [96, 8, 4096], BF16)
                KTb = [Buf(f"KT{i}") for i in range(NT)]
                VT = mk(st, "VT", [128, NT, 8, 65], BF16)
                VTb = [Buf(f"VT{i}") for i in range(NT)]
                S.op("pool", lambda e: e.memset(VT[:].rearrange("p a h c -> p (a h c)"), 1.0), [], VTb)

            xr = [mk(st, f"x{j}", [128, 1024], F32) for j in range(1)]
            junk = mk(st, "junk", [128, 1056], F32)
            ssx = mk(st, "ssx", [128, 2], F32)
            xn = mk(st, "xn", [128, 1024], BF16)
            xnT = mk(st, "xnT", [128, 8, 128], BF16)
            ub = [mk(st, f"ub{j}", [128, 512], BF16) for j in range(2)]
            zf = mk(st, "zf", [128, 1056], F32)
            sq = junk
            ssz = mk(st, "ssz", [128, 18], F32)
            qn32 = mk(st, "qn32", [128, 8, 96], F32)
            rtmp = mk(st, "rtmp", [128, 8, 32], F32)
            qfb = mk(st, "qfb", [128, 8, 96], BF16)
            qT = mk(st, "qT", [96, 8, 128], BF16)
            cn = [mk(st, f"cn{j}", [128, 256], F32) for j in range(2)]
            krn = mk(st, "krn", [128, 1, 32], F32)
            kro = [mk(st, f"kro{j}", [128, 1, 32], F32) for j in range(2)]
            cnb = mk(st, "cnb", [128, 256], BF16)
            cnT = mk(st, "cnT", [128, 2, 128], BF16)
            ksq = mk(st, "ksq", [128, 512], F32)
            ssk = mk(st, "ssk", [128, 8], F32)
            kfb = mk(st, "kfb", [128, 8, 96], BF16)
            cs = [mk(st, f"cs{j}", [128, 32], F32) for j in range(2)]
            pT = [mk(st, f"pT{j}", [128, 4, 128], BF16) for j in range(2)]
            orec = mk(st, "orec", [128, 8], F32)
            attb = mk(st, "attb", [128, 8, 64], BF16)
            dTb = mk(st, "dTb", [128, 4, 128], BF16)
            mixT = mk(st, "mixT", [128, 8, 128], BF16)
            xo = [mk(st, f"xo{j}", [128, 1024], F32) for j in range(1)]
            u32 = xo[0]
            if SAMPLE:
              hist = [mk(st, f"hist{j}", [128, 512], F32) for j in range(2)]
              histb = [mk(st, f"histb{j}", [128, 512], BF16) for j in range(2)]
              qgb = mk(st, "qgb", [128, 8, 64], BF16)
              qgT = mk(st, "qgT", [64, 8, 128], BF16)
              qrT = mk(st, "qrT", [32, 8, 128], BF16)
              QL = mk(st, "QL", [128, 2, NB, 64], BF16)
              qrB = mk(st, "qrB", [32, NB, 64], BF16)
              pidx = mk(st, "pidx", [128, NB * 64], I32)
              cpg = [mk(st, f"cpg{j}", [128, 256], F32) for j in range(3)]
              kpg = [mk(st, f"kpg{j}", [128, 32], F32) for j in range(3)]
              cb = [mk(st, f"cb{j}", [128, 257], BF16) for j in range(2)]
              krb2 = [mk(st, f"krb{j}", [128, 32], BF16) for j in range(2)]
              cT2 = [mk(st, f"cT{j}", [128, 2, 128], BF16) for j in range(2)]
              krT2 = [mk(st, f"krT{j}", [32, 128], BF16) for j in range(2)]
              ssp2 = [mk(st, f"ssp{j}", [128, 8], F32) for j in range(2)]
              sc2 = [mk(st, f"sc{j}", [128, 64], F32) for j in range(2)]
              ppT2 = [mk(st, f"ppT{j}", [128, 64], BF16) for j in range(2)]
              ksq2 = [ksq, mk(st, "ksqB", [128, 512], F32)]
              lrec = mk(st, "lrec", [64, 1], F32)
              latb = mk(st, "latb", [64, 256], BF16)
              LT = mk(st, "LT", [128, 2, 8, 128], BF16)
              for j in range(2):
                  G(lambda e, j=j: e.memset(cb[j][:, 256:257], 1.0), [], [cb[j]])

            def l0_tile(i):
                sample = (i == NT)
                x = xr[0]
                src = xs if sample else xp[rows(i), :]
                LD(x, x[:], src[:, :] if sample else src)
                c_ = cs[i % 2]
                LD(c_, c_[:], c_cs[rows(i), :], q="act")
                S.op("act", lambda e: e.activation(out=junk[:, 0:1024], in_=x[:], func=AF.Square, accum_out=ssx[:, 0:1]), [x.b], [junk.b, ssx.b])
                V(lambda e: e.tensor_scalar(out=ssx[:, 1:2], in0=ssx[:, 0:1], scalar1=1.0 / 1024, scalar2=EPS, op0=ALU.mult, op1=ALU.add), [ssx], [ssx])
                A(lambda e: e.activation(out=ssx[:, 1:2], in_=ssx[:, 1:2], func=AF.Sqrt), [ssx], [ssx])
                V(lambda e: e.reciprocal(out=ssx[:, 1:2], in_=ssx[:, 1:2]), [ssx], [ssx])
                V(lambda e: e.tensor_scalar(out=xn[:], in0=x[:], scalar1=ssx[:, 1:2], scalar2=None, op0=ALU.mult), [x, ssx], [xn])
                transpose8(xn, xnT, PS[4])
                for n, (c0, c1) in enumerate([(0, 512), (512, 1024), (1024, 1536), (1536, 1568)]):
                    for k in range(8):
                        P(lambda e, n=n, k=k, c0=c0, c1=c1: e.matmul(PS[n][:, 0:c1 - c0], lhsT=xnT[:, k, :], rhs=win[:, k, c0:c1], start=(k == 0), stop=(k == 7)), [xnT, win], [PS[n]])
                ucur = ub[i % 2]
                A(lambda e: e.activation(out=ucur[:], in_=PS[0][:, :], func=AF.Copy), [PS[0]], [ucur])
                if i >= NT - 1:
                    _v = _os0.environ.get("KDBG_VAR", "0")
                    if _v == "0":
                        V(lambda e: e.tensor_copy(out=u32[:, 0:512], in_=PS[0][:, :]), [PS[0]], [u32])
                    elif _v == "1":
                        V(lambda e: e.tensor_copy(out=zf[:, 0:512], in_=PS[0][:, :]), [PS[0]], [zf])
                    elif _v == "2":
                        V(lambda e: e.tensor_copy(out=u32[:, 0:512], in_=PS[1][:, :]), [PS[1]], [u32])
                    elif _v == "3":
                        A(lambda e: e.activation(out=u32[:, 0:512], in_=PS[0][:, :], func=AF.Copy), [PS[0]], [u32])
                    if i == NT - 1:
                        ST(u32, o_poolp[:, :], u32[113:128, 0:512])
                    else:
                        for b in range(NB):
                            ST(u32, o_pools[b, 7:15, :], u32[b * 8:(b + 1) * 8, 0:512])
                A(lambda e: e.activation(out=zf[:, 0:512], in_=PS[1][:, :], func=AF.Copy), [PS[1]], [zf])
                V(lambda e: e.tensor_copy(out=zf[:, 512:1024], in_=PS[2][:, :]), [PS[2]], [zf])
                V(lambda e: e.tensor_copy(out=zf[:, 1024:1056], in_=PS[3][:, 0:32]), [PS[3]], [zf])
                G(lambda e: e.tensor_tensor(out=sq[:], in0=zf[:], in1=zf[:], op=ALU.mult), [zf], [sq])
                sqq = sq[:, 0:768].rearrange("p (h d) -> p h d", h=8)
                V(lambda e: e.tensor_reduce(out=ssz[:, 0:8], in_=sqq[:, :, 0:64], axis=AX.X, op=ALU.add), [sq], [ssz])
                V(lambda e: e.tensor_reduce(out=ssz[:, 8:16], in_=sqq[:, :, 64:96], axis=AX.X, op=ALU.add), [sq], [ssz])
                V(lambda e: e.tensor_reduce(out=ssz[:, 16:17], in_=sq[:, 768:1024], axis=AX.X, op=ALU.add), [sq], [ssz])
                V(lambda e: e.tensor_reduce(out=ssz[:, 17:18], in_=sq[:, 1024:1056], axis=AX.X, op=ALU.add), [sq], [ssz])
                rstd_cols(ssz, 0, 8, 64); rstd_cols(ssz, 8, 16, 32); rstd_cols(ssz, 16, 17, 256); rstd_cols(ssz, 17, 18, 32)
                sqrt_recip(ssz, 0, 18)
                zq = zf[:, 0:768].rearrange("p (h d) -> p h d", h=8)
                V(lambda e: e.tensor_tensor(out=qn32[:, :, 0:64], in0=zq[:, :, 0:64], in1=ssz[:, 0:8].unsqueeze(2).broadcast_to([128, 8, 64]), op=ALU.mult), [zf, ssz], [qn32])
                V(lambda e: e.tensor_tensor(out=qn32[:, :, 0:64], in0=qn32[:, :, 0:64], in1=row[:, RQN:RQN + 64].unsqueeze(1).broadcast_to([128, 8, 64]), op=ALU.mult), [qn32, row], [qn32])
                V(lambda e: e.tensor_tensor(out=qn32[:, :, 64:96], in0=zq[:, :, 64:96], in1=ssz[:, 8:16].unsqueeze(2).broadcast_to([128, 8, 32]), op=ALU.mult), [zf, ssz], [qn32])
                V(lambda e: e.tensor_tensor(out=qn32[:, :, 64:96], in0=qn32[:, :, 64:96], in1=row[:, RQR:RQR + 32].unsqueeze(1).broadcast_to([128, 8, 32]), op=ALU.mult), [qn32, row], [qn32])
                G(lambda e: e.tensor_copy(out=qfb[:, :, 0:64], in_=qn32[:, :, 0:64]), [qn32], [qfb])
                rope(APM(qfb, lambda lo, hi: qfb[:, :, 64 + lo:64 + hi]), APM(qn32, lambda lo, hi: qn32[:, :, 64 + lo:64 + hi]), c_, 8, rtmp)
                cn_ = cn[i % 2]
                V(lambda e: e.scalar_tensor_tensor(out=cn_[:], in0=zf[:, 768:1024], scalar=ssz[:, 16:17], in1=row[:, RCKV:RCKV + 256], op0=ALU.mult, op1=ALU.mult), [zf, ssz, row], [cn_])
                ST(cn_, (o_ckvs[:, :] if sample else o_ckvp[rows(i), :]), cn_[:])
                V(lambda e: e.scalar_tensor_tensor(out=krn[:, 0, :], in0=zf[:, 1024:1056], scalar=ssz[:, 17:18], in1=row[:, RKR:RKR + 32], op0=ALU.mult, op1=ALU.mult), [zf, ssz, row], [krn])
                kro_ = kro[i % 2]
                rope(APM(kro_, lambda lo, hi: kro_[:, :, lo:hi]), APM(krn, lambda lo, hi: krn[:, :, lo:hi]), c_, 1, rtmp)
                ST(kro_, (o_krs[:, :] if sample else o_krp[rows(i), :]), kro_[:, 0, :])
                b4 = bank_bf(PS[4])
                if not sample:
                    G(lambda e: e.tensor_copy(out=cnb[:], in_=cn_[:]), [cn_], [cnb])
                    for k in range(2):
                        P(lambda e, k=k: e.transpose(b4[:, k * 128:(k + 1) * 128], cnb[:, k * 128:(k + 1) * 128], ident[:]), [cnb, ident], [PS[4]])
                    V(lambda e: e.tensor_copy(out=cnT[:].rearrange("p k t -> p (k t)"), in_=b4[:, 0:256]), [PS[4]], [cnT])
                    for k in range(2):
                        P(lambda e, k=k: e.matmul(PS[0][:, :], lhsT=cnT[:, k, :], rhs=wuk[:, k, :], start=(k == 0), stop=(k == 1)), [cnT, wuk], [PS[0]])
                    for k in range(2):
                        P(lambda e, k=k: e.matmul(PS[1][:, :], lhsT=cnT[:, k, :], rhs=wuv[:, k, :], start=(k == 0), stop=(k == 1)), [cnT, wuv], [PS[1]])
                    A(lambda e: e.activation(out=ksq[:], in_=PS[0][:, :], func=AF.Square), [PS[0]], [ksq])
                    V(lambda e: e.tensor_reduce(out=ssk[:], in_=ksq[:].rearrange("p (h d) -> p h d", h=8), axis=AX.X, op=ALU.add), [ksq], [ssk])
                    rstd_cols(ssk, 0, 8, 64); sqrt_recip(ssk, 0, 8)
                    V(lambda e: e.tensor_tensor(out=ksq[:].rearrange("p (h d) -> p h d", h=8), in0=PS[0][:, :].rearrange("p (h d) -> p h d", h=8), in1=ssk[:, 0:8].unsqueeze(2).broadcast_to([128, 8, 64]), op=ALU.mult), [PS[0], ssk], [ksq])
                    V(lambda e: e.tensor_tensor(out=kfb[:, :, 0:64], in0=ksq[:].rearrange("p (h d) -> p h d", h=8), in1=row[:, RKN:RKN + 64].unsqueeze(1).broadcast_to([128, 8, 64]), op=ALU.mult), [ksq, row], [kfb])
                    V(lambda e: e.tensor_copy(out=kfb[:, :, 64:96], in_=kro_[:, 0:1, :].broadcast_to([128, 8, 32])), [kro_], [kfb])
                    S.op("act", lambda e: e.activation(out=VT[:, i, :, 0:64], in_=PS[1][:, :].rearrange("p (h d) -> p h d", h=8), func=AF.Copy), [PS[1].b], [VTb[i]])
                    for h in range(8):
                        P(lambda e, h=h: e.transpose(b4[0:96, h * 128:(h + 1) * 128], kfb[:, h, :], ident[:]), [kfb, ident], [PS[4]])
                    S.op("dve", lambda e: e.tensor_copy(out=KT[:, :, rows(i)], in_=b4[0:96, :].rearrange("p (h t) -> p h t", h=8)), [PS[4].b], [KTb[i]])
                    for h in range(8):
                        P(lambda e, h=h: e.transpose(b4[0:96, h * 128:(h + 1) * 128], qfb[:, h, :], ident[:]), [qfb, ident], [PS[4]])
                    V(lambda e: e.tensor_copy(out=qT[:].rearrange("p h t -> p (h t)"), in_=b4[0:96, :]), [PS[4]], [qT])
                    g = 0
                    for h in range(8):
                        ob = PS[5 + h // 4]
                        oc = (h % 4) * 65
                        for j0 in range(0, i + 1, 4):
                            js = list(range(j0, min(i + 1, j0 + 4)))
                            sb_ = PS[2 + g % 2]; pt_ = pT[g % 2]; g += 1
                            for jj, j in enumerate(js):
                                S.op("pe", lambda e, jj=jj, j=j, h=h, sb_=sb_: e.matmul(sb_[:, jj * 128:(jj + 1) * 128], lhsT=KT[:, h, rows(j)], rhs=qT[:, h, :], start=True, stop=True), [KTb[j], qT.b], [sb_.b])
                            n = len(js)
                            A(lambda e, n=n, sb_=sb_, pt_=pt_: e.activation(out=pt_[:, 0:n, :].rearrange("p a t -> p (a t)"), in_=sb_[:, 0:n * 128], func=AF.Exp, scale=SCALE), [sb_], [pt_])
                            if js[-1] == i:
                                jj = len(js) - 1
                                G(lambda e, jj=jj, pt_=pt_: e.tensor_tensor(out=pt_[:, jj, :], in0=pt_[:, jj, :], in1=maskc[:], op=ALU.mult), [pt_, maskc], [pt_])
                            for jj, j in enumerate(js):
                                S.op("pe", lambda e, jj=jj, j=j, h=h, pt_=pt_, ob=ob, oc=oc: e.matmul(ob[:, oc:oc + 65], lhsT=pt_[:, jj, :], rhs=VT[:, j, h, :], start=(j == 0), stop=(j == i)), [pt_.b, VTb[j]], [ob.b])
                    for hb in range(2):
                        ob = PS[5 + hb]
                        ov = ob[:, 0:260].rearrange("p (h c) -> p h c", h=4)
                        V(lambda e, hb=hb, ov=ov: e.reciprocal(out=orec[:, hb * 4:hb * 4 + 4], in_=ov[:, :, 64]), [ob], [orec])
                        V(lambda e, hb=hb, ov=ov: e.tensor_tensor(out=attb[:, hb * 4:hb * 4 + 4, :], in0=ov[:, :, 0:64], in1=orec[:, hb * 4:hb * 4 + 4].unsqueeze(2).broadcast_to([128, 4, 64]), op=ALU.mult), [ob, orec], [attb])
                    for c in range(4):
                        P(lambda e, c=c: e.transpose(b4[:, c * 128:(c + 1) * 128], attb[:, 2 * c:2 * c + 2, :].rearrange("p h d -> p (h d)"), ident[:]), [attb, ident], [PS[4]])
                    V(lambda e: e.tensor_copy(out=mixT[:, 4:8, :].rearrange("p k t -> p (k t)"), in_=b4[:, 0:512]), [PS[4]], [mixT])
                else:
                    sample_attention(kro_, cn_)
                for g_ in range(4):
                    if sample:
                        mats = [(ucur, None, band[:, 16 + g_, :]), (histb[0], 128, bandh[:, 2 * g_, :]), (histb[1], 112, bandh[0:112, 2 * g_ + 1, :])]
                    elif i == 0:
                        mats = [(ucur, None, band[:, 4 + g_, :])]
                    else:
                        mats = [(ucur, None, band[:, g_, :]), (ub[(i - 1) % 2], None, band[:, 8 + g_, :])]
                    for mi, (ut, nr, bm) in enumerate(mats):
                        lhs = ut[:, g_ * 128:(g_ + 1) * 128] if nr is None else ut[0:nr, g_ * 128:(g_ + 1) * 128]
                        P(lambda e, lhs=lhs, bm=bm, g_=g_, mi=mi, nm=len(mats): e.matmul(PS[7][:, g_ * 128:(g_ + 1) * 128], lhsT=lhs, rhs=bm, start=(mi == 0), stop=(mi == nm - 1)), [ut, band, bandh], [PS[7]])
                V(lambda e: e.tensor_copy(out=dTb[:].rearrange("p g t -> p (g t)"), in_=PS[7][:, :]), [PS[7]], [dTb])
                for g_ in range(4):
                    P(lambda e, g_=g_: e.matmul(PS[7][:, g_ * 128:(g_ + 1) * 128], lhsT=wpl[:, g_, :], rhs=dTb[:, g_, :], start=True, stop=True), [wpl, dTb], [PS[7]])
                for g_ in range(4):
                    V(lambda e, g_=g_: e.tensor_scalar(out=mixT[:, g_, :], in0=PS[7][:, g_ * 128:(g_ + 1) * 128], scalar1=vec[:, PSC + g_:PSC + g_ + 1], scalar2=None, op0=ALU.mult), [PS[7], vec], [mixT])
                xo_ = xo[0]
                for n in range(2):
                    for k in range(8):
                        P(lambda e, n=n, k=k: e.matmul(PS[n][:, :], lhsT=mixT[:, k, :], rhs=wout[:, k, n * 512:(n + 1) * 512], start=(k == 0), stop=(k == 7)), [mixT, wout], [PS[n]])
                    V(lambda e, n=n: e.tensor_tensor(out=xo_[:, n * 512:(n + 1) * 512], in0=PS[n][:, :], in1=x[:, n * 512:(n + 1) * 512], op=ALU.add), [PS[n], x], [xo_])
                ST(xo_, X1[rows(i), :], xo_[:])

            def sample_attention(kro_, cn_):
                b4 = bank_bf(PS[4])
                LD(hist[0], hist[0][:, :], spool[0:128, :])
                LD(hist[1], hist[1][0:112, :], spool[128:240, :])
                G(lambda e: e.tensor_copy(out=histb[0][:], in_=hist[0][:]), [hist[0]], [histb[0]])
                G(lambda e: e.tensor_copy(out=histb[1][0:112, :], in_=hist[1][0:112, :]), [hist[1]], [histb[1]])
                for b in range(NB):
                    r0 = b * 15 + 8
                    hh = hist[0] if r0 < 128 else hist[1]
                    rr = r0 if r0 < 128 else r0 - 128
                    ST(hh, o_pools[b, 0:7, :], hh[rr:rr + 7, :])
                V(lambda e: e.tensor_tensor(out=qgb[:], in0=qn32[:, :, 0:64], in1=row[:, RKN:RKN + 64].unsqueeze(1).broadcast_to([128, 8, 64]), op=ALU.mult), [qn32, row], [qgb])
                for h in range(8):
                    P(lambda e, h=h: e.transpose(b4[0:64, h * 128:(h + 1) * 128], qgb[:, h, :], ident[:]), [qgb, ident], [PS[4]])
                V(lambda e: e.tensor_copy(out=qgT[:].rearrange("p h t -> p (h t)"), in_=b4[0:64, :]), [PS[4]], [qgT])
                for h in range(8):
                    P(lambda e, h=h: e.transpose(b4[0:32, h * 128:(h + 1) * 128], qfb[:, h, 64:96], ident[:]), [qfb, ident], [PS[4]])
                V(lambda e: e.tensor_copy(out=qrT[:].rearrange("p h t -> p (h t)"), in_=b4[0:32, :]), [PS[4]], [qrT])
                V(lambda e: e.tensor_copy(out=qrB[:].rearrange("p b (h t) -> p b h t", h=8), in_=qrT[:].rearrange("p h (b t) -> p b h t", b=NB)), [qrT], [qrB])
                for k in range(2):
                    for hq in range(2):
                        pb_ = PS[hq]
                        for h4 in range(4):
                            h = hq * 4 + h4
                            P(lambda e, h=h, h4=h4, k=k, pb_=pb_: e.matmul(pb_[:, h4 * 128:(h4 + 1) * 128], lhsT=wukT[:, h, k * 128:(k + 1) * 128], rhs=qgT[:, h, :], start=True, stop=True), [wukT, qgT], [pb_])
                        V(lambda e, k=k, hq=hq, pb_=pb_: e.tensor_copy(out=QL[:, k, :, hq * 32:(hq + 1) * 32].rearrange("p b (h t) -> p b h t", h=4), in_=pb_[:, :].rearrange("p (h b t) -> p b h t", h=4, b=NB)), [pb_], [QL])
                LD(pidx, pidx[:], ptab[:, :])
                V(lambda e: e.tensor_scalar(out=pidx[:], in0=pidx[:], scalar1=128.0, scalar2=pcol[:, 0:1], op0=ALU.mult, op1=ALU.add), [pidx, pcol], [pidx])

                cnt = [0]

                def proc(c_tl, c_ap, k_tl, k_ap, b, first, last, mask_ap):
                    n = cnt[0]; cnt[0] += 1
                    cb_ = cb[n % 2]
                    krb = krb2[n % 2]; cT = cT2[n % 2]; krT = krT2[n % 2]; ssp = ssp2[n % 2]
                    sc = sc2[n % 2]; ppT = ppT2[n % 2]; ksq = ksq2[n % 2]
                    G(lambda e: e.tensor_copy(out=cb_[:, 0:256], in_=c_ap), [c_tl], [cb_])
                    G(lambda e: e.tensor_copy(out=krb[:], in_=k_ap), [k_tl], [krb])
                    for k in range(2):
                        P(lambda e, k=k: e.transpose(b4[:, k * 128:(k + 1) * 128], cb_[:, k * 128:(k + 1) * 128], ident[:]), [cb_, ident], [PS[4]])
                    P(lambda e: e.transpose(b4[0:32, 256:384], krb[:], ident[:]), [krb, ident], [PS[4]])
                    V(lambda e: e.tensor_copy(out=cT[:].rearrange("p k t -> p (k t)"), in_=b4[:, 0:256]), [PS[4]], [cT])
                    V(lambda e: e.tensor_copy(out=krT[:], in_=b4[0:32, 256:384]), [PS[4]], [krT])
                    kb_ = PS[n % 2]
                    for k in range(2):
                        P(lambda e, k=k: e.matmul(kb_[:, :], lhsT=cT[:, k, :], rhs=wuk[:, k, :], start=(k == 0), stop=(k == 1)), [cT, wuk], [kb_])
                    A(lambda e: e.activation(out=ksq[:], in_=kb_[:, :], func=AF.Square), [kb_], [ksq])
                    V(lambda e: e.tensor_reduce(out=ssp[:], in_=ksq[:].rearrange("p (h d) -> p h d", h=8), axis=AX.X, op=ALU.add), [ksq], [ssp])
                    rstd_cols(ssp, 0, 8, 64); sqrt_recip(ssp, 0, 8)
                    sb_ = PS[2 + n % 2]
                    for k in range(2):
                        P(lambda e, k=k: e.matmul(sb_[:, 0:64], lhsT=cT[:, k, :], rhs=QL[:, k, b, :], start=(k == 0), stop=(k == 1)), [cT, QL], [sb_])
                    P(lambda e: e.matmul(sb_[:, 64:128], lhsT=krT[:, :], rhs=qrB[:, b, :], start=True, stop=True), [krT, qrB], [sb_])
                    V(lambda e: e.tensor_tensor(out=sc[:].rearrange("p (h t) -> p h t", h=8), in0=sb_[:, 0:64].rearrange("p (h t) -> p h t", h=8), in1=ssp[:, 0:8].unsqueeze(2).broadcast_to([128, 8, 8]), op=ALU.mult), [sb_, ssp], [sc])
                    V(lambda e: e.tensor_tensor(out=sc[:], in0=sc[:], in1=sb_[:, 64:128], op=ALU.add), [sc, sb_], [sc])
                    A(lambda e: e.activation(out=ppT[:], in_=sc[:], func=AF.Exp, scale=SCALE), [sc], [ppT])
                    if mask_ap is not None:
                        V(lambda e: e.tensor_tensor(out=ppT[:], in0=ppT[:], in1=mask_ap, op=ALU.mult), [ppT, mnew], [ppT])
                    P(lambda e: e.matmul(PS[5][0:64, 0:257], lhsT=ppT[:], rhs=cb_[:, :], start=first, stop=last), [ppT, cb_], [PS[5]])

                for b in range(NB):
                    for j in range(int(os.environ.get('KDBG_PAGES', '64'))):
                        n = cnt[0]
                        cp = cpg[n % 3]; kp = kpg[n % 3]
                        col = b * 64 + j
                        S.dma("pool", lambda e, cp=cp, col=col: e.indirect_dma_start(out=cp[:, :], out_offset=None, in_=ckv[:, :], in_offset=bass.IndirectOffsetOnAxis(ap=pidx[:, col:col + 1], axis=0)), cp.b, True, extra_reads=[pidx.b])
                        S.dma("pool", lambda e, kp=kp, col=col: e.indirect_dma_start(out=kp[:, :], out_offset=None, in_=krp[:, :], in_offset=bass.IndirectOffsetOnAxis(ap=pidx[:, col:col + 1], axis=0)), kp.b, True, extra_reads=[pidx.b])
                        proc(cp, cp[:, :], kp, kp[:, :], b, j == 0, False, None)
                    proc(cn_, cn_[:, :], kro_, kro_[:, 0, :], b, False, True, mnew[:, b, :])
                    V(lambda e: e.reciprocal(out=lrec[:], in_=PS[5][0:64, 256:257]), [PS[5]], [lrec])
                    V(lambda e: e.tensor_scalar(out=latb[:], in0=PS[5][0:64, 0:256], scalar1=lrec[:, 0:1], scalar2=None, op0=ALU.mult), [PS[5], lrec], [latb])
                    for k in range(2):
                        P(lambda e, k=k: e.transpose(b4[:, 512 + k * 64:512 + (k + 1) * 64], latb[:, k * 128:(k + 1) * 128], ident[0:64, 0:64]), [latb, ident], [PS[4]])
                    V(lambda e, b=b: e.tensor_copy(out=LT[:, :, :, b * 8:(b + 1) * 8], in_=b4[:, 512:640].rearrange("p (k h t) -> p k h t", k=2, h=8)), [PS[4]], [LT])
                    S.end_stage()
                for h in range(8):
                    c = h // 2
                    for k in range(2):
                        P(lambda e, h=h, k=k, c=c: e.matmul(PS[6][(h % 2) * 64:(h % 2) * 64 + 64, c * 128:(c + 1) * 128], lhsT=wuv[:, k, h * 64:(h + 1) * 64], rhs=LT[:, k, h, :], start=(k == 0), stop=(k == 1)), [wuv, LT], [PS[6]])
                V(lambda e: e.tensor_copy(out=mixT[:, 4:8, :].rearrange("p k t -> p (k t)"), in_=PS[6][:, :]), [PS[6]], [mixT])

            pcol = mk(st, "pcol", [128, 1], F32)
            LD(pcol, pcol[:], c_pcol[:, :])
            if SAMPLE:
                l0_tile(NT)
            else:
                for i in range(NT):
                    l0_tile(i)
                    if i % 4 == 3 and i != NT - 1:
                        S.end_stage()
            S.end_stage()
            S.release([t for t in S.stage_bufs])
            S.stage_bufs = []

        l0_stage(False)
        if upto <= 1:
            return nc
        if os.environ.get("KDBG_SKIPS") != "1":
            l0_stage(True)
        if upto <= 2:
            return nc

        def mlp_stage(layer, Xin, Xout_fn, gcol):
            with ExitStack() as st:
                stg = [mk(st, f"mstg{j}", [128, 1024], F32) for j in range(2)]
                wup = load_w(st, "wup", w_up[layer], 8, 4096, gcol, stg=stg)
                wdn = load_w(st, "wdn", w_dn[layer], 32, 1024, stg=stg)
                xr = [mk(st, f"mx{j}", [128, 1024], F32) for j in range(2)]
                junk = mk(st, "mjunk", [128, 1024], F32)
                ssx = mk(st, "mssx", [128, 2], F32)
                xn = mk(st, "mxn", [128, 1024], BF16)
                xnT = mk(st, "mxnT", [128, 8, 128], BF16)
                rl = [mk(st, f"rl{j}", [128, 512], BF16) for j in range(2)]
                hT = mk(st, "hT", [128, 32, 128], BF16)
                xo = [mk(st, f"mxo{j}", [128, 1024], F32) for j in range(2)]
                for i in range(NT + 1):
                    x = xr[i % 2]
                    LD(x, x[:], Xin[rows(i), :])
                    rmsnorm(x, xn, junk, ssx)
                    transpose8(xn, xnT, PS[4])
                    for mg in range(8):
                        pb_ = PS[mg % 2]
                        for m4 in range(4):
                            m = mg * 4 + m4
                            for k in range(8):
                                P(lambda e, m=m, m4=m4, k=k, pb_=pb_: e.matmul(pb_[:, m4 * 128:(m4 + 1) * 128], lhsT=wup[:, k, m * 128:(m + 1) * 128], rhs=xnT[:, k, :], start=(k == 0), stop=(k == 7)), [wup, xnT], [pb_])
                        r_ = rl[mg % 2]
                        A(lambda e, pb_=pb_, r_=r_: e.activation(out=r_[:], in_=pb_[:, :], func=AF.Relu), [pb_], [r_])
                        G(lambda e, mg=mg, r_=r_: e.tensor_tensor(out=hT[:, mg * 4:mg * 4 + 4, :].rearrange("p m t -> p (m t)"), in0=r_[:], in1=r_[:], op=ALU.mult), [r_], [hT])
                    xo_ = xo[i % 2]
                    for n in range(2):
                        for m in range(32):
                            P(lambda e, n=n, m=m: e.matmul(PS[2 + n][:, :], lhsT=hT[:, m, :], rhs=wdn[:, m, n * 512:(n + 1) * 512], start=(m == 0), stop=(m == 31)), [hT, wdn], [PS[2 + n]])
                        V(lambda e, n=n, x=x, xo_=xo_: e.tensor_tensor(out=xo_[:, n * 512:(n + 1) * 512], in0=PS[2 + n][:, :], in1=x[:, n * 512:(n + 1) * 512], op=ALU.add), [PS[2 + n], x], [xo_])
                    ST(xo_, Xout_fn(i), xo_[:])
                    if i % 4 == 3:
                        S.end_stage()
                S.end_stage()
                S.release([t for t in S.stage_bufs]); S.stage_bufs = []

        mlp_stage(0, X1, lambda i: X2[rows(i), :], GF0)
        if upto <= 3:
            return nc

        with ExitStack() as st:
            stg = [mk(st, f"estg{j}", [128, 1024], F32) for j in range(2)]
            winr = load_w(st, "winr", w_inr, 8, 2048, GM1, stg=stg)
            woutr = load_w(st, "woutr", w_outr, 8, 1024, stg=stg)
            wga = mk(st, "wga", [128, 8, 128], BF16)
            wgx = mk(st, "wgx", [128, 8, 128], BF16)
            s32 = mk(st, "e_s32", [128, 1024], F32)
            LD(s32, s32[:, :].rearrange("p (g n) -> p g n", g=8), w_ga.rearrange("g p n -> p g n"))
            G(lambda e: e.tensor_copy(out=wga[:].rearrange("p g n -> p (g n)"), in_=s32[:, :]), [s32], [wga])
            LD(s32, s32[:, :].rearrange("p (g n) -> p g n", g=8), w_gx.rearrange("g p n -> p g n"))
            G(lambda e: e.tensor_copy(out=wgx[:].rearrange("p g n -> p (g n)"), in_=s32[:, :]), [s32], [wgx])
            sp8 = mk(st, "sp8", [128, 16], F32)
            A(lambda e: e.activation(out=sp8[:, 0:8], in_=vec[:, LAM:LAM + 8], func=AF.Exp, scale=-1.0), [vec], [sp8])
            A(lambda e: e.activation(out=sp8[:, 0:8], in_=sp8[:, 0:8], func=AF.Ln, bias=1.0), [sp8], [sp8])
            V(lambda e: e.tensor_scalar(out=sp8[:, 8:16], in0=sp8[:, 0:8], scalar1=-16.0, scalar2=None, op0=ALU.mult), [sp8], [sp8])
            V(lambda e: e.tensor_scalar(out=sp8[:, 0:8], in0=sp8[:, 0:8], scalar1=-8.0, scalar2=None, op0=ALU.mult), [sp8], [sp8])
            xr = [mk(st, f"ex{j}", [128, 1024], F32) for j in range(2)]
            junk = mk(st, "ejunk", [128, 1024], F32)
            ssx = mk(st, "essx", [128, 2], F32)
            xn = mk(st, "exn", [128, 1024], BF16)
            xnT = mk(st, "exnT", [128, 8, 128], BF16)
            gt = mk(st, "gt", [128, 8, 128], F32)
            ue = [mk(st, f"ue{j}", [128, 8, 176], F32) for j in range(2)]
            v32 = mk(st, "v32", [128, 8, 128], F32)
            vb = mk(st, "vb", [128, 8, 128], BF16)
            rg = mk(st, "rg", [128, 8, 128], F32)
            ig = mk(st, "ig", [128, 8, 128], F32)
            aa = mk(st, "aa", [128, 8, 128], F32)
            a2 = mk(st, "a2", [128, 8, 128], F32)
            bb = mk(st, "bb", [128, 8, 128], F32)
            hh = [mk(st, f"hh{j}", [128, 8, 128], F32) for j in range(2)]
            t1 = mk(st, "t1", [128, 8, 128], F32)
            yT = mk(st, "yT", [128, 8, 128], BF16)
            xo = [mk(st, f"exo{j}", [128, 1024], F32) for j in range(2)]
            utm = mk(st, "utm", [128, 1024], F32)
            sst = mk(st, "sst", [64, 1024], F32)
            h0T = mk(st, "h0T", [128, 8, NB], F32)
            tmp16 = mk(st, "tmp16", [128, 8, NB], F32)
            hlast = mk(st, "hlast", [NB, 1024], F32)

            def flat(t):
                return t[:].rearrange("p k t -> p (k t)")

            def l1_tile(i):
                sample = (i == NT)
                x = xr[i % 2]
                LD(x, x[:], X2[rows(i), :])
                rmsnorm(x, xn, junk, ssx)
                transpose8(xn, xnT, PS[4])
                ue_ = ue[i % 2]
                if sample:
                    uev = ue_[:].rearrange("p k (b s) -> p k b s", s=11)
                if sample:
                    LD(sst, sst[0:48, :], sconv[:, :])
                    for blk in range(8):
                        P(lambda e, blk=blk: e.transpose(PS[5][:, blk * 48:(blk + 1) * 48], sst[0:48, blk * 128:(blk + 1) * 128], ident32[0:48, 0:48]), [sst, ident32], [PS[5]])
                    V(lambda e: e.tensor_copy(out=uev[:, :, :, 0:3], in_=PS[5][:, 0:384].rearrange("p (k b r) -> p k b r", k=8, b=NB)), [PS[5]], [ue_])
                    LD(sst, sst[0:NB, :], slru[:, :])
                    for blk in range(8):
                        P(lambda e, blk=blk: e.transpose(PS[5][:, blk * NB:(blk + 1) * NB], sst[0:NB, blk * 128:(blk + 1) * 128], ident32[0:NB, 0:NB]), [sst, ident32], [PS[5]])
                    V(lambda e: e.tensor_copy(out=h0T[:].rearrange("p k b -> p (k b)"), in_=PS[5][:, 0:8 * NB]), [PS[5]], [h0T])
                elif i == 0:
                    G(lambda e: e.memset(ue_[:, :, 0:3], 0.0), [], [ue_])
                else:
                    up_ = ue[(i - 1) % 2]
                    G(lambda e, up_=up_: e.tensor_copy(out=ue_[:, :, 0:3], in_=up_[:, :, 128:131]), [up_], [ue_])
                for mg in range(4):
                    pb_ = PS[mg % 2]
                    for m4 in range(4):
                        m = mg * 4 + m4
                        for k in range(8):
                            P(lambda e, m=m, m4=m4, k=k, pb_=pb_: e.matmul(pb_[:, m4 * 128:(m4 + 1) * 128], lhsT=winr[:, k, m * 128:(m + 1) * 128], rhs=xnT[:, k, :], start=(k == 0), stop=(k == 7)), [winr, xnT], [pb_])
                    if mg < 2:
                        A(lambda e, mg=mg, pb_=pb_: e.activation(out=gt[:, mg * 4:mg * 4 + 4, :].rearrange("p k t -> p (k t)"), in_=pb_[:, :], func=AF.Copy), [pb_], [gt])
                    else:
                        k0 = (mg - 2) * 4
                        if sample:
                            V(lambda e, k0=k0, pb_=pb_: e.tensor_copy(out=uev[:, k0:k0 + 4, :, 3:11], in_=pb_[:, :].rearrange("p (k b t) -> p k b t", k=4, b=NB)), [pb_], [ue_])
                        else:
                            V(lambda e, k0=k0, pb_=pb_: e.tensor_copy(out=ue_[:, k0:k0 + 4, 3:131], in_=pb_[:, :].rearrange("p (k t) -> p k t", k=4)), [pb_], [ue_])
                if i >= NT - 1:
                    for n in range(2):
                        for k in range(8):
                            P(lambda e, n=n, k=k: e.matmul(PS[2 + n][:, :], lhsT=xnT[:, k, :], rhs=winr[:, k, 1024 + n * 512:1024 + (n + 1) * 512], start=(k == 0), stop=(k == 7)), [xnT, winr], [PS[2 + n]])
                        A(lambda e, n=n: e.activation(out=utm[:, n * 512:(n + 1) * 512], in_=PS[2 + n][:, :], func=AF.Copy), [PS[2 + n]], [utm])
                    if sample:
                        for b in range(NB):
                            ST(utm, o_convs[b, :, :], utm[b * 8 + 5:b * 8 + 8, :])
                    else:
                        ST(utm, o_convp[:, :], utm[125:128, :])
                def uk(blk, k):
                    if sample:
                        return uev[:, blk, :, k:k + 8]
                    return ue_[:, blk, k:k + 128]
                def vv(t, blk):
                    if sample:
                        return t[:, blk, :].rearrange("p (b t) -> p b t", b=NB)
                    return t[:, blk, :]
                for blk in range(8):
                    V(lambda e, blk=blk: e.tensor_scalar(out=vv(v32, blk), in0=uk(blk, 0), scalar1=vec[:, CW + blk:CW + blk + 1], scalar2=vec[:, CB + blk:CB + blk + 1], op0=ALU.mult, op1=ALU.add), [ue_, vec], [v32])
                    for k in range(1, 4):
                        V(lambda e, blk=blk, k=k: e.scalar_tensor_tensor(out=vv(v32, blk), in0=uk(blk, k), scalar=vec[:, CW + k * 8 + blk:CW + k * 8 + blk + 1], in1=vv(v32, blk), op0=ALU.mult, op1=ALU.add), [ue_, vec, v32], [v32])
                G(lambda e: e.tensor_copy(out=flat(vb), in_=flat(v32)), [v32], [vb])
                for (wg, bcol, dst) in ((wga, BGA, rg), (wgx, BGX, ig)):
                    for half in range(2):
                        pb_ = PS[half]
                        for b4_ in range(4):
                            blk = half * 4 + b4_
                            P(lambda e, blk=blk, b4_=b4_, pb_=pb_, wg=wg: e.matmul(pb_[:, b4_ * 128:(b4_ + 1) * 128], lhsT=wg[:, blk, :], rhs=vb[:, blk, :], start=True, stop=True), [wg, vb], [pb_])
                        for b4_ in range(4):
                            blk = half * 4 + b4_
                            A(lambda e, blk=blk, b4_=b4_, pb_=pb_, dst=dst, bcol=bcol: e.activation(out=dst[:, blk, :], in_=pb_[:, b4_ * 128:(b4_ + 1) * 128], func=AF.Sigmoid, bias=vec[:, bcol + blk:bcol + blk + 1]), [pb_, vec], [dst])
                for blk in range(8):
                    A(lambda e, blk=blk: e.activation(out=aa[:, blk, :], in_=rg[:, blk, :], func=AF.Exp, scale=sp8[:, blk:blk + 1]), [rg, sp8], [aa])
                for blk in range(8):
                    A(lambda e, blk=blk: e.activation(out=a2[:, blk, :], in_=rg[:, blk, :], func=AF.Exp, scale=sp8[:, 8 + blk:9 + blk]), [rg, sp8], [a2])
                V(lambda e: e.tensor_scalar(out=flat(a2), in0=flat(a2), scalar1=-1.0, scalar2=1.0, op0=ALU.mult, op1=ALU.add), [a2], [a2])
                V(lambda e: e.tensor_scalar_max(out=flat(a2), in0=flat(a2), scalar1=0.0), [a2], [a2])
                if i == 0:
                    G(lambda e: e.memset(a2[:, :, 0:1], 1.0), [], [a2])
                A(lambda e: e.activation(out=flat(a2), in_=flat(a2), func=AF.Sqrt), [a2], [a2])
                V(lambda e: e.tensor_tensor(out=flat(bb), in0=flat(a2), in1=flat(ig), op=ALU.mult), [a2, ig], [bb])
                V(lambda e: e.tensor_tensor(out=flat(bb), in0=flat(bb), in1=flat(v32), op=ALU.mult), [bb, v32], [bb])
                h_ = hh[i % 2]
                if sample:
                    a0 = aa[:].rearrange("p k (b t) -> p k b t", b=NB)[:, :, :, 0]
                    b0 = bb[:].rearrange("p k (b t) -> p k b t", b=NB)[:, :, :, 0]
                    V(lambda e: e.tensor_tensor(out=tmp16[:], in0=a0, in1=h0T[:], op=ALU.mult), [aa, h0T], [tmp16])
                    V(lambda e: e.tensor_tensor(out=b0, in0=b0, in1=tmp16[:], op=ALU.add), [bb, tmp16], [bb])
                    G(lambda e: e.memset(a0, 0.0), [], [aa])
                if (not sample) and i > 0:
                    hp = hh[(i - 1) % 2]
                    V(lambda e, hp=hp: e.tensor_tensor(out=tmp16[:, :, 0], in0=aa[:, :, 0], in1=hp[:, :, 127], op=ALU.mult), [aa, hp], [tmp16])
                    V(lambda e: e.tensor_tensor(out=bb[:, :, 0], in0=bb[:, :, 0], in1=tmp16[:, :, 0], op=ALU.add), [bb, tmp16], [bb])
                for blk in range(8):
                    V(lambda e, blk=blk: e.tensor_tensor_scan(out=h_[:, blk, :], data0=aa[:, blk, :], data1=bb[:, blk, :], initial=0.0, op0=ALU.mult, op1=ALU.add), [aa, bb], [h_])
                if i == NT - 1:
                    ST(h_, o_lrup.rearrange("(k p) -> p k", p=128), h_[:, :, 127], allow_slow_non_contiguous=True)
                if sample:
                    hv = h_[:].rearrange("p k (b t) -> p k b t", b=NB)
                    V(lambda e: e.tensor_copy(out=tmp16[:], in_=hv[:, :, :, 7]), [h_], [tmp16])
                    for blk in range(8):
                        P(lambda e, blk=blk: e.transpose(PS[5 + blk // 4][0:NB, (blk % 4) * 128:(blk % 4 + 1) * 128], tmp16[:, blk, :], ident32[:, :]), [tmp16, ident32], [PS[5 + blk // 4]])
                    for hb in range(2):
                        V(lambda e, hb=hb: e.tensor_copy(out=hlast[:, hb * 512:(hb + 1) * 512], in_=PS[5 + hb][0:NB, :]), [PS[5 + hb]], [hlast])
                    ST(hlast, o_lrus[:, :], hlast[:])
                A(lambda e: e.activation(out=flat(t1), in_=flat(gt), func=AF.Square), [gt], [t1])
                V(lambda e: e.tensor_scalar(out=flat(t1), in0=flat(t1), scalar1=0.044715, scalar2=1.0, op0=ALU.mult, op1=ALU.add), [t1], [t1])
                V(lambda e: e.tensor_tensor(out=flat(t1), in0=flat(t1), in1=flat(gt), op=ALU.mult), [t1, gt], [t1])
                A(lambda e: e.activation(out=flat(t1), in_=flat(t1), func=AF.Tanh, scale=0.7978845608028654), [t1], [t1])
                V(lambda e: e.scalar_tensor_tensor(out=flat(t1), in0=flat(t1), scalar=1.0, in1=flat(gt), op0=ALU.add, op1=ALU.mult), [t1, gt], [t1])
                V(lambda e: e.scalar_tensor_tensor(out=flat(yT), in0=flat(t1), scalar=0.5, in1=flat(h_), op0=ALU.mult, op1=ALU.mult), [t1, h_], [yT])
                xo_ = xo[i % 2]
                for n in range(2):
                    for k in range(8):
                        P(lambda e, n=n, k=k: e.matmul(PS[2 + n][:, :], lhsT=yT[:, k, :], rhs=woutr[:, k, n * 512:(n + 1) * 512], start=(k == 0), stop=(k == 7)), [yT, woutr], [PS[2 + n]])
                    V(lambda e, n=n, x=x, xo_=xo_: e.tensor_tensor(out=xo_[:, n * 512:(n + 1) * 512], in0=PS[2 + n][:, :], in1=x[:, n * 512:(n + 1) * 512], op=ALU.add), [PS[2 + n], x], [xo_])
                ST(xo_, X3[rows(i), :], xo_[:])

            for i in range(NT + 1):
                if os.environ.get("KDBG_SKIPS") == "1" and i == NT:
                    continue
                l1_tile(i)
                if i % 4 == 3:
                    S.end_stage()
            S.end_stage()
            S.release([t for t in S.stage_bufs]); S.stage_bufs = []

        if upto <= 4:
            return nc
        mlp_stage(1, X3, lambda i: (ys[:, :] if i == NT else yp[rows(i), :]), GF1)
    return nc


def _consts():
    f = np.float32
    ident = np.eye(128, dtype=f)
    half = 16
    inv = (np.float32(10000.0) ** (-np.arange(half, dtype=f) / np.float32(half))).astype(f)
    pos = np.concatenate([np.arange(4096), 8192 + (np.arange(128) % 8)]).astype(f)
    ang = (pos[:, None] * inv[None, :]).astype(f)
    cs = np.concatenate([np.cos(ang), np.sin(ang)], axis=1).astype(f)
    s = np.arange(128)[:, None]; t = np.arange(128)[None, :]
    band = np.zeros((20, 128, 128), f)
    bandh = np.zeros((8, 128, 128), f)
    for g, w in enumerate((2, 4, 8, 16)):
        inb = ((t - s) >= 0) & ((t - s) < w)
        band[g] = inb * (1.0 / w) - (s == t)
        band[4 + g] = inb * (1.0 / np.minimum(t + 1, w)) - (s == t)
        band[8 + g] = (((t - (s - 128)) >= 0) & ((t - (s - 128)) < w)) * (1.0 / w)
        sb, ss_ = s // 8, s % 8; tb, tt = t // 8, t % 8
        band[16 + g] = ((sb == tb) & ((tt - ss_) >= 0) & ((tt - ss_) < w)) * (1.0 / w) - (s == t)
        r = np.arange(240)[:, None]
        hb, hr = r // 15, r % 15
        m = ((hb == tb) & (hr >= 16 + tt - w)) * (1.0 / w)
        bandh[2 * g] = m[0:128]
        bandh[2 * g + 1, 0:112] = m[128:240]
    mask = (s <= t).astype(f)
    mnew = np.zeros((128, NB, 8, 8), f)
    for b in range(NB):
        for s_ in range(8):
            for t_ in range(8):
                if s_ <= t_:
                    mnew[b * 8 + s_, b, :, t_] = 1.0
    return dict(c_ident=ident, c_cs=cs,
                c_band=np.ascontiguousarray(band.transpose(1, 0, 2).reshape(128, 20 * 128)),
                c_bandh=np.ascontiguousarray(bandh.transpose(1, 0, 2).reshape(128, 8 * 128)),
                c_mask=mask, c_mnew=np.ascontiguousarray(mnew.reshape(128, NB * 64)),
                c_pcol=np.arange(128, dtype=f).reshape(128, 1))


_NC = None


def kernel(x_prompt, x_sample, cache_ckv, cache_krope, state_pool, state_conv, state_lru, page_table,
           norm_mix, w_in_even, g_q_nope, g_q_rope, g_ckv, g_k_rope, g_k_nope, w_uk, w_uv, w_pool, pool_scale,
           w_out_even, w_in_rnn, conv_w, conv_b, w_gate_a, b_gate_a, w_gate_x, b_gate_x, lru_lambda, w_out_rnn,
           norm_ffn, w_up, w_down):
    global _NC
    f = np.float32
    A_ = lambda a: np.ascontiguousarray(np.asarray(a))
    import os
    upto = int(os.environ.get("KDBG_UPTO", "9"))
    small = os.environ.get("KDBG_SMALL", "0") == "1"
    if _NC is None:
        _NC = build(upto, small)
    nc = _NC

    def pk(v):
        return A_(v).reshape(8, 128).T

    vecs = np.concatenate([
        pk(norm_mix[0]), pk(norm_ffn[0]), pk(norm_mix[1]), pk(norm_ffn[1]),
        A_(pool_scale[0]).reshape(4, 128).T,
        np.concatenate([pk(conv_w[0, k]) for k in range(4)], axis=1),
        pk(conv_b[0]), pk(b_gate_a[0]), pk(b_gate_x[0]), pk(lru_lambda[0])], axis=1).astype(f)
    assert vecs.shape == (128, 100)
    rowv = np.concatenate([A_(g_q_nope[0]), A_(g_q_rope[0]), A_(g_ckv[0]), A_(g_k_rope[0]), A_(g_k_nope[0])]).astype(f)
    rowv = np.ascontiguousarray(np.broadcast_to(rowv[None, :], (128, 448)))
    consts = _consts()
    ckv2 = A_(cache_ckv).reshape(10240 * 128, 256)
    krp2 = A_(cache_krope).reshape(10240 * 128, 32)
    if small:
        ckv2 = ckv2[:256]; krp2 = krp2[:256]
    shared = dict(
        ckv=ckv2, krp=krp2, w_in=A_(w_in_even[0]), w_uk=A_(w_uk[0]).reshape(256, 512),
        w_ukT=np.ascontiguousarray(A_(w_uk[0]).transpose(2, 1, 0).reshape(64, 8 * 256)),
        w_uv=A_(w_uv[0]).reshape(256, 512), w_pool=A_(w_pool[0]), w_out=A_(w_out_even[0]),
        w_inr=A_(w_in_rnn[0]), w_ga=A_(w_gate_a[0]), w_gx=A_(w_gate_x[0]), w_outr=A_(w_out_rnn[0]),
        w_up=A_(w_up), w_dn=A_(w_down), vecs=vecs, rowv=rowv, **consts)
    in_maps = []
    for c in range(8):
        b0 = c * NB
        m = dict(shared)
        m["xp"] = A_(x_prompt[c // 2])
        m["xs"] = A_(x_sample[b0:b0 + NB]).reshape(128, 1024)
        m["ptab"] = np.ascontiguousarray(np.broadcast_to(A_(page_table[b0:b0 + NB]).reshape(1, NB * 64), (128, NB * 64))).astype(np.int32)
        m["spool"] = A_(state_pool[0, b0:b0 + NB]).reshape(NB * 15, 512)
        m["sconv"] = A_(state_conv[0, b0:b0 + NB]).reshape(NB * 3, 1024)
        m["slru"] = A_(state_lru[0, b0:b0 + NB])
        in_maps.append(m)
    res = run_bass_kernel_spmd(nc, in_maps, core_ids=list(range(8))).results
    ev = [res[2 * b] for b in range(4)]
    cat = lambda k: np.concatenate([r[k] for r in res], axis=0)
    y_prompt = np.stack([r["yp"] for r in ev]).astype(f)
    y_sample = cat("ys").reshape(128, 8, 1024)
    return (y_prompt, y_sample,
            np.stack([r["o_ckvp"] for r in ev])[None], np.stack([r["o_krp"] for r in ev])[None],
            np.stack([r["o_poolp"] for r in ev])[None], np.stack([r["o_convp"] for r in ev])[None],
            np.stack([r["o_lrup"].reshape(1024) for r in ev])[None],
            cat("o_ckvs").reshape(1, 128, 8, 256), cat("o_krs").reshape(1, 128, 8, 32),
            cat("o_pools")[None], cat("o_convs")[None], cat("o_lrus")[None])
```

```python
import numpy as np
import concourse.bass as bass
import concourse.mybir as mybir
from concourse.bass_utils import run_bass_kernel_spmd
from contextlib import ExitStack

F32 = mybir.dt.float32
BF16 = mybir.dt.bfloat16
I32 = mybir.dt.int32
AF = mybir.ActivationFunctionType
ALU = mybir.AluOpType
AX = mybir.AxisListType

ENGS = ("pe", "act", "dve", "pool", "sp")
SEM_ROT = 30000
SAME_ENGINE_SYNC = True
import os as _os0
CUT = int(_os0.environ.get('KDBG_CUT', '100000000'))
SKIP = set(int(x) for x in _os0.environ.get('KDBG_SKIP', '').split(',') if x)


class Buf:
    def __init__(self, name, t=None):
        self.name = name
        self.t = t
        self.w = None
        self.r = []
        self.dsem = None
        self.excl = False


class Sched:
    def __init__(self, nc, stack, nsem=96):
        self.nc = nc
        self.sems = [stack.enter_context(nc.semaphore(f"s{i}")) for i in range(nsem)]
        self.free = list(range(nsem))
        self.total = [0] * nsem
        self.cur = {}
        self.own = {e: set() for e in ENGS}
        for e in ENGS[:4]:
            self.cur[e] = self.free.pop(0)
            self.own[e].add(self.cur[e])
        self.waited = {e: {} for e in ENGS}
        self.streams = {e: [] for e in ENGS}
        self.stage_dsems = set()
        self.stage_bufs = []
        self.nops = 0

    def _ev_resolve(self, ev):
        s, v = ev
        if v is None:
            v = self.total[s]
        return s, v

    def _wait(self, eng, evs):
        need = {}
        for ev in evs:
            if ev is None:
                continue
            s, v = self._ev_resolve(ev)
            if v <= 0:
                continue
            if need.get(s, 0) < v:
                need[s] = v
        for s, v in need.items():
            if self.waited[eng].get(s, 0) >= v:
                continue
            if (not SAME_ENGINE_SYNC or eng == "pe") and s in self.own[eng]:
                continue
            self.waited[eng][s] = v
            sem = self.sems[s]
            self.streams[eng].append(lambda e, sem=sem, v=v: e.wait_ge(sem, v))

    def _deps(self, reads, writes, eng=None):
        evs = []
        for b in reads:
            evs.append(b.w)
            if b.excl:
                evs.extend(ev for ev in b.r if ev[0] not in self.own.get(eng, ()))
        for b in writes:
            evs.append(b.w)
            evs.extend(b.r)
        return evs

    def op(self, eng, fn, reads=(), writes=()):
        self.nops += 1
        if self.nops > CUT or self.nops in SKIP:
            return
        self._wait(eng, self._deps(reads, writes, eng))
        s = self.cur[eng]
        self.total[s] += 1
        v = self.total[s]
        sem = self.sems[s]
        self.streams[eng].append(lambda e, fn=fn, sem=sem: fn(e).then_inc(sem, 1))
        ev = (s, v)
        for b in reads:
            b.r.append(ev)
        for b in writes:
            b.w = ev
            b.r = []
        if v >= SEM_ROT:
            self.cur[eng] = self.free.pop(0)
            self.own[eng].add(self.cur[eng])

    def _dsem(self, b):
        if b.dsem is None:
            b.dsem = self.free.pop(0)
            self.stage_bufs.append(b)
        self.stage_dsems.add(b.dsem)
        return b.dsem

    def dma(self, q, fn, sb, load, extra_reads=()):
        self.nops += 1
        if self.nops > CUT or self.nops in SKIP:
            return
        if load:
            self._wait(q, self._deps(extra_reads, [sb]))
        else:
            self._wait(q, self._deps([sb] + list(extra_reads), []))
        s = self._dsem(sb)
        self.total[s] += 16
        sem = self.sems[s]
        self.streams[q].append(lambda e, fn=fn, sem=sem: fn(e).then_inc(sem, 16))
        ev = (s, None)
        if load:
            sb.w = ev
            sb.r = []
        else:
            sb.r.append(ev)

    def release(self, bufs):
        for b in bufs:
            if b.dsem is not None:
                self.free.append(b.dsem)
                b.dsem = None

    def end_stage(self, final=False):
        nc = self.nc
        for s in sorted(self.stage_dsems):
            v = self.total[s]
            if self.waited["sp"].get(s, 0) < v:
                self.waited["sp"][s] = v
                sem = self.sems[s]
                self.streams["sp"].append(lambda e, sem=sem, v=v: e.wait_ge(sem, v))
        streams = self.streams
        with nc.Block() as block:
            @block.tensor
            def _(e):
                for f in streams["pe"]:
                    f(e)

            @block.scalar
            def _(e):
                for f in streams["act"]:
                    f(e)

            @block.vector
            def _(e):
                for f in streams["dve"]:
                    f(e)

            @block.gpsimd
            def _(e):
                for f in streams["pool"]:
                    f(e)

            @block.sync
            def _(e):
                for f in streams["sp"]:
                    f(e)
        self.streams = {e: [] for e in ENGS}
        self.stage_dsems = set()
        for e in ENGS:
            for s in range(len(self.total)):
                self.waited[e][s] = self.total[s]


EPS = 1e-6
SCALE = 96 ** -0.5
import os as _os
NT = int(_os.environ.get('KDBG_NT', '32'))
NB = 16
import ml_dtypes
import os
NPBF = ml_dtypes.bfloat16


class TL:
    def __init__(self, t, name):
        self.t = t
        self.b = Buf(name)

    def __getitem__(self, k):
        return self.t[k]


def build(upto=9, small_cache=False):
    NPOOL = 2 if small_cache else 10240
    nc = bass.Bass("TRN2", target_bir_lowering=False)

    def DI(name, shape, dt=F32):
        return nc.dram_tensor(name, shape, dt, kind="ExternalInput").ap()

    def DO(name, shape, dt=F32):
        return nc.dram_tensor(name, shape, dt, kind="ExternalOutput").ap()

    xp = DI("xp", [4096, 1024]); xs = DI("xs", [128, 1024])
    ckv = DI("ckv", [NPOOL * 128, 256]); krp = DI("krp", [NPOOL * 128, 32])
    ptab = DI("ptab", [128, NB * 64], I32)
    spool = DI("spool", [NB * 15, 512]); sconv = DI("sconv", [NB * 3, 1024]); slru = DI("slru", [NB, 1024])
    w_in = DI("w_in", [1024, 1568]); w_uk = DI("w_uk", [256, 512]); w_ukT = DI("w_ukT", [64, 8 * 256])
    w_uv = DI("w_uv", [256, 512]); w_pool = DI("w_pool", [4, 128, 128]); w_out = DI("w_out", [1024, 1024])
    w_inr = DI("w_inr", [1024, 2048]); w_ga = DI("w_ga", [8, 128, 128]); w_gx = DI("w_gx", [8, 128, 128])
    w_outr = DI("w_outr", [1024, 1024]); w_up = DI("w_up", [2, 1024, 4096]); w_dn = DI("w_dn", [2, 4096, 1024])
    vecs = DI("vecs", [128, 100]); rowv = DI("rowv", [128, 448])
    c_ident = DI("c_ident", [128, 128]); c_cs = DI("c_cs", [4096 + 128, 32])
    c_band = DI("c_band", [128, 20 * 128]); c_bandh = DI("c_bandh", [128, 8 * 128])
    c_mask = DI("c_mask", [128, 128]); c_mnew = DI("c_mnew", [128, NB * 64]); c_pcol = DI("c_pcol", [128, 1])

    yp = DO("yp", [4096, 1024]); ys = DO("ys", [128, 1024])
    o_ckvp = DO("o_ckvp", [4096, 256]); o_krp = DO("o_krp", [4096, 32]); o_poolp = DO("o_poolp", [15, 512])
    o_convp = DO("o_convp", [3, 1024]); o_lrup = DO("o_lrup", [1024])
    o_ckvs = DO("o_ckvs", [128, 256]); o_krs = DO("o_krs", [128, 32]); o_pools = DO("o_pools", [NB, 15, 512])
    o_convs = DO("o_convs", [NB, 3, 1024]); o_lrus = DO("o_lrus", [NB, 1024])
    X1 = nc.dram_tensor("X1", [4224, 1024], F32).ap()
    X2 = nc.dram_tensor("X2", [4224, 1024], F32).ap()
    X3 = nc.dram_tensor("X3", [4224, 1024], F32).ap()

    def rows(i):
        return slice(i * 128, (i + 1) * 128)

    with ExitStack() as gst:
        S = Sched(nc, gst)

        uid = [0]

        def mk(stack, name, shape, dt):
            uid[0] += 1
            name = f"{name}_{uid[0]}"
            return TL(stack.enter_context(nc.sbuf_tensor(name, shape, dt)), name)

        def mkps(stack, name):
            t = TL(stack.enter_context(nc.psum_tensor(name, [128, 512], F32)), name)
            t.b.excl = True
            return t

        def bs(xs_):
            return [x.b for x in xs_]

        def P(fn, r=(), w=()): S.op("pe", fn, bs(r), bs(w))
        def A(fn, r=(), w=()): S.op("act", fn, bs(r), bs(w))
        def V(fn, r=(), w=()): S.op("dve", fn, bs(r), bs(w))
        def G(fn, r=(), w=()): S.op("pool", fn, bs(r), bs(w))
        def LD(tl, out_ap, in_ap, q="sp", **kw): S.dma(q, lambda e: e.dma_start(out=out_ap, in_=in_ap, **kw), tl.b, True)
        def ST(tl, out_ap, in_ap, q="sp", **kw): S.dma(q, lambda e: e.dma_start(out=out_ap, in_=in_ap, **kw), tl.b, False)

        ident32 = mk(gst, "ident32", [128, 128], F32)
        ident = mk(gst, "ident", [128, 128], BF16)
        vec = mk(gst, "vec", [128, 100], F32)
        row = mk(gst, "row", [128, 448], F32)
        PS = [mkps(gst, f"B{i}") for i in range(8)]
        LD(ident32, ident32[:], c_ident[:, :])
        LD(vec, vec[:], vecs[:, :])
        LD(row, row[:], rowv[:, :])
        G(lambda e: e.tensor_copy(out=ident[:], in_=ident32[:]), [ident32], [ident])
        GM0, GF0, GM1, GF1, PSC, CW, CB, BGA, BGX, LAM = 0, 8, 16, 24, 32, 36, 68, 76, 84, 92
        RQN, RQR, RCKV, RKR, RKN = 0, 64, 96, 352, 384

        def bank_bf(p):
            return p.t[:].bitcast(BF16)

        def load_w(stack, name, src_ap, K, N, gcol=None, q="sp", stg=None):
            wt = mk(stack, name, [128, K, N], BF16)
            wt.kb = [Buf(f"{name}_{k}") for k in range(K)]
            CH = 1024
            n = 0
            for k in range(K):
                for c0 in range(0, N, CH):
                    c1 = min(N, c0 + CH)
                    s = stg[n % 2]
                    LD(s, s[:, 0:c1 - c0], src_ap[k * 128:(k + 1) * 128, c0:c1], q=("sp" if n % 2 == 0 else "act"))
                    if gcol is None:
                        if n % 2 == 0:
                            S.op("pool", lambda e, s=s, k=k, c0=c0, c1=c1: e.tensor_copy(out=wt[:, k, c0:c1], in_=s[:, 0:c1 - c0]), [s.b], [wt.b])
                        else:
                            S.op("dve", lambda e, s=s, k=k, c0=c0, c1=c1: e.tensor_copy(out=wt[:, k, c0:c1], in_=s[:, 0:c1 - c0]), [s.b], [wt.b])
                    else:
                        eng = "pool" if n % 2 == 0 else "dve"
                        S.op(eng, lambda e, s=s, k=k, c0=c0, c1=c1: e.tensor_scalar(out=wt[:, k, c0:c1], in0=s[:, 0:c1 - c0], scalar1=vec[:, gcol + k:gcol + k + 1], scalar2=None, op0=ALU.mult), [s.b, vec.b], [wt.b])
                    n += 1
            return wt

        def rmsnorm(x, xn, junk, ss):
            A(lambda e: e.activation(out=junk[:], in_=x[:], func=AF.Square, accum_out=ss[:, 0:1]), [x], [junk, ss])
            V(lambda e: e.tensor_scalar(out=ss[:, 1:2], in0=ss[:, 0:1], scalar1=1.0 / 1024, scalar2=EPS, op0=ALU.mult, op1=ALU.add), [ss], [ss])
            A(lambda e: e.activation(out=ss[:, 1:2], in_=ss[:, 1:2], func=AF.Ln), [ss], [ss])
            A(lambda e: e.activation(out=ss[:, 1:2], in_=ss[:, 1:2], func=AF.Exp, scale=-0.5), [ss], [ss])
            V(lambda e: e.tensor_scalar(out=xn[:], in0=x[:], scalar1=ss[:, 1:2], scalar2=None, op0=ALU.mult), [x, ss], [xn])

        def transpose8(xn, xnT, pb):
            pbb = bank_bf(pb)
            for k in range(8):
                P(lambda e, k=k: e.transpose(pbb[:, k * 128:(k + 1) * 128], xn[:, k * 128:(k + 1) * 128], ident[:]), [xn, ident], [pb])
            V(lambda e: e.tensor_copy(out=xnT[:].rearrange("p k t -> p (k t)"), in_=pbb[:, :]), [pb], [xnT])

        def rstd_cols(ss, c0, c1, n):
            V(lambda e: e.tensor_scalar(out=ss[:, c0:c1], in0=ss[:, c0:c1], scalar1=1.0 / n, scalar2=EPS, op0=ALU.mult, op1=ALU.add), [ss], [ss])

        def sqrt_recip(ss, c0, c1):
            A(lambda e: e.activation(out=ss[:, c0:c1], in_=ss[:, c0:c1], func=AF.Ln), [ss], [ss])
            A(lambda e: e.activation(out=ss[:, c0:c1], in_=ss[:, c0:c1], func=AF.Exp, scale=-0.5), [ss], [ss])

        def rope(dst, src, cs, H, tmp):
            cosb = cs[:, 0:16].unsqueeze(1).broadcast_to([128, H, 16])
            sinb = cs[:, 16:32].unsqueeze(1).broadcast_to([128, H, 16])
            x1 = src(0, 16); x2 = src(16, 32)
            t = tmp.t[:, 0:H, :]
            V(lambda e: e.tensor_tensor(out=t[:, :, 0:16], in0=x2, in1=sinb, op=ALU.mult), [src.tl, cs], [tmp])
            V(lambda e: e.tensor_tensor(out=t[:, :, 16:32], in0=x2, in1=cosb, op=ALU.mult), [src.tl, cs], [tmp])
            V(lambda e: e.tensor_tensor(out=dst(0, 16), in0=x1, in1=cosb, op=ALU.mult), [src.tl, cs], [dst.tl])
            V(lambda e: e.tensor_tensor(out=dst(16, 32), in0=x1, in1=sinb, op=ALU.mult), [src.tl, cs], [dst.tl])
            V(lambda e: e.tensor_tensor(out=dst(0, 16), in0=dst(0, 16), in1=t[:, :, 0:16], op=ALU.subtract), [dst.tl, tmp], [dst.tl])
            V(lambda e: e.tensor_tensor(out=dst(16, 32), in0=dst(16, 32), in1=t[:, :, 16:32], op=ALU.add), [dst.tl, tmp], [dst.tl])

        class APM:
            def __init__(self, tl, f):
                self.tl = tl; self.f = f
            def __call__(self, lo, hi):
                return self.f(lo, hi)

        S.end_stage()
        if upto <= 0:
            return nc

        def l0_stage(SAMPLE):
          with ExitStack() as st:
            stg = [mk(st, f"stg{j}", [128, 1024], F32) for j in range(2)]
            win = load_w(st, "win", w_in, 8, 1568, GM0, stg=stg)
            wout = load_w(st, "wout", w_out, 8, 1024, stg=stg)
            wuk = load_w(st, "wuk", w_uk, 2, 512, stg=stg)
            wuv = load_w(st, "wuv", w_uv, 2, 512, stg=stg)
            wpl = mk(st, "wpl", [128, 4, 128], BF16)
            s32 = stg[0]
            LD(s32, s32[:, 0:512].rearrange("p (g n) -> p g n", g=4), w_pool.rearrange("g p n -> p g n"))
            G(lambda e: e.tensor_copy(out=wpl[:].rearrange("p g n -> p (g n)"), in_=s32[:, 0:512]), [s32], [wpl])
            band = mk(st, "band", [128, 20, 128], BF16)
            for q4 in range(5):
                LD(s32, s32[:, 0:512], c_band[:, q4 * 512:(q4 + 1) * 512])
                G(lambda e, q4=q4: e.tensor_copy(out=band[:, q4 * 4:q4 * 4 + 4, :].rearrange("p a n -> p (a n)"), in_=s32[:, 0:512]), [s32], [band])
            maskc = mk(st, "maskc", [128, 128], BF16)
            LD(s32, s32[:, 0:128], c_mask[:, :])
            G(lambda e: e.tensor_copy(out=maskc[:], in_=s32[:, 0:128]), [s32], [maskc])
            if SAMPLE:
                wukT = mk(st, "wukT", [64, 8, 256], BF16)
                for q2 in range(2):
                    LD(s32, s32[0:64, :], w_ukT[:, q2 * 1024:(q2 + 1) * 1024])
                    G(lambda e, q2=q2: e.tensor_copy(out=wukT[:, q2 * 4:q2 * 4 + 4, :].rearrange("p h n -> p (h n)"), in_=s32[0:64, :]), [s32], [wukT])
                bandh = mk(st, "bandh", [128, 8, 128], BF16)
                mnew = mk(st, "mnew", [128, NB, 64], BF16)
                LD(s32, s32[:, 0:1024], c_bandh[:, :])
                G(lambda e: e.tensor_copy(out=bandh[:].rearrange("p a n -> p (a n)"), in_=s32[:, 0:1024]), [s32], [bandh])
                LD(s32, s32[:, 0:1024], c_mnew[:, :])
                G(lambda e: e.tensor_copy(out=mnew[:].rearrange("p a n -> p (a n)"), in_=s32[:, 0:1024]), [s32], [mnew])
            else:
                bandh = band
                KT = mk(st, "KT", [96, 8, 4096], BF16)
                KTb = [Buf(f"KT{i}") for i in range(NT)]
                VT = mk(st, "VT", [128, NT, 8, 65], BF16)
                VTb = [Buf(f"VT{i}") for i in range(NT)]
                S.op("pool", lambda e: e.memset(VT[:].rearrange("p a h c -> p (a h c)"), 1.0), [], VTb)

            xr = [mk(st, f"x{j}", [128, 1024], F32) for j in range(1)]
            junk = mk(st, "junk", [128, 1056], F32)
            ssx = mk(st, "ssx", [128, 2], F32)
            xn = mk(st, "xn", [128, 1024], BF16)
            xnT = mk(st, "xnT", [128, 8, 128], BF16)
            ub = [mk(st, f"ub{j}", [128, 512], BF16) for j in range(2)]
            zf = mk(st, "zf", [128, 1056], F32)
            sq = junk
            ssz = mk(st, "ssz", [128, 18], F32)
            qn32 = mk(st, "qn32", [128, 8, 96], F32)
            rtmp = mk(st, "rtmp", [128, 8, 32], F32)
            qfb = mk(st, "qfb", [128, 8, 96], BF16)
            qT = mk(st, "qT", [96, 8, 128], BF16)
            cn = [mk(st, f"cn{j}", [128, 256], F32) for j in range(2)]
            krn = mk(st, "krn", [128, 1, 32], F32)
            kro = [mk(st, f"kro{j}", [128, 1, 32], F32) for j in range(2)]
            cnb = mk(st, "cnb", [128, 256], BF16)
            cnT = mk(st, "cnT", [128, 2, 128], BF16)
            ksq = mk(st, "ksq", [128, 512], F32)
            ssk = mk(st, "ssk", [128, 8], F32)
            kfb = mk(st, "kfb", [128, 8, 96], BF16)
            cs = [mk(st, f"cs{j}", [128, 32], F32) for j in range(2)]
            pT = [mk(st, f"pT{j}", [128, 4, 128], BF16) for j in range(2)]
            orec = mk(st, "orec", [128, 8], F32)
            attb = mk(st, "attb", [128, 8, 64], BF16)
            dTb = mk(st, "dTb", [128, 4, 128], BF16)
            mixT = mk(st, "mixT", [128, 8, 128], BF16)
            xo = [mk(st, f"xo{j}", [128, 1024], F32) for j in range(1)]
            u32 = xo[0]
            if SAMPLE:
              hist = [mk(st, f"hist{j}", [128, 512], F32) for j in range(2)]
              histb = [mk(st, f"histb{j}", [128, 512], BF16) for j in range(2)]
              qgb = mk(st, "qgb", [128, 8, 64], BF16)
              qgT = mk(st, "qgT", [64, 8, 128], BF16)
              qrT = mk(st, "qrT", [32, 8, 128], BF16)
              QL = mk(st, "QL", [128, 2, NB, 64], BF16)
              qrB = mk(st, "qrB", [32, NB, 64], BF16)
              pidx = mk(st, "pidx", [128, NB * 64], I32)
              cpg = [mk(st, f"cpg{j}", [128, 256], F32) for j in range(3)]
              kpg = [mk(st, f"kpg{j}", [128, 32], F32) for j in range(3)]
              cb = [mk(st, f"cb{j}", [128, 257], BF16) for j in range(2)]
              krb2 = [mk(st, f"krb{j}", [128, 32], BF16) for j in range(2)]
              cT2 = [mk(st, f"cT{j}", [128, 2, 128], BF16) for j in range(2)]
              krT2 = [mk(st, f"krT{j}", [32, 128], BF16) for j in range(2)]
              ssp2 = [mk(st, f"ssp{j}", [128, 8], F32) for j in range(2)]
              sc2 = [mk(st, f"sc{j}", [128, 64], F32) for j in range(2)]
              ppT2 = [mk(st, f"ppT{j}", [128, 64], BF16) for j in range(2)]
              ksq2 = [ksq, mk(st, "ksqB", [128, 512], F32)]
              lrec = mk(st, "lrec", [64, 1], F32)
              latb = mk(st, "latb", [64, 256], BF16)
              LT = mk(st, "LT", [128, 2, 8, 128], BF16)
              for j in range(2):
                  G(lambda e, j=j: e.memset(cb[j][:, 256:257], 1.0), [], [cb[j]])

            def l0_tile(i):
                sample = (i == NT)
                x = xr[0]
                src = xs if sample else xp[rows(i), :]
                LD(x, x[:], src[:, :] if sample else src)
                c_ = cs[i % 2]
                LD(c_, c_[:], c_cs[rows(i), :], q="act")
                S.op("act", lambda e: e.activation(out=junk[:, 0:1024], in_=x[:], func=AF.Square, accum_out=ssx[:, 0:1]), [x.b], [junk.b, ssx.b])
                V(lambda e: e.tensor_scalar(out=ssx[:, 1:2], in0=ssx[:, 0:1], scalar1=1.0 / 1024, scalar2=EPS, op0=ALU.mult, op1=ALU.add), [ssx], [ssx])
                A(lambda e: e.activation(out=ssx[:, 1:2], in_=ssx[:, 1:2], func=AF.Ln), [ssx], [ssx])
                A(lambda e: e.activation(out=ssx[:, 1:2], in_=ssx[:, 1:2], func=AF.Exp, scale=-0.5), [ssx], [ssx])
                V(lambda e: e.tensor_scalar(out=xn[:], in0=x[:], scalar1=ssx[:, 1:2], scalar2=None, op0=ALU.mult), [x, ssx], [xn])
                transpose8(xn, xnT, PS[4])
                for n, (c0, c1) in enumerate([(0, 512), (512, 1024), (1024, 1536), (1536, 1568)]):
                    for k in range(8):
                        P(lambda e, n=n, k=k, c0=c0, c1=c1: e.matmul(PS[n][:, 0:c1 - c0], lhsT=xnT[:, k, :], rhs=win[:, k, c0:c1], start=(k == 0), stop=(k == 7)), [xnT, win], [PS[n]])
                ucur = ub[i % 2]
                A(lambda e: e.activation(out=ucur[:], in_=PS[0][:, :], func=AF.Copy), [PS[0]], [ucur])
                if i >= NT - 1:
                    _v = _os0.environ.get("KDBG_VAR", "0")
                    if _v == "0":
                        V(lambda e: e.tensor_copy(out=u32[:, 0:512], in_=PS[0][:, :]), [PS[0]], [u32])
                    elif _v == "1":
                        V(lambda e: e.tensor_copy(out=zf[:, 0:512], in_=PS[0][:, :]), [PS[0]], [zf])
                    elif _v == "2":
                        V(lambda e: e.tensor_copy(out=u32[:, 0:512], in_=PS[1][:, :]), [PS[1]], [u32])
                    elif _v == "3":
                        A(lambda e: e.activation(out=u32[:, 0:512], in_=PS[0][:, :], func=AF.Copy), [PS[0]], [u32])
                    if i == NT - 1:
                        ST(u32, o_poolp[:, :], u32[113:128, 0:512])
                    else:
                        for b in range(NB):
                            ST(u32, o_pools[b, 7:15, :], u32[b * 8:(b + 1) * 8, 0:512])
                A(lambda e: e.activation(out=zf[:, 0:512], in_=PS[1][:, :], func=AF.Copy), [PS[1]], [zf])
                V(lambda e: e.tensor_copy(out=zf[:, 512:1024], in_=PS[2][:, :]), [PS[2]], [zf])
                V(lambda e: e.tensor_copy(out=zf[:, 1024:1056], in_=PS[3][:, 0:32]), [PS[3]], [zf])
                G(lambda e: e.tensor_tensor(out=sq[:], in0=zf[:], in1=zf[:], op=ALU.mult), [zf], [sq])
                sqq = sq[:, 0:768].rearrange("p (h d) -> p h d", h=8)
                V(lambda e: e.tensor_reduce(out=ssz[:, 0:8], in_=sqq[:, :, 0:64], axis=AX.X, op=ALU.add), [sq], [ssz])
                V(lambda e: e.tensor_reduce(out=ssz[:, 8:16], in_=sqq[:, :, 64:96], axis=AX.X, op=ALU.add), [sq], [ssz])
                V(lambda e: e.tensor_reduce(out=ssz[:, 16:17], in_=sq[:, 768:1024], axis=AX.X, op=ALU.add), [sq], [ssz])
                V(lambda e: e.tensor_reduce(out=ssz[:, 17:18], in_=sq[:, 1024:1056], axis=AX.X, op=ALU.add), [sq], [ssz])
                rstd_cols(ssz, 0, 8, 64); rstd_cols(ssz, 8, 16, 32); rstd_cols(ssz, 16, 17, 256); rstd_cols(ssz, 17, 18, 32)
                sqrt_recip(ssz, 0, 18)
                zq = zf[:, 0:768].rearrange("p (h d) -> p h d", h=8)
                V(lambda e: e.tensor_tensor(out=qn32[:, :, 0:64], in0=zq[:, :, 0:64], in1=ssz[:, 0:8].unsqueeze(2).broadcast_to([128, 8, 64]), op=ALU.mult), [zf, ssz], [qn32])
                V(lambda e: e.tensor_tensor(out=qn32[:, :, 0:64], in0=qn32[:, :, 0:64], in1=row[:, RQN:RQN + 64].unsqueeze(1).broadcast_to([128, 8, 64]), op=ALU.mult), [qn32, row], [qn32])
                V(lambda e: e.tensor_tensor(out=qn32[:, :, 64:96], in0=zq[:, :, 64:96], in1=ssz[:, 8:16].unsqueeze(2).broadcast_to([128, 8, 32]), op=ALU.mult), [zf, ssz], [qn32])
                V(lambda e: e.tensor_tensor(out=qn32[:, :, 64:96], in0=qn32[:, :, 64:96], in1=row[:, RQR:RQR + 32].unsqueeze(1).broadcast_to([128, 8, 32]), op=ALU.mult), [qn32, row], [qn32])
                G(lambda e: e.tensor_copy(out=qfb[:, :, 0:64], in_=qn32[:, :, 0:64]), [qn32], [qfb])
                rope(APM(qfb, lambda lo, hi: qfb[:, :, 64 + lo:64 + hi]), APM(qn32, lambda lo, hi: qn32[:, :, 64 + lo:64 + hi]), c_, 8, rtmp)
                cn_ = cn[i % 2]
                V(lambda e: e.scalar_tensor_tensor(out=cn_[:], in0=zf[:, 768:1024], scalar=ssz[:, 16:17], in1=row[:, RCKV:RCKV + 256], op0=ALU.mult, op1=ALU.mult), [zf, ssz, row], [cn_])
                ST(cn_, (o_ckvs[:, :] if sample else o_ckvp[rows(i), :]), cn_[:])
                V(lambda e: e.scalar_tensor_tensor(out=krn[:, 0, :], in0=zf[:, 1024:1056], scalar=ssz[:, 17:18], in1=row[:, RKR:RKR + 32], op0=ALU.mult, op1=ALU.mult), [zf, ssz, row], [krn])
                kro_ = kro[i % 2]
                rope(APM(kro_, lambda lo, hi: kro_[:, :, lo:hi]), APM(krn, lambda lo, hi: krn[:, :, lo:hi]), c_, 1, rtmp)
                ST(kro_, (o_krs[:, :] if sample else o_krp[rows(i), :]), kro_[:, 0, :])
                b4 = bank_bf(PS[4])
                if not sample:
                    G(lambda e: e.tensor_copy(out=cnb[:], in_=cn_[:]), [cn_], [cnb])
                    for k in range(2):
                        P(lambda e, k=k: e.transpose(b4[:, k * 128:(k + 1) * 128], cnb[:, k * 128:(k + 1) * 128], ident[:]), [cnb, ident], [PS[4]])
                    V(lambda e: e.tensor_copy(out=cnT[:].rearrange("p k t -> p (k t)"), in_=b4[:, 0:256]), [PS[4]], [cnT])
                    for k in range(2):
                        P(lambda e, k=k: e.matmul(PS[0][:, :], lhsT=cnT[:, k, :], rhs=wuk[:, k, :], start=(k == 0), stop=(k == 1)), [cnT, wuk], [PS[0]])
                    for k in range(2):
                        P(lambda e, k=k: e.matmul(PS[1][:, :], lhsT=cnT[:, k, :], rhs=wuv[:, k, :], start=(k == 0), stop=(k == 1)), [cnT, wuv], [PS[1]])
                    A(lambda e: e.activation(out=ksq[:], in_=PS[0][:, :], func=AF.Square), [PS[0]], [ksq])
                    V(lambda e: e.tensor_reduce(out=ssk[:], in_=ksq[:].rearrange("p (h d) -> p h d", h=8), axis=AX.X, op=ALU.add), [ksq], [ssk])
                    rstd_cols(ssk, 0, 8, 64); sqrt_recip(ssk, 0, 8)
                    V(lambda e: e.tensor_tensor(out=ksq[:].rearrange("p (h d) -> p h d", h=8), in0=PS[0][:, :].rearrange("p (h d) -> p h d", h=8), in1=ssk[:, 0:8].unsqueeze(2).broadcast_to([128, 8, 64]), op=ALU.mult), [PS[0], ssk], [ksq])
                    V(lambda e: e.tensor_tensor(out=kfb[:, :, 0:64], in0=ksq[:].rearrange("p (h d) -> p h d", h=8), in1=row[:, RKN:RKN + 64].unsqueeze(1).broadcast_to([128, 8, 64]), op=ALU.mult), [ksq, row], [kfb])
                    V(lambda e: e.tensor_copy(out=kfb[:, :, 64:96], in_=kro_[:, 0:1, :].broadcast_to([128, 8, 32])), [kro_], [kfb])
                    S.op("act", lambda e: e.activation(out=VT[:, i, :, 0:64], in_=PS[1][:, :].rearrange("p (h d) -> p h d", h=8), func=AF.Copy), [PS[1].b], [VTb[i]])
                    for h in range(8):
                        P(lambda e, h=h: e.transpose(b4[0:96, h * 128:(h + 1) * 128], kfb[:, h, :], ident[:]), [kfb, ident], [PS[4]])
                    S.op("dve", lambda e: e.tensor_copy(out=KT[:, :, rows(i)], in_=b4[0:96, :].rearrange("p (h t) -> p h t", h=8)), [PS[4].b], [KTb[i]])
                    for h in range(8):
                        P(lambda e, h=h: e.transpose(b4[0:96, h * 128:(h + 1) * 128], qfb[:, h, :], ident[:]), [qfb, ident], [PS[4]])
                    V(lambda e: e.tensor_copy(out=qT[:].rearrange("p h t -> p (h t)"), in_=b4[0:96, :]), [PS[4]], [qT])
                    g = 0
                    for h in range(8):
                        ob = PS[5 + h // 4]
                        oc = (h % 4) * 65
                        for j0 in range(0, i + 1, 4):
                            js = list(range(j0, min(i + 1, j0 + 4)))
                            sb_ = PS[2 + g % 2]; pt_ = pT[g % 2]; g += 1
                            for jj, j in enumerate(js):
                                S.op("pe", lambda e, jj=jj, j=j, h=h, sb_=sb_: e.matmul(sb_[:, jj * 128:(jj + 1) * 128], lhsT=KT[:, h, rows(j)], rhs=qT[:, h, :], start=True, stop=True), [KTb[j], qT.b], [sb_.b])
                            n = len(js)
                            A(lambda e, n=n, sb_=sb_, pt_=pt_: e.activation(out=pt_[:, 0:n, :].rearrange("p a t -> p (a t)"), in_=sb_[:, 0:n * 128], func=AF.Exp, scale=SCALE), [sb_], [pt_])
                            if js[-1] == i:
                                jj = len(js) - 1
                                G(lambda e, jj=jj, pt_=pt_: e.tensor_tensor(out=pt_[:, jj, :], in0=pt_[:, jj, :], in1=maskc[:], op=ALU.mult), [pt_, maskc], [pt_])
                            for jj, j in enumerate(js):
                                S.op("pe", lambda e, jj=jj, j=j, h=h, pt_=pt_, ob=ob, oc=oc: e.matmul(ob[:, oc:oc + 65], lhsT=pt_[:, jj, :], rhs=VT[:, j, h, :], start=(j == 0), stop=(j == i)), [pt_.b, VTb[j]], [ob.b])
                    for hb in range(2):
                        ob = PS[5 + hb]
                        ov = ob[:, 0:260].rearrange("p (h c) -> p h c", h=4)
                        V(lambda e, hb=hb, ov=ov: e.reciprocal(out=orec[:, hb * 4:hb * 4 + 4], in_=ov[:, :, 64]), [ob], [orec])
                        V(lambda e, hb=hb, ov=ov: e.tensor_tensor(out=attb[:, hb * 4:hb * 4 + 4, :], in0=ov[:, :, 0:64], in1=orec[:, hb * 4:hb * 4 + 4].unsqueeze(2).broadcast_to([128, 4, 64]), op=ALU.mult), [ob, orec], [attb])
                    for c in range(4):
                        P(lambda e, c=c: e.transpose(b4[:, c * 128:(c + 1) * 128], attb[:, 2 * c:2 * c + 2, :].rearrange("p h d -> p (h d)"), ident[:]), [attb, ident], [PS[4]])
                    V(lambda e: e.tensor_copy(out=mixT[:, 4:8, :].rearrange("p k t -> p (k t)"), in_=b4[:, 0:512]), [PS[4]], [mixT])
                else:
                    sample_attention(kro_, cn_)
                for g_ in range(4):
                    if sample:
                        mats = [(ucur, None, band[:, 16 + g_, :]), (histb[0], 128, bandh[:, 2 * g_, :]), (histb[1], 112, bandh[0:112, 2 * g_ + 1, :])]
                    elif i == 0:
                        mats = [(ucur, None, band[:, 4 + g_, :])]
                    else:
                        mats = [(ucur, None, band[:, g_, :]), (ub[(i - 1) % 2], None, band[:, 8 + g_, :])]
                    for mi, (ut, nr, bm) in enumerate(mats):
                        lhs = ut[:, g_ * 128:(g_ + 1) * 128] if nr is None else ut[0:nr, g_ * 128:(g_ + 1) * 128]
                        P(lambda e, lhs=lhs, bm=bm, g_=g_, mi=mi, nm=len(mats): e.matmul(PS[7][:, g_ * 128:(g_ + 1) * 128], lhsT=lhs, rhs=bm, start=(mi == 0), stop=(mi == nm - 1)), [ut, band, bandh], [PS[7]])
                V(lambda e: e.tensor_copy(out=dTb[:].rearrange("p g t -> p (g t)"), in_=PS[7][:, :]), [PS[7]], [dTb])
                for g_ in range(4):
                    P(lambda e, g_=g_: e.matmul(PS[7][:, g_ * 128:(g_ + 1) * 128], lhsT=wpl[:, g_, :], rhs=dTb[:, g_, :], start=True, stop=True), [wpl, dTb], [PS[7]])
                for g_ in range(4):
                    V(lambda e, g_=g_: e.tensor_scalar(out=mixT[:, g_, :], in0=PS[7][:, g_ * 128:(g_ + 1) * 128], scalar1=vec[:, PSC + g_:PSC + g_ + 1], scalar2=None, op0=ALU.mult), [PS[7], vec], [mixT])
                xo_ = xo[0]
                for n in range(2):
                    for k in range(8):
                        P(lambda e, n=n, k=k: e.matmul(PS[n][:, :], lhsT=mixT[:, k, :], rhs=wout[:, k, n * 512:(n + 1) * 512], start=(k == 0), stop=(k == 7)), [mixT, wout], [PS[n]])
                    V(lambda e, n=n: e.tensor_tensor(out=xo_[:, n * 512:(n + 1) * 512], in0=PS[n][:, :], in1=x[:, n * 512:(n + 1) * 512], op=ALU.add), [PS[n], x], [xo_])
                ST(xo_, X1[rows(i), :], xo_[:])

            def sample_attention(kro_, cn_):
                b4 = bank_bf(PS[4])
                LD(hist[0], hist[0][:, :], spool[0:128, :])
                LD(hist[1], hist[1][0:112, :], spool[128:240, :])
                G(lambda e: e.tensor_copy(out=histb[0][:], in_=hist[0][:]), [hist[0]], [histb[0]])
                G(lambda e: e.tensor_copy(out=histb[1][0:112, :], in_=hist[1][0:112, :]), [hist[1]], [histb[1]])
                for b in range(NB):
                    r0 = b * 15 + 8
                    hh = hist[0] if r0 < 128 else hist[1]
                    rr = r0 if r0 < 128 else r0 - 128
                    ST(hh, o_pools[b, 0:7, :], hh[rr:rr + 7, :])
                V(lambda e: e.tensor_tensor(out=qgb[:], in0=qn32[:, :, 0:64], in1=row[:, RKN:RKN + 64].unsqueeze(1).broadcast_to([128, 8, 64]), op=ALU.mult), [qn32, row], [qgb])
                for h in range(8):
                    P(lambda e, h=h: e.transpose(b4[0:64, h * 128:(h + 1) * 128], qgb[:, h, :], ident[:]), [qgb, ident], [PS[4]])
                V(lambda e: e.tensor_copy(out=qgT[:].rearrange("p h t -> p (h t)"), in_=b4[0:64, :]), [PS[4]], [qgT])
                for h in range(8):
                    P(lambda e, h=h: e.transpose(b4[0:32, h * 128:(h + 1) * 128], qfb[:, h, 64:96], ident[:]), [qfb, ident], [PS[4]])
                V(lambda e: e.tensor_copy(out=qrT[:].rearrange("p h t -> p (h t)"), in_=b4[0:32, :]), [PS[4]], [qrT])
                V(lambda e: e.tensor_copy(out=qrB[:].rearrange("p b (h t) -> p b h t", h=8), in_=qrT[:].rearrange("p h (b t) -> p b h t", b=NB)), [qrT], [qrB])
                for k in range(2):
                    for hq in range(2):
                        pb_ = PS[hq]
                        for h4 in range(4):
                            h = hq * 4 + h4
                            P(lambda e, h=h, h4=h4, k=k, pb_=pb_: e.matmul(pb_[:, h4 * 128:(h4 + 1) * 128], lhsT=wukT[:, h, k * 128:(k + 1) * 128], rhs=qgT[:, h, :], start=True, stop=True), [wukT, qgT], [pb_])
                        V(lambda e, k=k, hq=hq, pb_=pb_: e.tensor_copy(out=QL[:, k, :, hq * 32:(hq + 1) * 32].rearrange("p b (h t) -> p b h t", h=4), in_=pb_[:, :].rearrange("p (h b t) -> p b h t", h=4, b=NB)), [pb_], [QL])
                LD(pidx, pidx[:], ptab[:, :])
                V(lambda e: e.tensor_scalar(out=pidx[:], in0=pidx[:], scalar1=128.0, scalar2=pcol[:, 0:1], op0=ALU.mult, op1=ALU.add), [pidx, pcol], [pidx])

                cnt = [0]

                def proc(c_tl, c_ap, k_tl, k_ap, b, first, last, mask_ap):
                    n = cnt[0]; cnt[0] += 1
                    cb_ = cb[n % 2]
                    krb = krb2[n % 2]; cT = cT2[n % 2]; krT = krT2[n % 2]; ssp = ssp2[n % 2]
                    sc = sc2[n % 2]; ppT = ppT2[n % 2]; ksq = ksq2[n % 2]
                    G(lambda e: e.tensor_copy(out=cb_[:, 0:256], in_=c_ap), [c_tl], [cb_])
                    G(lambda e: e.tensor_copy(out=krb[:], in_=k_ap), [k_tl], [krb])
                    for k in range(2):
                        P(lambda e, k=k: e.transpose(b4[:, k * 128:(k + 1) * 128], cb_[:, k * 128:(k + 1) * 128], ident[:]), [cb_, ident], [PS[4]])
                    P(lambda e: e.transpose(b4[0:32, 256:384], krb[:], ident[:]), [krb, ident], [PS[4]])
                    V(lambda e: e.tensor_copy(out=cT[:].rearrange("p k t -> p (k t)"), in_=b4[:, 0:256]), [PS[4]], [cT])
                    V(lambda e: e.tensor_copy(out=krT[:], in_=b4[0:32, 256:384]), [PS[4]], [krT])
                    kb_ = PS[n % 2]
                    for k in range(2):
                        P(lambda e, k=k: e.matmul(kb_[:, :], lhsT=cT[:, k, :], rhs=wuk[:, k, :], start=(k == 0), stop=(k == 1)), [cT, wuk], [kb_])
                    A(lambda e: e.activation(out=ksq[:], in_=kb_[:, :], func=AF.Square), [kb_], [ksq])
                    V(lambda e: e.tensor_reduce(out=ssp[:], in_=ksq[:].rearrange("p (h d) -> p h d", h=8), axis=AX.X, op=ALU.add), [ksq], [ssp])
                    rstd_cols(ssp, 0, 8, 64); sqrt_recip(ssp, 0, 8)
                    sb_ = PS[2 + n % 2]
                    for k in range(2):
                        P(lambda e, k=k: e.matmul(sb_[:, 0:64], lhsT=cT[:, k, :], rhs=QL[:, k, b, :], start=(k == 0), stop=(k == 1)), [cT, QL], [sb_])
                    P(lambda e: e.matmul(sb_[:, 64:128], lhsT=krT[:, :], rhs=qrB[:, b, :], start=True, stop=True), [krT, qrB], [sb_])
                    V(lambda e: e.tensor_tensor(out=sc[:].rearrange("p (h t) -> p h t", h=8), in0=sb_[:, 0:64].rearrange("p (h t) -> p h t", h=8), in1=ssp[:, 0:8].unsqueeze(2).broadcast_to([128, 8, 8]), op=ALU.mult), [sb_, ssp], [sc])
                    V(lambda e: e.tensor_tensor(out=sc[:], in0=sc[:], in1=sb_[:, 64:128], op=ALU.add), [sc, sb_], [sc])
                    A(lambda e: e.activation(out=ppT[:], in_=sc[:], func=AF.Exp, scale=SCALE), [sc], [ppT])
                    if mask_ap is not None:
                        V(lambda e: e.tensor_tensor(out=ppT[:], in0=ppT[:], in1=mask_ap, op=ALU.mult), [ppT, mnew], [ppT])
                    P(lambda e: e.matmul(PS[5][0:64, 0:257], lhsT=ppT[:], rhs=cb_[:, :], start=first, stop=last), [ppT, cb_], [PS[5]])

                for b in range(NB):
                    for j in range(int(os.environ.get('KDBG_PAGES', '64'))):
                        n = cnt[0]
                        cp = cpg[n % 3]; kp = kpg[n % 3]
                        col = b * 64 + j
                        S.dma("pool", lambda e, cp=cp, col=col: e.indirect_dma_start(out=cp[:, :], out_offset=None, in_=ckv[:, :], in_offset=bass.IndirectOffsetOnAxis(ap=pidx[:, col:col + 1], axis=0)), cp.b, True, extra_reads=[pidx.b])
                        S.dma("pool", lambda e, kp=kp, col=col: e.indirect_dma_start(out=kp[:, :], out_offset=None, in_=krp[:, :], in_offset=bass.IndirectOffsetOnAxis(ap=pidx[:, col:col + 1], axis=0)), kp.b, True, extra_reads=[pidx.b])
                        proc(cp, cp[:, :], kp, kp[:, :], b, j == 0, False, None)
                    proc(cn_, cn_[:, :], kro_, kro_[:, 0, :], b, False, True, mnew[:, b, :])
                    V(lambda e: e.reciprocal(out=lrec[:], in_=PS[5][0:64, 256:257]), [PS[5]], [lrec])
                    V(lambda e: e.tensor_scalar(out=latb[:], in0=PS[5][0:64, 0:256], scalar1=lrec[:, 0:1], scalar2=None, op0=ALU.mult), [PS[5], lrec], [latb])
                    for k in range(2):
                        P(lambda e, k=k: e.transpose(b4[:, 512 + k * 64:512 + (k + 1) * 64], latb[:, k * 128:(k + 1) * 128], ident[0:64, 0:64]), [latb, ident], [PS[4]])
                    V(lambda e, b=b: e.tensor_copy(out=LT[:, :, :, b * 8:(b + 1) * 8], in_=b4[:, 512:640].rearrange("p (k h t) -> p k h t", k=2, h=8)), [PS[4]], [LT])
                    S.end_stage()
                for h in range(8):
                    c = h // 2
                    for k in range(2):
                        P(lambda e, h=h, k=k, c=c: e.matmul(PS[6][(h % 2) * 64:(h % 2) * 64 + 64, c * 128:(c + 1) * 128], lhsT=wuv[:, k, h * 64:(h + 1) * 64], rhs=LT[:, k, h, :], start=(k == 0), stop=(k == 1)), [wuv, LT], [PS[6]])
                V(lambda e: e.tensor_copy(out=mixT[:, 4:8, :].rearrange("p k t -> p (k t)"), in_=PS[6][:, :]), [PS[6]], [mixT])

            pcol = mk(st, "pcol", [128, 1], F32)
            LD(pcol, pcol[:], c_pcol[:, :])
            if SAMPLE:
                l0_tile(NT)
            else:
                for i in range(NT):
                    l0_tile(i)
                    if i % 4 == 3 and i != NT - 1:
                        S.end_stage()
            S.end_stage()
            S.release([t for t in S.stage_bufs])
            S.stage_bufs = []

        l0_stage(False)
        if upto <= 1:
            return nc
        if os.environ.get("KDBG_SKIPS") != "1":
            l0_stage(True)
        if upto <= 2:
            return nc

        def mlp_stage(layer, Xin, Xout_fn, gcol):
            with ExitStack() as st:
                stg = [mk(st, f"mstg{j}", [128, 1024], F32) for j in range(2)]
                wup = load_w(st, "wup", w_up[layer], 8, 4096, gcol, stg=stg)
                wdn = load_w(st, "wdn", w_dn[layer], 32, 1024, stg=stg)
                xr = [mk(st, f"mx{j}", [128, 1024], F32) for j in range(2)]
                junk = mk(st, "mjunk", [128, 1024], F32)
                ssx = mk(st, "mssx", [128, 2], F32)
                xn = mk(st, "mxn", [128, 1024], BF16)
                xnT = mk(st, "mxnT", [128, 8, 128], BF16)
                rl = [mk(st, f"rl{j}", [128, 512], BF16) for j in range(2)]
                hT = mk(st, "hT", [128, 32, 128], BF16)
                xo = [mk(st, f"mxo{j}", [128, 1024], F32) for j in range(2)]
                for i in range(NT + 1):
                    x = xr[i % 2]
                    LD(x, x[:], Xin[rows(i), :])
                    rmsnorm(x, xn, junk, ssx)
                    transpose8(xn, xnT, PS[4])
                    for mg in range(8):
                        pb_ = PS[mg % 2]
                        for m4 in range(4):
                            m = mg * 4 + m4
                            for k in range(8):
                                P(lambda e, m=m, m4=m4, k=k, pb_=pb_: e.matmul(pb_[:, m4 * 128:(m4 + 1) * 128], lhsT=wup[:, k, m * 128:(m + 1) * 128], rhs=xnT[:, k, :], start=(k == 0), stop=(k == 7)), [wup, xnT], [pb_])
                        r_ = rl[mg % 2]
                        A(lambda e, pb_=pb_, r_=r_: e.activation(out=r_[:], in_=pb_[:, :], func=AF.Relu), [pb_], [r_])
                        G(lambda e, mg=mg, r_=r_: e.tensor_tensor(out=hT[:, mg * 4:mg * 4 + 4, :].rearrange("p m t -> p (m t)"), in0=r_[:], in1=r_[:], op=ALU.mult), [r_], [hT])
                    xo_ = xo[i % 2]
                    for n in range(2):
                        for m in range(32):
                            P(lambda e, n=n, m=m: e.matmul(PS[2 + n][:, :], lhsT=hT[:, m, :], rhs=wdn[:, m, n * 512:(n + 1) * 512], start=(m == 0), stop=(m == 31)), [hT, wdn], [PS[2 + n]])
                        V(lambda e, n=n, x=x, xo_=xo_: e.tensor_tensor(out=xo_[:, n * 512:(n + 1) * 512], in0=PS[2 + n][:, :], in1=x[:, n * 512:(n + 1) * 512], op=ALU.add), [PS[2 + n], x], [xo_])
                    ST(xo_, Xout_fn(i), xo_[:])
                    if i % 4 == 3:
                        S.end_stage()
                S.end_stage()
                S.release([t for t in S.stage_bufs]); S.stage_bufs = []

        mlp_stage(0, X1, lambda i: X2[rows(i), :], GF0)
        if upto <= 3:
            return nc

        with ExitStack() as st:
            stg = [mk(st, f"estg{j}", [128, 1024], F32) for j in range(2)]
            winr = load_w(st, "winr", w_inr, 8, 2048, GM1, stg=stg)
            woutr = load_w(st, "woutr", w_outr, 8, 1024, stg=stg)
            wga = mk(st, "wga", [128, 8, 128], BF16)
            wgx = mk(st, "wgx", [128, 8, 128], BF16)
            s32 = mk(st, "e_s32", [128, 1024], F32)
            LD(s32, s32[:, :].rearrange("p (g n) -> p g n", g=8), w_ga.rearrange("g p n -> p g n"))
            G(lambda e: e.tensor_copy(out=wga[:].rearrange("p g n -> p (g n)"), in_=s32[:, :]), [s32], [wga])
            LD(s32, s32[:, :].rearrange("p (g n) -> p g n", g=8), w_gx.rearrange("g p n -> p g n"))
            G(lambda e: e.tensor_copy(out=wgx[:].rearrange("p g n -> p (g n)"), in_=s32[:, :]), [s32], [wgx])
            sp8 = mk(st, "sp8", [128, 16], F32)
            A(lambda e: e.activation(out=sp8[:, 0:8], in_=vec[:, LAM:LAM + 8], func=AF.Exp, scale=-1.0), [vec], [sp8])
            A(lambda e: e.activation(out=sp8[:, 0:8], in_=sp8[:, 0:8], func=AF.Ln, bias=1.0), [sp8], [sp8])
            V(lambda e: e.tensor_scalar(out=sp8[:, 8:16], in0=sp8[:, 0:8], scalar1=-16.0, scalar2=None, op0=ALU.mult), [sp8], [sp8])
            V(lambda e: e.tensor_scalar(out=sp8[:, 0:8], in0=sp8[:, 0:8], scalar1=-8.0, scalar2=None, op0=ALU.mult), [sp8], [sp8])
            xr = [mk(st, f"ex{j}", [128, 1024], F32) for j in range(2)]
            junk = mk(st, "ejunk", [128, 1024], F32)
            ssx = mk(st, "essx", [128, 2], F32)
            xn = mk(st, "exn", [128, 1024], BF16)
            xnT = mk(st, "exnT", [128, 8, 128], BF16)
            gt = mk(st, "gt", [128, 8, 128], F32)
            ue = [mk(st, f"ue{j}", [128, 8, 176], F32) for j in range(2)]
            v32 = mk(st, "v32", [128, 8, 128], F32)
            vb = mk(st, "vb", [128, 8, 128], BF16)
            rg = mk(st, "rg", [128, 8, 128], F32)
            ig = mk(st, "ig", [128, 8, 128], F32)
            aa = mk(st, "aa", [128, 8, 128], F32)
            a2 = mk(st, "a2", [128, 8, 128], F32)
            bb = mk(st, "bb", [128, 8, 128], F32)
            hh = [mk(st, f"hh{j}", [128, 8, 128], F32) for j in range(2)]
            t1 = mk(st, "t1", [128, 8, 128], F32)
            yT = mk(st, "yT", [128, 8, 128], BF16)
            xo = [mk(st, f"exo{j}", [128, 1024], F32) for j in range(2)]
            utm = mk(st, "utm", [128, 1024], F32)
            sst = mk(st, "sst", [64, 1024], F32)
            h0T = mk(st, "h0T", [128, 8, NB], F32)
            tmp16 = mk(st, "tmp16", [128, 8, NB], F32)
            hlast = mk(st, "hlast", [NB, 1024], F32)

            def flat(t):
                return t[:].rearrange("p k t -> p (k t)")

            def l1_tile(i):
                sample = (i == NT)
                x = xr[i % 2]
                LD(x, x[:], X2[rows(i), :])
                rmsnorm(x, xn, junk, ssx)
                transpose8(xn, xnT, PS[4])
                ue_ = ue[i % 2]
                if sample:
                    uev = ue_[:].rearrange("p k (b s) -> p k b s", s=11)
                if sample:
                    LD(sst, sst[0:48, :], sconv[:, :])
                    for blk in range(8):
                        P(lambda e, blk=blk: e.transpose(PS[5][:, blk * 48:(blk + 1) * 48], sst[0:48, blk * 128:(blk + 1) * 128], ident32[0:48, 0:48]), [sst, ident32], [PS[5]])
                    V(lambda e: e.tensor_copy(out=uev[:, :, :, 0:3], in_=PS[5][:, 0:384].rearrange("p (k b r) -> p k b r", k=8, b=NB)), [PS[5]], [ue_])
                    LD(sst, sst[0:NB, :], slru[:, :])
                    for blk in range(8):
                        P(lambda e, blk=blk: e.transpose(PS[5][:, blk * NB:(blk + 1) * NB], sst[0:NB, blk * 128:(blk + 1) * 128], ident32[0:NB, 0:NB]), [sst, ident32], [PS[5]])
                    V(lambda e: e.tensor_copy(out=h0T[:].rearrange("p k b -> p (k b)"), in_=PS[5][:, 0:8 * NB]), [PS[5]], [h0T])
                elif i == 0:
                    G(lambda e: e.memset(ue_[:, :, 0:3], 0.0), [], [ue_])
                else:
                    up_ = ue[(i - 1) % 2]
                    G(lambda e, up_=up_: e.tensor_copy(out=ue_[:, :, 0:3], in_=up_[:, :, 128:131]), [up_], [ue_])
                for mg in range(4):
                    pb_ = PS[mg % 2]
                    for m4 in range(4):
                        m = mg * 4 + m4
                        for k in range(8):
                            P(lambda e, m=m, m4=m4, k=k, pb_=pb_: e.matmul(pb_[:, m4 * 128:(m4 + 1) * 128], lhsT=winr[:, k, m * 128:(m + 1) * 128], rhs=xnT[:, k, :], start=(k == 0), stop=(k == 7)), [winr, xnT], [pb_])
                    if mg < 2:
                        A(lambda e, mg=mg, pb_=pb_: e.activation(out=gt[:, mg * 4:mg * 4 + 4, :].rearrange("p k t -> p (k t)"), in_=pb_[:, :], func=AF.Copy), [pb_], [gt])
                    else:
                        k0 = (mg - 2) * 4
                        if sample:
                            V(lambda e, k0=k0, pb_=pb_: e.tensor_copy(out=uev[:, k0:k0 + 4, :, 3:11], in_=pb_[:, :].rearrange("p (k b t) -> p k b t", k=4, b=NB)), [pb_], [ue_])
                        else:
                            V(lambda e, k0=k0, pb_=pb_: e.tensor_copy(out=ue_[:, k0:k0 + 4, 3:131], in_=pb_[:, :].rearrange("p (k t) -> p k t", k=4)), [pb_], [ue_])
                if i >= NT - 1:
                    for n in range(2):
                        for k in range(8):
                            P(lambda e, n=n, k=k: e.matmul(PS[2 + n][:, :], lhsT=xnT[:, k, :], rhs=winr[:, k, 1024 + n * 512:1024 + (n + 1) * 512], start=(k == 0), stop=(k == 7)), [xnT, winr], [PS[2 + n]])
                        A(lambda e, n=n: e.activation(out=utm[:, n * 512:(n + 1) * 512], in_=PS[2 + n][:, :], func=AF.Copy), [PS[2 + n]], [utm])
                    if sample:
                        for b in range(NB):
                            ST(utm, o_convs[b, :, :], utm[b * 8 + 5:b * 8 + 8, :])
                    else:
                        ST(utm, o_convp[:, :], utm[125:128, :])
                def uk(blk, k):
                    if sample:
                        return uev[:, blk, :, k:k + 8]
                    return ue_[:, blk, k:k + 128]
                def vv(t, blk):
                    if sample:
                        return t[:, blk, :].rearrange("p (b t) -> p b t", b=NB)
                    return t[:, blk, :]
                for blk in range(8):
                    V(lambda e, blk=blk: e.tensor_scalar(out=vv(v32, blk), in0=uk(blk, 0), scalar1=vec[:, CW + blk:CW + blk + 1], scalar2=vec[:, CB + blk:CB + blk + 1], op0=ALU.mult, op1=ALU.add), [ue_, vec], [v32])
                    for k in range(1, 4):
                        V(lambda e, blk=blk, k=k: e.scalar_tensor_tensor(out=vv(v32, blk), in0=uk(blk, k), scalar=vec[:, CW + k * 8 + blk:CW + k * 8 + blk + 1], in1=vv(v32, blk), op0=ALU.mult, op1=ALU.add), [ue_, vec, v32], [v32])
                G(lambda e: e.tensor_copy(out=flat(vb), in_=flat(v32)), [v32], [vb])
                for (wg, bcol, dst) in ((wga, BGA, rg), (wgx, BGX, ig)):
                    for half in range(2):
                        pb_ = PS[half]
                        for b4_ in range(4):
                            blk = half * 4 + b4_
                            P(lambda e, blk=blk, b4_=b4_, pb_=pb_, wg=wg: e.matmul(pb_[:, b4_ * 128:(b4_ + 1) * 128], lhsT=wg[:, blk, :], rhs=vb[:, blk, :], start=True, stop=True), [wg, vb], [pb_])
                        for b4_ in range(4):
                            blk = half * 4 + b4_
                            A(lambda e, blk=blk, b4_=b4_, pb_=pb_, dst=dst, bcol=bcol: e.activation(out=dst[:, blk, :], in_=pb_[:, b4_ * 128:(b4_ + 1) * 128], func=AF.Sigmoid, bias=vec[:, bcol + blk:bcol + blk + 1]), [pb_, vec], [dst])
                for blk in range(8):
                    A(lambda e, blk=blk: e.activation(out=aa[:, blk, :], in_=rg[:, blk, :], func=AF.Exp, scale=sp8[:, blk:blk + 1]), [rg, sp8], [aa])
                for blk in range(8):
                    A(lambda e, blk=blk: e.activation(out=a2[:, blk, :], in_=rg[:, blk, :], func=AF.Exp, scale=sp8[:, 8 + blk:9 + blk]), [rg, sp8], [a2])
                V(lambda e: e.tensor_scalar(out=flat(a2), in0=flat(a2), scalar1=-1.0, scalar2=1.0, op0=ALU.mult, op1=ALU.add), [a2], [a2])
                V(lambda e: e.tensor_scalar_max(out=flat(a2), in0=flat(a2), scalar1=0.0), [a2], [a2])
                if i == 0:
                    G(lambda e: e.memset(a2[:, :, 0:1], 1.0), [], [a2])
                A(lambda e: e.activation(out=flat(a2), in_=flat(a2), func=AF.Sqrt), [a2], [a2])
                V(lambda e: e.tensor_tensor(out=flat(bb), in0=flat(a2), in1=flat(ig), op=ALU.mult), [a2, ig], [bb])
                V(lambda e: e.tensor_tensor(out=flat(bb), in0=flat(bb), in1=flat(v32), op=ALU.mult), [bb, v32], [bb])
                h_ = hh[i % 2]
                if sample:
                    a0 = aa[:].rearrange("p k (b t) -> p k b t", b=NB)[:, :, :, 0]
                    b0 = bb[:].rearrange("p k (b t) -> p k b t", b=NB)[:, :, :, 0]
                    V(lambda e: e.tensor_tensor(out=tmp16[:], in0=a0, in1=h0T[:], op=ALU.mult), [aa, h0T], [tmp16])
                    V(lambda e: e.tensor_tensor(out=b0, in0=b0, in1=tmp16[:], op=ALU.add), [bb, tmp16], [bb])
                    G(lambda e: e.memset(a0, 0.0), [], [aa])
                if (not sample) and i > 0:
                    hp = hh[(i - 1) % 2]
                    V(lambda e, hp=hp: e.tensor_tensor(out=tmp16[:, :, 0], in0=aa[:, :, 0], in1=hp[:, :, 127], op=ALU.mult), [aa, hp], [tmp16])
                    V(lambda e: e.tensor_tensor(out=bb[:, :, 0], in0=bb[:, :, 0], in1=tmp16[:, :, 0], op=ALU.add), [bb, tmp16], [bb])
                for blk in range(8):
                    V(lambda e, blk=blk: e.tensor_tensor_scan(out=h_[:, blk, :], data0=aa[:, blk, :], data1=bb[:, blk, :], initial=0.0, op0=ALU.mult, op1=ALU.add), [aa, bb], [h_])
                if i == NT - 1:
                    ST(h_, o_lrup.rearrange("(k p) -> p k", p=128), h_[:, :, 127], allow_slow_non_contiguous=True)
                if sample:
                    hv = h_[:].rearrange("p k (b t) -> p k b t", b=NB)
                    V(lambda e: e.tensor_copy(out=tmp16[:], in_=hv[:, :, :, 7]), [h_], [tmp16])
                    for blk in range(8):
                        P(lambda e, blk=blk: e.transpose(PS[5 + blk // 4][0:NB, (blk % 4) * 128:(blk % 4 + 1) * 128], tmp16[:, blk, :], ident32[:, :]), [tmp16, ident32], [PS[5 + blk // 4]])
                    for hb in range(2):
                        V(lambda e, hb=hb: e.tensor_copy(out=hlast[:, hb * 512:(hb + 1) * 512], in_=PS[5 + hb][0:NB, :]), [PS[5 + hb]], [hlast])
                    ST(hlast, o_lrus[:, :], hlast[:])
                A(lambda e: e.activation(out=flat(t1), in_=flat(gt), func=AF.Square), [gt], [t1])
                V(lambda e: e.tensor_scalar(out=flat(t1), in0=flat(t1), scalar1=0.044715, scalar2=1.0, op0=ALU.mult, op1=ALU.add), [t1], [t1])
                V(lambda e: e.tensor_tensor(out=flat(t1), in0=flat(t1), in1=flat(gt), op=ALU.mult), [t1, gt], [t1])
                A(lambda e: e.activation(out=flat(t1), in_=flat(t1), func=AF.Tanh, scale=0.7978845608028654), [t1], [t1])
                V(lambda e: e.scalar_tensor_tensor(out=flat(t1), in0=flat(t1), scalar=1.0, in1=flat(gt), op0=ALU.add, op1=ALU.mult), [t1, gt], [t1])
                V(lambda e: e.scalar_tensor_tensor(out=flat(yT), in0=flat(t1), scalar=0.5, in1=flat(h_), op0=ALU.mult, op1=ALU.mult), [t1, h_], [yT])
                xo_ = xo[i % 2]
                for n in range(2):
                    for k in range(8):
                        P(lambda e, n=n, k=k: e.matmul(PS[2 + n][:, :], lhsT=yT[:, k, :], rhs=woutr[:, k, n * 512:(n + 1) * 512], start=(k == 0), stop=(k == 7)), [yT, woutr], [PS[2 + n]])
                    V(lambda e, n=n, x=x, xo_=xo_: e.tensor_tensor(out=xo_[:, n * 512:(n + 1) * 512], in0=PS[2 + n][:, :], in1=x[:, n * 512:(n + 1) * 512], op=ALU.add), [PS[2 + n], x], [xo_])
                ST(xo_, X3[rows(i), :], xo_[:])

            for i in range(NT + 1):
                if os.environ.get("KDBG_SKIPS") == "1" and i == NT:
                    continue
                l1_tile(i)
                if i % 4 == 3:
                    S.end_stage()
            S.end_stage()
            S.release([t for t in S.stage_bufs]); S.stage_bufs = []

        if upto <= 4:
            return nc
        mlp_stage(1, X3, lambda i: (ys[:, :] if i == NT else yp[rows(i), :]), GF1)
    return nc


def _consts():
    f = np.float32
    ident = np.eye(128, dtype=f)
    half = 16
    inv = (np.float32(10000.0) ** (-np.arange(half, dtype=f) / np.float32(half))).astype(f)
    pos = np.concatenate([np.arange(4096), 8192 + (np.arange(128) % 8)]).astype(f)
    ang = (pos[:, None] * inv[None, :]).astype(f)
    cs = np.concatenate([np.cos(ang), np.sin(ang)], axis=1).astype(f)
    s = np.arange(128)[:, None]; t = np.arange(128)[None, :]
    band = np.zeros((20, 128, 128), f)
    bandh = np.zeros((8, 128, 128), f)
    for g, w in enumerate((2, 4, 8, 16)):
        inb = ((t - s) >= 0) & ((t - s) < w)
        band[g] = inb * (1.0 / w) - (s == t)
        band[4 + g] = inb * (1.0 / np.minimum(t + 1, w)) - (s == t)
        band[8 + g] = (((t - (s - 128)) >= 0) & ((t - (s - 128)) < w)) * (1.0 / w)
        sb, ss_ = s // 8, s % 8; tb, tt = t // 8, t % 8
        band[16 + g] = ((sb == tb) & ((tt - ss_) >= 0) & ((tt - ss_) < w)) * (1.0 / w) - (s == t)
        r = np.arange(240)[:, None]
        hb, hr = r // 15, r % 15
        m = ((hb == tb) & (hr >= 16 + tt - w)) * (1.0 / w)
        bandh[2 * g] = m[0:128]
        bandh[2 * g + 1, 0:112] = m[128:240]
    mask = (s <= t).astype(f)
    mnew = np.zeros((128, NB, 8, 8), f)
    for b in range(NB):
        for s_ in range(8):
            for t_ in range(8):
                if s_ <= t_:
                    mnew[b * 8 + s_, b, :, t_] = 1.0
    return dict(c_ident=ident, c_cs=cs,
                c_band=np.ascontiguousarray(band.transpose(1, 0, 2).reshape(128, 20 * 128)),
                c_bandh=np.ascontiguousarray(bandh.transpose(1, 0, 2).reshape(128, 8 * 128)),
                c_mask=mask, c_mnew=np.ascontiguousarray(mnew.reshape(128, NB * 64)),
                c_pcol=np.arange(128, dtype=f).reshape(128, 1))


_NC = None


def kernel(x_prompt, x_sample, cache_ckv, cache_krope, state_pool, state_conv, state_lru, page_table,
           norm_mix, w_in_even, g_q_nope, g_q_rope, g_ckv, g_k_rope, g_k_nope, w_uk, w_uv, w_pool, pool_scale,
           w_out_even, w_in_rnn, conv_w, conv_b, w_gate_a, b_gate_a, w_gate_x, b_gate_x, lru_lambda, w_out_rnn,
           norm_ffn, w_up, w_down):
    global _NC
    f = np.float32
    A_ = lambda a: np.ascontiguousarray(np.asarray(a))
    import os
    upto = int(os.environ.get("KDBG_UPTO", "9"))
    small = os.environ.get("KDBG_SMALL", "0") == "1"
    if _NC is None:
        _NC = build(upto, small)
    nc = _NC

    def pk(v):
        return A_(v).reshape(8, 128).T

    vecs = np.concatenate([
        pk(norm_mix[0]), pk(norm_ffn[0]), pk(norm_mix[1]), pk(norm_ffn[1]),
        A_(pool_scale[0]).reshape(4, 128).T,
        np.concatenate([pk(conv_w[0, k]) for k in range(4)], axis=1),
        pk(conv_b[0]), pk(b_gate_a[0]), pk(b_gate_x[0]), pk(lru_lambda[0])], axis=1).astype(f)
    assert vecs.shape == (128, 100)
    rowv = np.concatenate([A_(g_q_nope[0]), A_(g_q_rope[0]), A_(g_ckv[0]), A_(g_k_rope[0]), A_(g_k_nope[0])]).astype(f)
    rowv = np.ascontiguousarray(np.broadcast_to(rowv[None, :], (128, 448)))
    consts = _consts()
    ckv2 = A_(cache_ckv).reshape(10240 * 128, 256)
    krp2 = A_(cache_krope).reshape(10240 * 128, 32)
    if small:
        ckv2 = ckv2[:256]; krp2 = krp2[:256]
    shared = dict(
        ckv=ckv2, krp=krp2, w_in=A_(w_in_even[0]), w_uk=A_(w_uk[0]).reshape(256, 512),
        w_ukT=np.ascontiguousarray(A_(w_uk[0]).transpose(2, 1, 0).reshape(64, 8 * 256)),
        w_uv=A_(w_uv[0]).reshape(256, 512), w_pool=A_(w_pool[0]), w_out=A_(w_out_even[0]),
        w_inr=A_(w_in_rnn[0]), w_ga=A_(w_gate_a[0]), w_gx=A_(w_gate_x[0]), w_outr=A_(w_out_rnn[0]),
        w_up=A_(w_up), w_dn=A_(w_down), vecs=vecs, rowv=rowv, **consts)
    in_maps = []
    for c in range(8):
        b0 = c * NB
        m = dict(shared)
        m["xp"] = A_(x_prompt[c // 2])
        m["xs"] = A_(x_sample[b0:b0 + NB]).reshape(128, 1024)
        m["ptab"] = np.ascontiguousarray(np.broadcast_to(A_(page_table[b0:b0 + NB]).reshape(1, NB * 64), (128, NB * 64))).astype(np.int32)
        m["spool"] = A_(state_pool[0, b0:b0 + NB]).reshape(NB * 15, 512)
        m["sconv"] = A_(state_conv[0, b0:b0 + NB]).reshape(NB * 3, 1024)
        m["slru"] = A_(state_lru[0, b0:b0 + NB])
        in_maps.append(m)
    res = run_bass_kernel_spmd(nc, in_maps, core_ids=list(range(8))).results
    ev = [res[2 * b] for b in range(4)]
    cat = lambda k: np.concatenate([r[k] for r in res], axis=0)
    y_prompt = np.stack([r["yp"] for r in ev]).astype(f)
    y_sample = cat("ys").reshape(128, 8, 1024)
    return (y_prompt, y_sample,
            np.stack([r["o_ckvp"] for r in ev])[None], np.stack([r["o_krp"] for r in ev])[None],
            np.stack([r["o_poolp"] for r in ev])[None], np.stack([r["o_convp"] for r in ev])[None],
            np.stack([r["o_lrup"].reshape(1024) for r in ev])[None],
            cat("o_ckvs").reshape(1, 128, 8, 256), cat("o_krs").reshape(1, 128, 8, 32),
            cat("o_pools")[None], cat("o_convs")[None], cat("o_lrus")[None])
```

```python
import numpy as np
import concourse.bass as bass
import concourse.mybir as mybir
from concourse.bass_utils import run_bass_kernel_spmd
from contextlib import ExitStack

F32 = mybir.dt.float32
BF16 = mybir.dt.bfloat16
I32 = mybir.dt.int32
AF = mybir.ActivationFunctionType
ALU = mybir.AluOpType
AX = mybir.AxisListType

ENGS = ("pe", "act", "dve", "pool", "sp")
SEM_ROT = 30000
SAME_ENGINE_SYNC = True
import os as _os0
CUT = int(_os0.environ.get('KDBG_CUT', '100000000'))
SKIP = set(int(x) for x in _os0.environ.get('KDBG_SKIP', '').split(',') if x)


class Buf:
    def __init__(self, name, t=None):
        self.name = name
        self.t = t
        self.w = None
        self.r = []
        self.dsem = None
        self.excl = False


class Sched:
    def __init__(self, nc, stack, nsem=96):
        self.nc = nc
        self.sems = [stack.enter_context(nc.semaphore(f"s{i}")) for i in range(nsem)]
        self.free = list(range(nsem))
        self.total = [0] * nsem
        self.cur = {}
        self.own = {e: set() for e in ENGS}
        for e in ENGS[:4]:
            self.cur[e] = self.free.pop(0)
            self.own[e].add(self.cur[e])
        self.waited = {e: {} for e in ENGS}
        self.streams = {e: [] for e in ENGS}
        self.stage_dsems = set()
        self.stage_bufs = []
        self.nops = 0

    def _ev_resolve(self, ev):
        s, v = ev
        if v is None:
            v = self.total[s]
        return s, v

    def _wait(self, eng, evs):
        need = {}
        for ev in evs:
            if ev is None:
                continue
            s, v = self._ev_resolve(ev)
            if v <= 0:
                continue
            if need.get(s, 0) < v:
                need[s] = v
        for s, v in need.items():
            if self.waited[eng].get(s, 0) >= v:
                continue
            if (not SAME_ENGINE_SYNC or eng == "pe") and s in self.own[eng]:
                continue
            self.waited[eng][s] = v
            sem = self.sems[s]
            self.streams[eng].append(lambda e, sem=sem, v=v: e.wait_ge(sem, v))

    def _deps(self, reads, writes, eng=None):
        evs = []
        for b in reads:
            evs.append(b.w)
            if b.excl:
                evs.extend(ev for ev in b.r if ev[0] not in self.own.get(eng, ()))
        for b in writes:
            evs.append(b.w)
            evs.extend(b.r)
        return evs

    def op(self, eng, fn, reads=(), writes=()):
        self.nops += 1
        if self.nops > CUT or self.nops in SKIP:
            return
        self._wait(eng, self._deps(reads, writes, eng))
        s = self.cur[eng]
        self.total[s] += 1
        v = self.total[s]
        sem = self.sems[s]
        self.streams[eng].append(lambda e, fn=fn, sem=sem: fn(e).then_inc(sem, 1))
        ev = (s, v)
        for b in reads:
            b.r.append(ev)
        for b in writes:
            b.w = ev
            b.r = []
        if v >= SEM_ROT:
            self.cur[eng] = self.free.pop(0)
            self.own[eng].add(self.cur[eng])

    def _dsem(self, b):
        if b.dsem is None:
            b.dsem = self.free.pop(0)
            self.stage_bufs.append(b)
        self.stage_dsems.add(b.dsem)
        return b.dsem

    def dma(self, q, fn, sb, load, extra_reads=()):
        self.nops += 1
        if self.nops > CUT or self.nops in SKIP:
            return
        if load:
            self._wait(q, self._deps(extra_reads, [sb]))
        else:
            self._wait(q, self._deps([sb] + list(extra_reads), []))
        s = self._dsem(sb)
        self.total[s] += 16
        sem = self.sems[s]
        self.streams[q].append(lambda e, fn=fn, sem=sem: fn(e).then_inc(sem, 16))
        ev = (s, None)
        if load:
            sb.w = ev
            sb.r = []
        else:
            sb.r.append(ev)

    def release(self, bufs):
        for b in bufs:
            if b.dsem is not None:
                self.free.append(b.dsem)
                b.dsem = None

    def end_stage(self, final=False):
        nc = self.nc
        for s in sorted(self.stage_dsems):
            v = self.total[s]
            if self.waited["sp"].get(s, 0) < v:
                self.waited["sp"][s] = v
                sem = self.sems[s]
                self.streams["sp"].append(lambda e, sem=sem, v=v: e.wait_ge(sem, v))
        streams = self.streams
        with nc.Block() as block:
            @block.tensor
            def _(e):
                for f in streams["pe"]:
                    f(e)

            @block.scalar
            def _(e):
                for f in streams["act"]:
                    f(e)

            @block.vector
            def _(e):
                for f in streams["dve"]:
                    f(e)

            @block.gpsimd
            def _(e):
                for f in streams["pool"]:
                    f(e)

            @block.sync
            def _(e):
                for f in streams["sp"]:
                    f(e)
        self.streams = {e: [] for e in ENGS}
        self.stage_dsems = set()
        for e in ENGS:
            for s in range(len(self.total)):
                self.waited[e][s] = self.total[s]


EPS = 1e-6
SCALE = 96 ** -0.5
import os as _os
NT = int(_os.environ.get('KDBG_NT', '32'))
NB = 16
import ml_dtypes
import os
NPBF = ml_dtypes.bfloat16


class TL:
    def __init__(self, t, name):
        self.t = t
        self.b = Buf(name)

    def __getitem__(self, k):
        return self.t[k]


def build(upto=9, small_cache=False):
    NPOOL = 2 if small_cache else 10240
    nc = bass.Bass("TRN2", target_bir_lowering=False)

    def DI(name, shape, dt=F32):
        return nc.dram_tensor(name, shape, dt, kind="ExternalInput").ap()

    def DO(name, shape, dt=F32):
        return nc.dram_tensor(name, shape, dt, kind="ExternalOutput").ap()

    xp = DI("xp", [4096, 1024]); xs = DI("xs", [128, 1024])
    ckv = DI("ckv", [NPOOL * 128, 256]); krp = DI("krp", [NPOOL * 128, 32])
    ptab = DI("ptab", [128, NB * 64], I32)
    spool = DI("spool", [NB * 15, 512]); sconv = DI("sconv", [NB * 3, 1024]); slru = DI("slru", [NB, 1024])
    w_in = DI("w_in", [1024, 1568]); w_uk = DI("w_uk", [256, 512]); w_ukT = DI("w_ukT", [64, 8 * 256])
    w_uv = DI("w_uv", [256, 512]); w_pool = DI("w_pool", [4, 128, 128]); w_out = DI("w_out", [1024, 1024])
    w_inr = DI("w_inr", [1024, 2048]); w_ga = DI("w_ga", [8, 128, 128]); w_gx = DI("w_gx", [8, 128, 128])
    w_outr = DI("w_outr", [1024, 1024]); w_up = DI("w_up", [2, 1024, 4096]); w_dn = DI("w_dn", [2, 4096, 1024])
    vecs = DI("vecs", [128, 100]); rowv = DI("rowv", [128, 448])
    c_ident = DI("c_ident", [128, 128]); c_cs = DI("c_cs", [4096 + 128, 32])
    c_band = DI("c_band", [128, 20 * 128]); c_bandh = DI("c_bandh", [128, 8 * 128])
    c_mask = DI("c_mask", [128, 128]); c_mnew = DI("c_mnew", [128, NB * 64]); c_pcol = DI("c_pcol", [128, 1])

    yp = DO("yp", [4096, 1024]); ys = DO("ys", [128, 1024])
    o_ckvp = DO("o_ckvp", [4096, 256]); o_krp = DO("o_krp", [4096, 32]); o_poolp = DO("o_poolp", [15, 512])
    o_convp = DO("o_convp", [3, 1024]); o_lrup = DO("o_lrup", [1024])
    o_ckvs = DO("o_ckvs", [128, 256]); o_krs = DO("o_krs", [128, 32]); o_pools = DO("o_pools", [NB, 15, 512])
    o_convs = DO("o_convs", [NB, 3, 1024]); o_lrus = DO("o_lrus", [NB, 1024])
    X1 = nc.dram_tensor("X1", [4224, 1024], F32).ap()
    X2 = nc.dram_tensor("X2", [4224, 1024], F32).ap()
    X3 = nc.dram_tensor("X3", [4224, 1024], F32).ap()

    def rows(i):
        return slice(i * 128, (i + 1) * 128)

    with ExitStack() as gst:
        S = Sched(nc, gst)

        uid = [0]

        def mk(stack, name, shape, dt):
            uid[0] += 1
            name = f"{name}_{uid[0]}"
            return TL(stack.enter_context(nc.sbuf_tensor(name, shape, dt)), name)

        def mkps(stack, name):
            t = TL(stack.enter_context(nc.psum_tensor(name, [128, 512], F32)), name)
            t.b.excl = True
            return t

        def bs(xs_):
            return [x.b for x in xs_]

        def P(fn, r=(), w=()): S.op("pe", fn, bs(r), bs(w))
        def A(fn, r=(), w=()): S.op("act", fn, bs(r), bs(w))
        def V(fn, r=(), w=()): S.op("dve", fn, bs(r), bs(w))
        def G(fn, r=(), w=()): S.op("pool", fn, bs(r), bs(w))
        def LD(tl, out_ap, in_ap, q="sp", **kw): S.dma(q, lambda e: e.dma_start(out=out_ap, in_=in_ap, **kw), tl.b, True)
        def ST(tl, out_ap, in_ap, q="sp", **kw): S.dma(q, lambda e: e.dma_start(out=out_ap, in_=in_ap, **kw), tl.b, False)

        ident32 = mk(gst, "ident32", [128, 128], F32)
        ident = mk(gst, "ident", [128, 128], BF16)
        vec = mk(gst, "vec", [128, 100], F32)
        row = mk(gst, "row", [128, 448], F32)
        PS = [mkps(gst, f"B{i}") for i in range(8)]
        LD(ident32, ident32[:], c_ident[:, :])
        LD(vec, vec[:], vecs[:, :])
        LD(row, row[:], rowv[:, :])
        G(lambda e: e.tensor_copy(out=ident[:], in_=ident32[:]), [ident32], [ident])
        GM0, GF0, GM1, GF1, PSC, CW, CB, BGA, BGX, LAM = 0, 8, 16, 24, 32, 36, 68, 76, 84, 92
        RQN, RQR, RCKV, RKR, RKN = 0, 64, 96, 352, 384

        def bank_bf(p):
            return p.t[:].bitcast(BF16)

        def load_w(stack, name, src_ap, K, N, gcol=None, q="sp", stg=None):
            wt = mk(stack, name, [128, K, N], BF16)
            wt.kb = [Buf(f"{name}_{k}") for k in range(K)]
            CH = 1024
            n = 0
            for k in range(K):
                for c0 in range(0, N, CH):
                    c1 = min(N, c0 + CH)
                    s = stg[n % 2]
                    LD(s, s[:, 0:c1 - c0], src_ap[k * 128:(k + 1) * 128, c0:c1], q=("sp" if n % 2 == 0 else "act"))
                    if gcol is None:
                        if n % 2 == 0:
                            S.op("pool", lambda e, s=s, k=k, c0=c0, c1=c1: e.tensor_copy(out=wt[:, k, c0:c1], in_=s[:, 0:c1 - c0]), [s.b], [wt.b])
                        else:
                            S.op("dve", lambda e, s=s, k=k, c0=c0, c1=c1: e.tensor_copy(out=wt[:, k, c0:c1], in_=s[:, 0:c1 - c0]), [s.b], [wt.b])
                    else:
                        eng = "pool" if n % 2 == 0 else "dve"
                        S.op(eng, lambda e, s=s, k=k, c0=c0, c1=c1: e.tensor_scalar(out=wt[:, k, c0:c1], in0=s[:, 0:c1 - c0], scalar1=vec[:, gcol + k:gcol + k + 1], scalar2=None, op0=ALU.mult), [s.b, vec.b], [wt.b])
                    n += 1
            return wt

        def rmsnorm(x, xn, junk, ss):
            A(lambda e: e.activation(out=junk[:], in_=x[:], func=AF.Square, accum_out=ss[:, 0:1]), [x], [junk, ss])
            V(lambda e: e.tensor_scalar(out=ss[:, 1:2], in0=ss[:, 0:1], scalar1=1.0 / 1024, scalar2=EPS, op0=ALU.mult, op1=ALU.add), [ss], [ss])
            A(lambda e: e.activation(out=ss[:, 1:2], in_=ss[:, 1:2], func=AF.Ln), [ss], [ss])
            A(lambda e: e.activation(out=ss[:, 1:2], in_=ss[:, 1:2], func=AF.Exp, scale=-0.5), [ss], [ss])
            V(lambda e: e.tensor_scalar(out=xn[:], in0=x[:], scalar1=ss[:, 1:2], scalar2=None, op0=ALU.mult), [x, ss], [xn])

        def transpose8(xn, xnT, pb):
            pbb = bank_bf(pb)
            for k in range(8):
                P(lambda e, k=k: e.transpose(pbb[:, k * 128:(k + 1) * 128], xn[:, k * 128:(k + 1) * 128], ident[:]), [xn, ident], [pb])
            V(lambda e: e.tensor_copy(out=xnT[:].rearrange("p k t -> p (k t)"), in_=pbb[:, :]), [pb], [xnT])

        def rstd_cols(ss, c0, c1, n):
            V(lambda e: e.tensor_scalar(out=ss[:, c0:c1], in0=ss[:, c0:c1], scalar1=1.0 / n, scalar2=EPS, op0=ALU.mult, op1=ALU.add), [ss], [ss])

        def sqrt_recip(ss, c0, c1):
            A(lambda e: e.activation(out=ss[:, c0:c1], in_=ss[:, c0:c1], func=AF.Ln), [ss], [ss])
            A(lambda e: e.activation(out=ss[:, c0:c1], in_=ss[:, c0:c1], func=AF.Exp, scale=-0.5), [ss], [ss])

        def rope(dst, src, cs, H, tmp):
            cosb = cs[:, 0:16].unsqueeze(1).broadcast_to([128, H, 16])
            sinb = cs[:, 16:32].unsqueeze(1).broadcast_to([128, H, 16])
            x1 = src(0, 16); x2 = src(16, 32)
            t = tmp.t[:, 0:H, :]
            V(lambda e: e.tensor_tensor(out=t[:, :, 0:16], in0=x2, in1=sinb, op=ALU.mult), [src.tl, cs], [tmp])
            V(lambda e: e.tensor_tensor(out=t[:, :, 16:32], in0=x2, in1=cosb, op=ALU.mult), [src.tl, cs], [tmp])
            V(lambda e: e.tensor_tensor(out=dst(0, 16), in0=x1, in1=cosb, op=ALU.mult), [src.tl, cs], [dst.tl])
            V(lambda e: e.tensor_tensor(out=dst(16, 32), in0=x1, in1=sinb, op=ALU.mult), [src.tl, cs], [dst.tl])
            V(lambda e: e.tensor_tensor(out=dst(0, 16), in0=dst(0, 16), in1=t[:, :, 0:16], op=ALU.subtract), [dst.tl, tmp], [dst.tl])
            V(lambda e: e.tensor_tensor(out=dst(16, 32), in0=dst(16, 32), in1=t[:, :, 16:32], op=ALU.add), [dst.tl, tmp], [dst.tl])

        class APM:
            def __init__(self, tl, f):
                self.tl = tl; self.f = f
            def __call__(self, lo, hi):
                return self.f(lo, hi)

        S.end_stage()
        if upto <= 0:
            return nc

        def l0_stage(SAMPLE):
          with ExitStack() as st:
            stg = [mk(st, f"stg{j}", [128, 1024], F32) for j in range(2)]
            win = load_w(st, "win", w_in, 8, 1568, GM0, stg=stg)
            wout = load_w(st, "wout", w_out, 8, 1024, stg=stg)
            wuk = load_w(st, "wuk", w_uk, 2, 512, stg=stg)
            wuv = load_w(st, "wuv", w_uv, 2, 512, stg=stg)
            wpl = mk(st, "wpl", [128, 4, 128], BF16)
            s32 = stg[0]
            LD(s32, s32[:, 0:512].rearrange("p (g n) -> p g n", g=4), w_pool.rearrange("g p n -> p g n"))
            G(lambda e: e.tensor_copy(out=wpl[:].rearrange("p g n -> p (g n)"), in_=s32[:, 0:512]), [s32], [wpl])
            band = mk(st, "band", [128, 20, 128], BF16)
            for q4 in range(5):
                LD(s32, s32[:, 0:512], c_band[:, q4 * 512:(q4 + 1) * 512])
                G(lambda e, q4=q4: e.tensor_copy(out=band[:, q4 * 4:q4 * 4 + 4, :].rearrange("p a n -> p (a n)"), in_=s32[:, 0:512]), [s32], [band])
            maskc = mk(st, "maskc", [128, 128], BF16)
            LD(s32, s32[:, 0:128], c_mask[:, :])
            G(lambda e: e.tensor_copy(out=maskc[:], in_=s32[:, 0:128]), [s32], [maskc])
            if SAMPLE:
                wukT = mk(st, "wukT", [64, 8, 256], BF16)
                for q2 in range(2):
                    LD(s32, s32[0:64, :], w_ukT[:, q2 * 1024:(q2 + 1) * 1024])
                    G(lambda e, q2=q2: e.tensor_copy(out=wukT[:, q2 * 4:q2 * 4 + 4, :].rearrange("p h n -> p (h n)"), in_=s32[0:64, :]), [s32], [wukT])
                bandh = mk(st, "bandh", [128, 8, 128], BF16)
                mnew = mk(st, "mnew", [128, NB, 64], BF16)
                LD(s32, s32[:, 0:1024], c_bandh[:, :])
                G(lambda e: e.tensor_copy(out=bandh[:].rearrange("p a n -> p (a n)"), in_=s32[:, 0:1024]), [s32], [bandh])
                LD(s32, s32[:, 0:1024], c_mnew[:, :])
                G(lambda e: e.tensor_copy(out=mnew[:].rearrange("p a n -> p (a n)"), in_=s32[:, 0:1024]), [s32], [mnew])
            else:
                bandh = band
                KT = mk(st, "KT", [96, 8, 4096], BF16)
                KTb = [Buf(f"KT{i}") for i in range(NT)]
                VT = mk(st, "VT", [128, NT, 8, 65], BF16)
                VTb = [Buf(f"VT{i}") for i in range(NT)]
                S.op("pool", lambda e: e.memset(VT[:].rearrange("p a h c -> p (a h c)"), 1.0), [], VTb)

            xr = [mk(st, f"x{j}", [128, 1024], F32) for j in range(1)]
            junk = mk(st, "junk", [128, 1056], F32)
            ssx = mk(st, "ssx", [128, 2], F32)
            xn = mk(st, "xn", [128, 1024], BF16)
            xnT = mk(st, "xnT", [128, 8, 128], BF16)
            ub = [mk(st, f"ub{j}", [128, 512], BF16) for j in range(2)]
            zf = mk(st, "zf", [128, 1056], F32)
            sq = junk
            ssz = mk(st, "ssz", [128, 18], F32)
            qn32 = mk(st, "qn32", [128, 8, 96], F32)
            rtmp = mk(st, "rtmp", [128, 8, 32], F32)
            qfb = mk(st, "qfb", [128, 8, 96], BF16)
            qT = mk(st, "qT", [96, 8, 128], BF16)
            cn = [mk(st, f"cn{j}", [128, 256], F32) for j in range(2)]
            krn = mk(st, "krn", [128, 1, 32], F32)
            kro = [mk(st, f"kro{j}", [128, 1, 32], F32) for j in range(2)]
            cnb = mk(st, "cnb", [128, 256], BF16)
            cnT = mk(st, "cnT", [128, 2, 128], BF16)
            ksq = mk(st, "ksq", [128, 512], F32)
            ssk = mk(st, "ssk", [128, 8], F32)
            kfb = mk(st, "kfb", [128, 8, 96], BF16)
            cs = [mk(st, f"cs{j}", [128, 32], F32) for j in range(2)]
            pT = [mk(st, f"pT{j}", [128, 4, 128], BF16) for j in range(2)]
            orec = mk(st, "orec", [128, 8], F32)
            attb = mk(st, "attb", [128, 8, 64], BF16)
            dTb = mk(st, "dTb", [128, 4, 128], BF16)
            mixT = mk(st, "mixT", [128, 8, 128], BF16)
            xo = [mk(st, f"xo{j}", [128, 1024], F32) for j in range(1)]
            u32 = xo[0]
            if SAMPLE:
              hist = [mk(st, f"hist{j}", [128, 512], F32) for j in range(2)]
              histb = [mk(st, f"histb{j}", [128, 512], BF16) for j in range(2)]
              qgb = mk(st, "qgb", [128, 8, 64], BF16)
              qgT = mk(st, "qgT", [64, 8, 128], BF16)
              qrT = mk(st, "qrT", [32, 8, 128], BF16)
              QL = mk(st, "QL", [128, 2, NB, 64], BF16)
              qrB = mk(st, "qrB", [32, NB, 64], BF16)
              pidx = mk(st, "pidx", [128, NB * 64], I32)
              cpg = [mk(st, f"cpg{j}", [128, 256], F32) for j in range(3)]
              kpg = [mk(st, f"kpg{j}", [128, 32], F32) for j in range(3)]
              cb = [mk(st, f"cb{j}", [128, 257], BF16) for j in range(2)]
              krb2 = [mk(st, f"krb{j}", [128, 32], BF16) for j in range(2)]
              cT2 = [mk(st, f"cT{j}", [128, 2, 128], BF16) for j in range(2)]
              krT2 = [mk(st, f"krT{j}", [32, 128], BF16) for j in range(2)]
              ssp2 = [mk(st, f"ssp{j}", [128, 8], F32) for j in range(2)]
              sc2 = [mk(st, f"sc{j}", [128, 64], F32) for j in range(2)]
              ppT2 = [mk(st, f"ppT{j}", [128, 64], BF16) for j in range(2)]
              ksq2 = [ksq, mk(st, "ksqB", [128, 512], F32)]
              lrec = mk(st, "lrec", [64, 1], F32)
              latb = mk(st, "latb", [64, 256], BF16)
              LT = mk(st, "LT", [128, 2, 8, 128], BF16)
              for j in range(2):
                  G(lambda e, j=j: e.memset(cb[j][:, 256:257], 1.0), [], [cb[j]])

            def l0_tile(i):
                sample = (i == NT)
                x = xr[0]
                src = xs if sample else xp[rows(i), :]
                LD(x, x[:], src[:, :] if sample else src)
                c_ = cs[i % 2]
                LD(c_, c_[:], c_cs[rows(i), :], q="act")
                S.op("act", lambda e: e.activation(out=junk[:, 0:1024], in_=x[:], func=AF.Square, accum_out=ssx[:, 0:1]), [x.b], [junk.b, ssx.b])
                V(lambda e: e.tensor_scalar(out=ssx[:, 1:2], in0=ssx[:, 0:1], scalar1=1.0 / 1024, scalar2=EPS, op0=ALU.mult, op1=ALU.add), [ssx], [ssx])
                A(lambda e: e.activation(out=ssx[:, 1:2], in_=ssx[:, 1:2], func=AF.Ln), [ssx], [ssx])
                A(lambda e: e.activation(out=ssx[:, 1:2], in_=ssx[:, 1:2], func=AF.Exp, scale=-0.5), [ssx], [ssx])
                V(lambda e: e.tensor_scalar(out=xn[:], in0=x[:], scalar1=ssx[:, 1:2], scalar2=None, op0=ALU.mult), [x, ssx], [xn])
                transpose8(xn, xnT, PS[4])
                for n, (c0, c1) in enumerate([(0, 512), (512, 1024), (1024, 1536), (1536, 1568)]):
                    for k in range(8):
                        P(lambda e, n=n, k=k, c0=c0, c1=c1: e.matmul(PS[n][:, 0:c1 - c0], lhsT=xnT[:, k, :], rhs=win[:, k, c0:c1], start=(k == 0), stop=(k == 7)), [xnT, win], [PS[n]])
                ucur = ub[i % 2]
                A(lambda e: e.activation(out=ucur[:], in_=PS[0][:, :], func=AF.Copy), [PS[0]], [ucur])
                if i >= NT - 1:
                    _v = _os0.environ.get("KDBG_VAR", "0")
                    if _v == "0":
                        V(lambda e: e.tensor_copy(out=u32[:, 0:512], in_=PS[0][:, :]), [PS[0]], [u32])
                    elif _v == "1":
                        V(lambda e: e.tensor_copy(out=zf[:, 0:512], in_=PS[0][:, :]), [PS[0]], [zf])
                    elif _v == "2":
                        V(lambda e: e.tensor_copy(out=u32[:, 0:512], in_=PS[1][:, :]), [PS[1]], [u32])
                    elif _v == "3":
                        A(lambda e: e.activation(out=u32[:, 0:512], in_=PS[0][:, :], func=AF.Copy), [PS[0]], [u32])
                    if i == NT - 1:
                        ST(u32, o_poolp[:, :], u32[113:128, 0:512])
                    else:
                        for b in range(NB):
                            ST(u32, o_pools[b, 7:15, :], u32[b * 8:(b + 1) * 8, 0:512])
                A(lambda e: e.activation(out=zf[:, 0:512], in_=PS[1][:, :], func=AF.Copy), [PS[1]], [zf])
                V(lambda e: e.tensor_copy(out=zf[:, 512:1024], in_=PS[2][:, :]), [PS[2]], [zf])
                V(lambda e: e.tensor_copy(out=zf[:, 1024:1056], in_=PS[3][:, 0:32]), [PS[3]], [zf])
                G(lambda e: e.tensor_tensor(out=sq[:], in0=zf[:], in1=zf[:], op=ALU.mult), [zf], [sq])
                sqq = sq[:, 0:768].rearrange("p (h d) -> p h d", h=8)
                V(lambda e: e.tensor_reduce(out=ssz[:, 0:8], in_=sqq[:, :, 0:64], axis=AX.X, op=ALU.add), [sq], [ssz])
                V(lambda e: e.tensor_reduce(out=ssz[:, 8:16], in_=sqq[:, :, 64:96], axis=AX.X, op=ALU.add), [sq], [ssz])
                V(lambda e: e.tensor_reduce(out=ssz[:, 16:17], in_=sq[:, 768:1024], axis=AX.X, op=ALU.add), [sq], [ssz])
                V(lambda e: e.tensor_reduce(out=ssz[:, 17:18], in_=sq[:, 1024:1056], axis=AX.X, op=ALU.add), [sq], [ssz])
                rstd_cols(ssz, 0, 8, 64); rstd_cols(ssz, 8, 16, 32); rstd_cols(ssz, 16, 17, 256); rstd_cols(ssz, 17, 18, 32)
                sqrt_recip(ssz, 0, 18)
                zq = zf[:, 0:768].rearrange("p (h d) -> p h d", h=8)
                V(lambda e: e.tensor_tensor(out=qn32[:, :, 0:64], in0=zq[:, :, 0:64], in1=ssz[:, 0:8].unsqueeze(2).broadcast_to([128, 8, 64]), op=ALU.mult), [zf, ssz], [qn32])
                V(lambda e: e.tensor_tensor(out=qn32[:, :, 0:64], in0=qn32[:, :, 0:64], in1=row[:, RQN:RQN + 64].unsqueeze(1).broadcast_to([128, 8, 64]), op=ALU.mult), [qn32, row], [qn32])
                V(lambda e: e.tensor_tensor(out=qn32[:, :, 64:96], in0=zq[:, :, 64:96], in1=ssz[:, 8:16].unsqueeze(2).broadcast_to([128, 8, 32]), op=ALU.mult), [zf, ssz], [qn32])
                V(lambda e: e.tensor_tensor(out=qn32[:, :, 64:96], in0=qn32[:, :, 64:96], in1=row[:, RQR:RQR + 32].unsqueeze(1).broadcast_to([128, 8, 32]), op=ALU.mult), [qn32, row], [qn32])
                G(lambda e: e.tensor_copy(out=qfb[:, :, 0:64], in_=qn32[:, :, 0:64]), [qn32], [qfb])
                rope(APM(qfb, lambda lo, hi: qfb[:, :, 64 + lo:64 + hi]), APM(qn32, lambda lo, hi: qn32[:, :, 64 + lo:64 + hi]), c_, 8, rtmp)
                cn_ = cn[i % 2]
                V(lambda e: e.scalar_tensor_tensor(out=cn_[:], in0=zf[:, 768:1024], scalar=ssz[:, 16:17], in1=row[:, RCKV:RCKV + 256], op0=ALU.mult, op1=ALU.mult), [zf, ssz, row], [cn_])
                ST(cn_, (o_ckvs[:, :] if sample else o_ckvp[rows(i), :]), cn_[:])
                V(lambda e: e.scalar_tensor_tensor(out=krn[:, 0, :], in0=zf[:, 1024:1056], scalar=ssz[:, 17:18], in1=row[:, RKR:RKR + 32], op0=ALU.mult, op1=ALU.mult), [zf, ssz, row], [krn])
                kro_ = kro[i % 2]
                rope(APM(kro_, lambda lo, hi: kro_[:, :, lo:hi]), APM(krn, lambda lo, hi: krn[:, :, lo:hi]), c_, 1, rtmp)
                ST(kro_, (o_krs[:, :] if sample else o_krp[rows(i), :]), kro_[:, 0, :])
                b4 = bank_bf(PS[4])
                if not sample:
                    G(lambda e: e.tensor_copy(out=cnb[:], in_=cn_[:]), [cn_], [cnb])
                    for k in range(2):
                        P(lambda e, k=k: e.transpose(b4[:, k * 128:(k + 1) * 128], cnb[:, k * 128:(k + 1) * 128], ident[:]), [cnb, ident], [PS[4]])
                    V(lambda e: e.tensor_copy(out=cnT[:].rearrange("p k t -> p (k t)"), in_=b4[:, 0:256]), [PS[4]], [cnT])
                    for k in range(2):
                        P(lambda e, k=k: e.matmul(PS[0][:, :], lhsT=cnT[:, k, :], rhs=wuk[:, k, :], start=(k == 0), stop=(k == 1)), [cnT, wuk], [PS[0]])
                    for k in range(2):
                        P(lambda e, k=k: e.matmul(PS[1][:, :], lhsT=cnT[:, k, :], rhs=wuv[:, k, :], start=(k == 0), stop=(k == 1)), [cnT, wuv], [PS[1]])
                    A(lambda e: e.activation(out=ksq[:], in_=PS[0][:, :], func=AF.Square), [PS[0]], [ksq])
                    V(lambda e: e.tensor_reduce(out=ssk[:], in_=ksq[:].rearrange("p (h d) -> p h d", h=8), axis=AX.X, op=ALU.add), [ksq], [ssk])
                    rstd_cols(ssk, 0, 8, 64); sqrt_recip(ssk, 0, 8)
                    V(lambda e: e.tensor_tensor(out=ksq[:].rearrange("p (h d) -> p h d", h=8), in0=PS[0][:, :].rearrange("p (h d) -> p h d", h=8), in1=ssk[:, 0:8].unsqueeze(2).broadcast_to([128, 8, 64]), op=ALU.mult), [PS[0], ssk], [ksq])
                    V(lambda e: e.tensor_tensor(out=kfb[:, :, 0:64], in0=ksq[:].rearrange("p (h d) -> p h d", h=8), in1=row[:, RKN:RKN + 64].unsqueeze(1).broadcast_to([128, 8, 64]), op=ALU.mult), [ksq, row], [kfb])
                    V(lambda e: e.tensor_copy(out=kfb[:, :, 64:96], in_=kro_[:, 0:1, :].broadcast_to([128, 8, 32])), [kro_], [kfb])
                    S.op("act", lambda e: e.activation(out=VT[:, i, :, 0:64], in_=PS[1][:, :].rearrange("p (h d) -> p h d", h=8), func=AF.Copy), [PS[1].b], [VTb[i]])
                    for h in range(8):
                        P(lambda e, h=h: e.transpose(b4[0:96, h * 128:(h + 1) * 128], kfb[:, h, :], ident[:]), [kfb, ident], [PS[4]])
                    S.op("dve", lambda e: e.tensor_copy(out=KT[:, :, rows(i)], in_=b4[0:96, :].rearrange("p (h t) -> p h t", h=8)), [PS[4].b], [KTb[i]])
                    for h in range(8):
                        P(lambda e, h=h: e.transpose(b4[0:96, h * 128:(h + 1) * 128], qfb[:, h, :], ident[:]), [qfb, ident], [PS[4]])
                    V(lambda e: e.tensor_copy(out=qT[:].rearrange("p h t -> p (h t)"), in_=b4[0:96, :]), [PS[4]], [qT])
                    g = 0
                    for h in range(8):
                        ob = PS[5 + h // 4]
                        oc = (h % 4) * 65
                        for j0 in range(0, i + 1, 4):
                            js = list(range(j0, min(i + 1, j0 + 4)))
                            sb_ = PS[2 + g % 2]; pt_ = pT[g % 2]; g += 1
                            for jj, j in enumerate(js):
                                S.op("pe", lambda e, jj=jj, j=j, h=h, sb_=sb_: e.matmul(sb_[:, jj * 128:(jj + 1) * 128], lhsT=KT[:, h, rows(j)], rhs=qT[:, h, :], start=True, stop=True), [KTb[j], qT.b], [sb_.b])
                            n = len(js)
                            A(lambda e, n=n, sb_=sb_, pt_=pt_: e.activation(out=pt_[:, 0:n, :].rearrange("p a t -> p (a t)"), in_=sb_[:, 0:n * 128], func=AF.Exp, scale=SCALE), [sb_], [pt_])
                            if js[-1] == i:
                                jj = len(js) - 1
                                G(lambda e, jj=jj, pt_=pt_: e.tensor_tensor(out=pt_[:, jj, :], in0=pt_[:, jj, :], in1=maskc[:], op=ALU.mult), [pt_, maskc], [pt_])
                            for jj, j in enumerate(js):
                                S.op("pe", lambda e, jj=jj, j=j, h=h, pt_=pt_, ob=ob, oc=oc: e.matmul(ob[:, oc:oc + 65], lhsT=pt_[:, jj, :], rhs=VT[:, j, h, :], start=(j == 0), stop=(j == i)), [pt_.b, VTb[j]], [ob.b])
                    for hb in range(2):
                        ob = PS[5 + hb]
                        ov = ob[:, 0:260].rearrange("p (h c) -> p h c", h=4)
                        V(lambda e, hb=hb, ov=ov: e.reciprocal(out=orec[:, hb * 4:hb * 4 + 4], in_=ov[:, :, 64]), [ob], [orec])
                        V(lambda e, hb=hb, ov=ov: e.tensor_tensor(out=attb[:, hb * 4:hb * 4 + 4, :], in0=ov[:, :, 0:64], in1=orec[:, hb * 4:hb * 4 + 4].unsqueeze(2).broadcast_to([128, 4, 64]), op=ALU.mult), [ob, orec], [attb])
                    for c in range(4):
                        P(lambda e, c=c: e.transpose(b4[:, c * 128:(c + 1) * 128], attb[:, 2 * c:2 * c + 2, :].rearrange("p h d -> p (h d)"), ident[:]), [attb, ident], [PS[4]])
                    V(lambda e: e.tensor_copy(out=mixT[:, 4:8, :].rearrange("p k t -> p (k t)"), in_=b4[:, 0:512]), [PS[4]], [mixT])
                else:
                    sample_attention(kro_, cn_)
                for g_ in range(4):
                    if sample:
                        mats = [(ucur, None, band[:, 16 + g_, :]), (histb[0], 128, bandh[:, 2 * g_, :]), (histb[1], 112, bandh[0:112, 2 * g_ + 1, :])]
                    elif i == 0:
                        mats = [(ucur, None, band[:, 4 + g_, :])]
                    else:
                        mats = [(ucur, None, band[:, g_, :]), (ub[(i - 1) % 2], None, band[:, 8 + g_, :])]
                    for mi, (ut, nr, bm) in enumerate(mats):
                        lhs = ut[:, g_ * 128:(g_ + 1) * 128] if nr is None else ut[0:nr, g_ * 128:(g_ + 1) * 128]
                        P(lambda e, lhs=lhs, bm=bm, g_=g_, mi=mi, nm=len(mats): e.matmul(PS[7][:, g_ * 128:(g_ + 1) * 128], lhsT=lhs, rhs=bm, start=(mi == 0), stop=(mi == nm - 1)), [ut, band, bandh], [PS[7]])
                V(lambda e: e.tensor_copy(out=dTb[:].rearrange("p g t -> p (g t)"), in_=PS[7][:, :]), [PS[7]], [dTb])
                for g_ in range(4):
                    P(lambda e, g_=g_: e.matmul(PS[7][:, g_ * 128:(g_ + 1) * 128], lhsT=wpl[:, g_, :], rhs=dTb[:, g_, :], start=True, stop=True), [wpl, dTb], [PS[7]])
                for g_ in range(4):
                    V(lambda e, g_=g_: e.tensor_scalar(out=mixT[:, g_, :], in0=PS[7][:, g_ * 128:(g_ + 1) * 128], scalar1=vec[:, PSC + g_:PSC + g_ + 1], scalar2=None, op0=ALU.mult), [PS[7], vec], [mixT])
                xo_ = xo[0]
                for n in range(2):
                    for k in range(8):
                        P(lambda e, n=n, k=k: e.matmul(PS[n][:, :], lhsT=mixT[:, k, :], rhs=wout[:, k, n * 512:(n + 1) * 512], start=(k == 0), stop=(k == 7)), [mixT, wout], [PS[n]])
                    V(lambda e, n=n: e.tensor_tensor(out=xo_[:, n * 512:(n + 1) * 512], in0=PS[n][:, :], in1=x[:, n * 512:(n + 1) * 512], op=ALU.add), [PS[n], x], [xo_])
                ST(xo_, X1[rows(i), :], xo_[:])

            def sample_attention(kro_, cn_):
                b4 = bank_bf(PS[4])
                LD(hist[0], hist[0][:, :], spool[0:128, :])
                LD(hist[1], hist[1][0:112, :], spool[128:240, :])
                G(lambda e: e.tensor_copy(out=histb[0][:], in_=hist[0][:]), [hist[0]], [histb[0]])
                G(lambda e: e.tensor_copy(out=histb[1][0:112, :], in_=hist[1][0:112, :]), [hist[1]], [histb[1]])
                for b in range(NB):
                    r0 = b * 15 + 8
                    hh = hist[0] if r0 < 128 else hist[1]
                    rr = r0 if r0 < 128 else r0 - 128
                    ST(hh, o_pools[b, 0:7, :], hh[rr:rr + 7, :])
                V(lambda e: e.tensor_tensor(out=qgb[:], in0=qn32[:, :, 0:64], in1=row[:, RKN:RKN + 64].unsqueeze(1).broadcast_to([128, 8, 64]), op=ALU.mult), [qn32, row], [qgb])
                for h in range(8):
                    P(lambda e, h=h: e.transpose(b4[0:64, h * 128:(h + 1) * 128], qgb[:, h, :], ident[:]), [qgb, ident], [PS[4]])
                V(lambda e: e.tensor_copy(out=qgT[:].rearrange("p h t -> p (h t)"), in_=b4[0:64, :]), [PS[4]], [qgT])
                for h in range(8):
                    P(lambda e, h=h: e.transpose(b4[0:32, h * 128:(h + 1) * 128], qfb[:, h, 64:96], ident[:]), [qfb, ident], [PS[4]])
                V(lambda e: e.tensor_copy(out=qrT[:].rearrange("p h t -> p (h t)"), in_=b4[0:32, :]), [PS[4]], [qrT])
                V(lambda e: e.tensor_copy(out=qrB[:].rearrange("p b (h t) -> p b h t", h=8), in_=qrT[:].rearrange("p h (b t) -> p b h t", b=NB)), [qrT], [qrB])
                for k in range(2):
                    for hq in range(2):
                        pb_ = PS[hq]
                        for h4 in range(4):
                            h = hq * 4 + h4
                            P(lambda e, h=h, h4=h4, k=k, pb_=pb_: e.matmul(pb_[:, h4 * 128:(h4 + 1) * 128], lhsT=wukT[:, h, k * 128:(k + 1) * 128], rhs=qgT[:, h, :], start=True, stop=True), [wukT, qgT], [pb_])
                        V(lambda e, k=k, hq=hq, pb_=pb_: e.tensor_copy(out=QL[:, k, :, hq * 32:(hq + 1) * 32].rearrange("p b (h t) -> p b h t", h=4), in_=pb_[:, :].rearrange("p (h b t) -> p b h t", h=4, b=NB)), [pb_], [QL])
                LD(pidx, pidx[:], ptab[:, :])
                V(lambda e: e.tensor_scalar(out=pidx[:], in0=pidx[:], scalar1=128.0, scalar2=pcol[:, 0:1], op0=ALU.mult, op1=ALU.add), [pidx, pcol], [pidx])

                cnt = [0]

                def proc(c_tl, c_ap, k_tl, k_ap, b, first, last, mask_ap):
                    n = cnt[0]; cnt[0] += 1
                    cb_ = cb[n % 2]
                    krb = krb2[n % 2]; cT = cT2[n % 2]; krT = krT2[n % 2]; ssp = ssp2[n % 2]
                    sc = sc2[n % 2]; ppT = ppT2[n % 2]; ksq = ksq2[n % 2]
                    A(lambda e: e.activation(out=cb_[:, 0:256], in_=c_ap, func=AF.Copy), [c_tl], [cb_])
                    A(lambda e: e.activation(out=krb[:], in_=k_ap, func=AF.Copy), [k_tl], [krb])
                    for k in range(2):
                        P(lambda e, k=k: e.transpose(b4[:, k * 128:(k + 1) * 128], cb_[:, k * 128:(k + 1) * 128], ident[:]), [cb_, ident], [PS[4]])
                    P(lambda e: e.transpose(b4[0:32, 256:384], krb[:], ident[:]), [krb, ident], [PS[4]])
                    V(lambda e: e.tensor_copy(out=cT[:].rearrange("p k t -> p (k t)"), in_=b4[:, 0:256]), [PS[4]], [cT])
                    V(lambda e: e.tensor_copy(out=krT[:], in_=b4[0:32, 256:384]), [PS[4]], [krT])
                    kb_ = PS[n % 2]
                    for k in range(2):
                        P(lambda e, k=k: e.matmul(kb_[:, :], lhsT=cT[:, k, :], rhs=wuk[:, k, :], start=(k == 0), stop=(k == 1)), [cT, wuk], [kb_])
                    A(lambda e: e.activation(out=ksq[:], in_=kb_[:, :], func=AF.Square), [kb_], [ksq])
                    V(lambda e: e.tensor_reduce(out=ssp[:], in_=ksq[:].rearrange("p (h d) -> p h d", h=8), axis=AX.X, op=ALU.add), [ksq], [ssp])
                    rstd_cols(ssp, 0, 8, 64); sqrt_recip(ssp, 0, 8)
                    sb_ = PS[2 + n % 2]
                    for k in range(2):
                        P(lambda e, k=k: e.matmul(sb_[:, 0:64], lhsT=cT[:, k, :], rhs=QL[:, k, b, :], start=(k == 0), stop=(k == 1)), [cT, QL], [sb_])
                    P(lambda e: e.matmul(sb_[:, 64:128], lhsT=krT[:, :], rhs=qrB[:, b, :], start=True, stop=True), [krT, qrB], [sb_])
                    V(lambda e: e.tensor_tensor(out=sc[:].rearrange("p (h t) -> p h t", h=8), in0=sb_[:, 0:64].rearrange("p (h t) -> p h t", h=8), in1=ssp[:, 0:8].unsqueeze(2).broadcast_to([128, 8, 8]), op=ALU.mult), [sb_, ssp], [sc])
                    V(lambda e: e.tensor_tensor(out=sc[:], in0=sc[:], in1=sb_[:, 64:128], op=ALU.add), [sc, sb_], [sc])
                    A(lambda e: e.activation(out=ppT[:], in_=sc[:], func=AF.Exp, scale=SCALE), [sc], [ppT])
                    if mask_ap is not None:
                        V(lambda e: e.tensor_tensor(out=ppT[:], in0=ppT[:], in1=mask_ap, op=ALU.mult), [ppT, mnew], [ppT])
                    P(lambda e: e.matmul(PS[5][0:64, 0:257], lhsT=ppT[:], rhs=cb_[:, :], start=first, stop=last), [ppT, cb_], [PS[5]])

                for b in range(NB):
                    for j in range(int(os.environ.get('KDBG_PAGES', '64'))):
                        n = cnt[0]
                        cp = cpg[n % 3]; kp = kpg[n % 3]
                        col = b * 64 + j
                        S.dma("pool", lambda e, cp=cp, col=col: e.indirect_dma_start(out=cp[:, :], out_offset=None, in_=ckv[:, :], in_offset=bass.IndirectOffsetOnAxis(ap=pidx[:, col:col + 1], axis=0)), cp.b, True, extra_reads=[pidx.b])
                        S.dma("pool", lambda e, kp=kp, col=col: e.indirect_dma_start(out=kp[:, :], out_offset=None, in_=krp[:, :], in_offset=bass.IndirectOffsetOnAxis(ap=pidx[:, col:col + 1], axis=0)), kp.b, True, extra_reads=[pidx.b])
                        proc(cp, cp[:, :], kp, kp[:, :], b, j == 0, False, None)
                    proc(cn_, cn_[:, :], kro_, kro_[:, 0, :], b, False, True, mnew[:, b, :])
                    V(lambda e: e.reciprocal(out=lrec[:], in_=PS[5][0:64, 256:257]), [PS[5]], [lrec])
                    V(lambda e: e.tensor_scalar(out=latb[:], in0=PS[5][0:64, 0:256], scalar1=lrec[:, 0:1], scalar2=None, op0=ALU.mult), [PS[5], lrec], [latb])
                    for k in range(2):
                        P(lambda e, k=k: e.transpose(b4[:, 512 + k * 64:512 + (k + 1) * 64], latb[:, k * 128:(k + 1) * 128], ident[0:64, 0:64]), [latb, ident], [PS[4]])
                    V(lambda e, b=b: e.tensor_copy(out=LT[:, :, :, b * 8:(b + 1) * 8], in_=b4[:, 512:640].rearrange("p (k h t) -> p k h t", k=2, h=8)), [PS[4]], [LT])
                    S.end_stage()
                for h in range(8):
                    c = h // 2
                    for k in range(2):
                        P(lambda e, h=h, k=k, c=c: e.matmul(PS[6][(h % 2) * 64:(h % 2) * 64 + 64, c * 128:(c + 1) * 128], lhsT=wuv[:, k, h * 64:(h + 1) * 64], rhs=LT[:, k, h, :], start=(k == 0), stop=(k == 1)), [wuv, LT], [PS[6]])
                V(lambda e: e.tensor_copy(out=mixT[:, 4:8, :].rearrange("p k t -> p (k t)"), in_=PS[6][:, :]), [PS[6]], [mixT])

            pcol = mk(st, "pcol", [128, 1], F32)
            LD(pcol, pcol[:], c_pcol[:, :])
            if SAMPLE:
                l0_tile(NT)
            else:
                for i in range(NT):
                    l0_tile(i)
                    if i % 4 == 3 and i != NT - 1:
                        S.end_stage()
            S.end_stage()
            S.release([t for t in S.stage_bufs])
            S.stage_bufs = []

        l0_stage(False)
        if upto <= 1:
            return nc
        if os.environ.get("KDBG_SKIPS") != "1":
            l0_stage(True)
        if upto <= 2:
            return nc

        def mlp_stage(layer, Xin, Xout_fn, gcol):
            with ExitStack() as st:
                stg = [mk(st, f"mstg{j}", [128, 1024], F32) for j in range(2)]
                wup = load_w(st, "wup", w_up[layer], 8, 4096, gcol, stg=stg)
                wdn = load_w(st, "wdn", w_dn[layer], 32, 1024, stg=stg)
                xr = [mk(st, f"mx{j}", [128, 1024], F32) for j in range(2)]
                junk = mk(st, "mjunk", [128, 1024], F32)
                ssx = mk(st, "mssx", [128, 2], F32)
                xn = mk(st, "mxn", [128, 1024], BF16)
                xnT = mk(st, "mxnT", [128, 8, 128], BF16)
                rl = [mk(st, f"rl{j}", [128, 512], BF16) for j in range(2)]
                hT = mk(st, "hT", [128, 32, 128], BF16)
                xo = [mk(st, f"mxo{j}", [128, 1024], F32) for j in range(2)]
                for i in range(NT + 1):
                    x = xr[i % 2]
                    LD(x, x[:], Xin[rows(i), :])
                    rmsnorm(x, xn, junk, ssx)
                    transpose8(xn, xnT, PS[4])
                    for mg in range(8):
                        pb_ = PS[mg % 2]
                        for m4 in range(4):
                            m = mg * 4 + m4
                            for k in range(8):
                                P(lambda e, m=m, m4=m4, k=k, pb_=pb_: e.matmul(pb_[:, m4 * 128:(m4 + 1) * 128], lhsT=wup[:, k, m * 128:(m + 1) * 128], rhs=xnT[:, k, :], start=(k == 0), stop=(k == 7)), [wup, xnT], [pb_])
                        r_ = rl[mg % 2]
                        A(lambda e, pb_=pb_, r_=r_: e.activation(out=r_[:], in_=pb_[:, :], func=AF.Relu), [pb_], [r_])
                        G(lambda e, mg=mg, r_=r_: e.tensor_tensor(out=hT[:, mg * 4:mg * 4 + 4, :].rearrange("p m t -> p (m t)"), in0=r_[:], in1=r_[:], op=ALU.mult), [r_], [hT])
                    xo_ = xo[i % 2]
                    for n in range(2):
                        for m in range(32):
                            P(lambda e, n=n, m=m: e.matmul(PS[2 + n][:, :], lhsT=hT[:, m, :], rhs=wdn[:, m, n * 512:(n + 1) * 512], start=(m == 0), stop=(m == 31)), [hT, wdn], [PS[2 + n]])
                        V(lambda e, n=n, x=x, xo_=xo_: e.tensor_tensor(out=xo_[:, n * 512:(n + 1) * 512], in0=PS[2 + n][:, :], in1=x[:, n * 512:(n + 1) * 512], op=ALU.add), [PS[2 + n], x], [xo_])
                    ST(xo_, Xout_fn(i), xo_[:])
                    if i % 4 == 3:
                        S.end_stage()
                S.end_stage()
                S.release([t for t in S.stage_bufs]); S.stage_bufs = []

        mlp_stage(0, X1, lambda i: X2[rows(i), :], GF0)
        if upto <= 3:
            return nc

        with ExitStack() as st:
            stg = [mk(st, f"estg{j}", [128, 1024], F32) for j in range(2)]
            winr = load_w(st, "winr", w_inr, 8, 2048, GM1, stg=stg)
            woutr = load_w(st, "woutr", w_outr, 8, 1024, stg=stg)
            wga = mk(st, "wga", [128, 8, 128], BF16)
            wgx = mk(st, "wgx", [128, 8, 128], BF16)
            s32 = mk(st, "e_s32", [128, 1024], F32)
            LD(s32, s32[:, :].rearrange("p (g n) -> p g n", g=8), w_ga.rearrange("g p n -> p g n"))
            G(lambda e: e.tensor_copy(out=wga[:].rearrange("p g n -> p (g n)"), in_=s32[:, :]), [s32], [wga])
            LD(s32, s32[:, :].rearrange("p (g n) -> p g n", g=8), w_gx.rearrange("g p n -> p g n"))
            G(lambda e: e.tensor_copy(out=wgx[:].rearrange("p g n -> p (g n)"), in_=s32[:, :]), [s32], [wgx])
            sp8 = mk(st, "sp8", [128, 16], F32)
            A(lambda e: e.activation(out=sp8[:, 0:8], in_=vec[:, LAM:LAM + 8], func=AF.Exp, scale=-1.0), [vec], [sp8])
            A(lambda e: e.activation(out=sp8[:, 0:8], in_=sp8[:, 0:8], func=AF.Ln, bias=1.0), [sp8], [sp8])
            V(lambda e: e.tensor_scalar(out=sp8[:, 8:16], in0=sp8[:, 0:8], scalar1=-16.0, scalar2=None, op0=ALU.mult), [sp8], [sp8])
            V(lambda e: e.tensor_scalar(out=sp8[:, 0:8], in0=sp8[:, 0:8], scalar1=-8.0, scalar2=None, op0=ALU.mult), [sp8], [sp8])
            xr = [mk(st, f"ex{j}", [128, 1024], F32) for j in range(2)]
            junk = mk(st, "ejunk", [128, 1024], F32)
            ssx = mk(st, "essx", [128, 2], F32)
            xn = mk(st, "exn", [128, 1024], BF16)
            xnT = mk(st, "exnT", [128, 8, 128], BF16)
            gt = mk(st, "gt", [128, 8, 128], F32)
            ue = [mk(st, f"ue{j}", [128, 8, 176], F32) for j in range(2)]
            v32 = mk(st, "v32", [128, 8, 128], F32)
            vb = mk(st, "vb", [128, 8, 128], BF16)
            rg = mk(st, "rg", [128, 8, 128], F32)
            ig = mk(st, "ig", [128, 8, 128], F32)
            aa = mk(st, "aa", [128, 8, 128], F32)
            a2 = mk(st, "a2", [128, 8, 128], F32)
            bb = mk(st, "bb", [128, 8, 128], F32)
            hh = [mk(st, f"hh{j}", [128, 8, 128], F32) for j in range(2)]
            t1 = mk(st, "t1", [128, 8, 128], F32)
            yT = mk(st, "yT", [128, 8, 128], BF16)
            xo = [mk(st, f"exo{j}", [128, 1024], F32) for j in range(2)]
            utm = mk(st, "utm", [128, 1024], F32)
            sst = mk(st, "sst", [64, 1024], F32)
            h0T = mk(st, "h0T", [128, 8, NB], F32)
            tmp16 = mk(st, "tmp16", [128, 8, NB], F32)
            hlast = mk(st, "hlast", [NB, 1024], F32)

            def flat(t):
                return t[:].rearrange("p k t -> p (k t)")

            def l1_tile(i):
                sample = (i == NT)
                x = xr[i % 2]
                LD(x, x[:], X2[rows(i), :])
                rmsnorm(x, xn, junk, ssx)
                transpose8(xn, xnT, PS[4])
                ue_ = ue[i % 2]
                if sample:
                    uev = ue_[:].rearrange("p k (b s) -> p k b s", s=11)
                if sample:
                    LD(sst, sst[0:48, :], sconv[:, :])
                    for blk in range(8):
                        P(lambda e, blk=blk: e.transpose(PS[5][:, blk * 48:(blk + 1) * 48], sst[0:48, blk * 128:(blk + 1) * 128], ident32[0:48, 0:48]), [sst, ident32], [PS[5]])
                    V(lambda e: e.tensor_copy(out=uev[:, :, :, 0:3], in_=PS[5][:, 0:384].rearrange("p (k b r) -> p k b r", k=8, b=NB)), [PS[5]], [ue_])
                    LD(sst, sst[0:NB, :], slru[:, :])
                    for blk in range(8):
                        P(lambda e, blk=blk: e.transpose(PS[5][:, blk * NB:(blk + 1) * NB], sst[0:NB, blk * 128:(blk + 1) * 128], ident32[0:NB, 0:NB]), [sst, ident32], [PS[5]])
                    V(lambda e: e.tensor_copy(out=h0T[:].rearrange("p k b -> p (k b)"), in_=PS[5][:, 0:8 * NB]), [PS[5]], [h0T])
                elif i == 0:
                    G(lambda e: e.memset(ue_[:, :, 0:3], 0.0), [], [ue_])
                else:
                    up_ = ue[(i - 1) % 2]
                    G(lambda e, up_=up_: e.tensor_copy(out=ue_[:, :, 0:3], in_=up_[:, :, 128:131]), [up_], [ue_])
                for mg in range(4):
                    pb_ = PS[mg % 2]
                    for m4 in range(4):
                        m = mg * 4 + m4
                        for k in range(8):
                            P(lambda e, m=m, m4=m4, k=k, pb_=pb_: e.matmul(pb_[:, m4 * 128:(m4 + 1) * 128], lhsT=winr[:, k, m * 128:(m + 1) * 128], rhs=xnT[:, k, :], start=(k == 0), stop=(k == 7)), [winr, xnT], [pb_])
                    if mg < 2:
                        A(lambda e, mg=mg, pb_=pb_: e.activation(out=gt[:, mg * 4:mg * 4 + 4, :].rearrange("p k t -> p (k t)"), in_=pb_[:, :], func=AF.Copy), [pb_], [gt])
                    else:
                        k0 = (mg - 2) * 4
                        if sample:
                            V(lambda e, k0=k0, pb_=pb_: e.tensor_copy(out=uev[:, k0:k0 + 4, :, 3:11], in_=pb_[:, :].rearrange("p (k b t) -> p k b t", k=4, b=NB)), [pb_], [ue_])
                        else:
                            V(lambda e, k0=k0, pb_=pb_: e.tensor_copy(out=ue_[:, k0:k0 + 4, 3:131], in_=pb_[:, :].rearrange("p (k t) -> p k t", k=4)), [pb_], [ue_])
                if i >= NT - 1:
                    for n in range(2):
                        for k in range(8):
                            P(lambda e, n=n, k=k: e.matmul(PS[2 + n][:, :], lhsT=xnT[:, k, :], rhs=winr[:, k, 1024 + n * 512:1024 + (n + 1) * 512], start=(k == 0), stop=(k == 7)), [xnT, winr], [PS[2 + n]])
                        A(lambda e, n=n: e.activation(out=utm[:, n * 512:(n + 1) * 512], in_=PS[2 + n][:, :], func=AF.Copy), [PS[2 + n]], [utm])
                    if sample:
                        for b in range(NB):
                            ST(utm, o_convs[b, :, :], utm[b * 8 + 5:b * 8 + 8, :])
                    else:
                        ST(utm, o_convp[:, :], utm[125:128, :])
                def uk(blk, k):
                    if sample:
                        return uev[:, blk, :, k:k + 8]
                    return ue_[:, blk, k:k + 128]
                def vv(t, blk):
                    if sample:
                        return t[:, blk, :].rearrange("p (b t) -> p b t", b=NB)
                    return t[:, blk, :]
                for blk in range(8):
                    V(lambda e, blk=blk: e.tensor_scalar(out=vv(v32, blk), in0=uk(blk, 0), scalar1=vec[:, CW + blk:CW + blk + 1], scalar2=vec[:, CB + blk:CB + blk + 1], op0=ALU.mult, op1=ALU.add), [ue_, vec], [v32])
                    for k in range(1, 4):
                        V(lambda e, blk=blk, k=k: e.scalar_tensor_tensor(out=vv(v32, blk), in0=uk(blk, k), scalar=vec[:, CW + k * 8 + blk:CW + k * 8 + blk + 1], in1=vv(v32, blk), op0=ALU.mult, op1=ALU.add), [ue_, vec, v32], [v32])
                G(lambda e: e.tensor_copy(out=flat(vb), in_=flat(v32)), [v32], [vb])
                for (wg, bcol, dst) in ((wga, BGA, rg), (wgx, BGX, ig)):
                    for half in range(2):
                        pb_ = PS[half]
                        for b4_ in range(4):
                            blk = half * 4 + b4_
                            P(lambda e, blk=blk, b4_=b4_, pb_=pb_, wg=wg: e.matmul(pb_[:, b4_ * 128:(b4_ + 1) * 128], lhsT=wg[:, blk, :], rhs=vb[:, blk, :], start=True, stop=True), [wg, vb], [pb_])
                        for b4_ in range(4):
                            blk = half * 4 + b4_
                            A(lambda e, blk=blk, b4_=b4_, pb_=pb_, dst=dst, bcol=bcol: e.activation(out=dst[:, blk, :], in_=pb_[:, b4_ * 128:(b4_ + 1) * 128], func=AF.Sigmoid, bias=vec[:, bcol + blk:bcol + blk + 1]), [pb_, vec], [dst])
                for blk in range(8):
                    A(lambda e, blk=blk: e.activation(out=aa[:, blk, :], in_=rg[:, blk, :], func=AF.Exp, scale=sp8[:, blk:blk + 1]), [rg, sp8], [aa])
                for blk in range(8):
                    A(lambda e, blk=blk: e.activation(out=a2[:, blk, :], in_=rg[:, blk, :], func=AF.Exp, scale=sp8[:, 8 + blk:9 + blk]), [rg, sp8], [a2])
                V(lambda e: e.tensor_scalar(out=flat(a2), in0=flat(a2), scalar1=-1.0, scalar2=1.0, op0=ALU.mult, op1=ALU.add), [a2], [a2])
                V(lambda e: e.tensor_scalar_max(out=flat(a2), in0=flat(a2), scalar1=0.0), [a2], [a2])
                if i == 0:
                    G(lambda e: e.memset(a2[:, :, 0:1], 1.0), [], [a2])
                A(lambda e: e.activation(out=flat(a2), in_=flat(a2), func=AF.Sqrt), [a2], [a2])
                V(lambda e: e.tensor_tensor(out=flat(bb), in0=flat(a2), in1=flat(ig), op=ALU.mult), [a2, ig], [bb])
                V(lambda e: e.tensor_tensor(out=flat(bb), in0=flat(bb), in1=flat(v32), op=ALU.mult), [bb, v32], [bb])
                h_ = hh[i % 2]
                if sample:
                    a0 = aa[:].rearrange("p k (b t) -> p k b t", b=NB)[:, :, :, 0]
                    b0 = bb[:].rearrange("p k (b t) -> p k b t", b=NB)[:, :, :, 0]
                    V(lambda e: e.tensor_tensor(out=tmp16[:], in0=a0, in1=h0T[:], op=ALU.mult), [aa, h0T], [tmp16])
                    V(lambda e: e.tensor_tensor(out=b0, in0=b0, in1=tmp16[:], op=ALU.add), [bb, tmp16], [bb])
                    G(lambda e: e.memset(a0, 0.0), [], [aa])
                if (not sample) and i > 0:
                    hp = hh[(i - 1) % 2]
                    V(lambda e, hp=hp: e.tensor_tensor(out=tmp16[:, :, 0], in0=aa[:, :, 0], in1=hp[:, :, 127], op=ALU.mult), [aa, hp], [tmp16])
                    V(lambda e: e.tensor_tensor(out=bb[:, :, 0], in0=bb[:, :, 0], in1=tmp16[:, :, 0], op=ALU.add), [bb, tmp16], [bb])
                for blk in range(8):
                    V(lambda e, blk=blk: e.tensor_tensor_scan(out=h_[:, blk, :], data0=aa[:, blk, :], data1=bb[:, blk, :], initial=0.0, op0=ALU.mult, op1=ALU.add), [aa, bb], [h_])
                if i == NT - 1:
                    ST(h_, o_lrup.rearrange("(k p) -> p k", p=128), h_[:, :, 127], allow_slow_non_contiguous=True)
                if sample:
                    hv = h_[:].rearrange("p k (b t) -> p k b t", b=NB)
                    V(lambda e: e.tensor_copy(out=tmp16[:], in_=hv[:, :, :, 7]), [h_], [tmp16])
                    for blk in range(8):
                        P(lambda e, blk=blk: e.transpose(PS[5 + blk // 4][0:NB, (blk % 4) * 128:(blk % 4 + 1) * 128], tmp16[:, blk, :], ident32[:, :]), [tmp16, ident32], [PS[5 + blk // 4]])
                    for hb in range(2):
                        V(lambda e, hb=hb: e.tensor_copy(out=hlast[:, hb * 512:(hb + 1) * 512], in_=PS[5 + hb][0:NB, :]), [PS[5 + hb]], [hlast])
                    ST(hlast, o_lrus[:, :], hlast[:])
                A(lambda e: e.activation(out=flat(t1), in_=flat(gt), func=AF.Square), [gt], [t1])
                V(lambda e: e.tensor_scalar(out=flat(t1), in0=flat(t1), scalar1=0.044715, scalar2=1.0, op0=ALU.mult, op1=ALU.add), [t1], [t1])
                V(lambda e: e.tensor_tensor(out=flat(t1), in0=flat(t1), in1=flat(gt), op=ALU.mult), [t1, gt], [t1])
                A(lambda e: e.activation(out=flat(t1), in_=flat(t1), func=AF.Tanh, scale=0.7978845608028654), [t1], [t1])
                V(lambda e: e.scalar_tensor_tensor(out=flat(t1), in0=flat(t1), scalar=1.0, in1=flat(gt), op0=ALU.add, op1=ALU.mult), [t1, gt], [t1])
                V(lambda e: e.scalar_tensor_tensor(out=flat(yT), in0=flat(t1), scalar=0.5, in1=flat(h_), op0=ALU.mult, op1=ALU.mult), [t1, h_], [yT])
                xo_ = xo[i % 2]
                for n in range(2):
                    for k in range(8):
                        P(lambda e, n=n, k=k: e.matmul(PS[2 + n][:, :], lhsT=yT[:, k, :], rhs=woutr[:, k, n * 512:(n + 1) * 512], start=(k == 0), stop=(k == 7)), [yT, woutr], [PS[2 + n]])
                    V(lambda e, n=n, x=x, xo_=xo_: e.tensor_tensor(out=xo_[:, n * 512:(n + 1) * 512], in0=PS[2 + n][:, :], in1=x[:, n * 512:(n + 1) * 512], op=ALU.add), [PS[2 + n], x], [xo_])
                ST(xo_, X3[rows(i), :], xo_[:])

            for i in range(NT + 1):
                if os.environ.get("KDBG_SKIPS") == "1" and i == NT:
                    continue
                l1_tile(i)
                if i % 4 == 3:
                    S.end_stage()
            S.end_stage()
            S.release([t for t in S.stage_bufs]); S.stage_bufs = []

        if upto <= 4:
            return nc
        mlp_stage(1, X3, lambda i: (ys[:, :] if i == NT else yp[rows(i), :]), GF1)
    return nc


def _consts():
    f = np.float32
    ident = np.eye(128, dtype=f)
    half = 16
    inv = (np.float32(10000.0) ** (-np.arange(half, dtype=f) / np.float32(half))).astype(f)
    pos = np.concatenate([np.arange(4096), 8192 + (np.arange(128) % 8)]).astype(f)
    ang = (pos[:, None] * inv[None, :]).astype(f)
    cs = np.concatenate([np.cos(ang), np.sin(ang)], axis=1).astype(f)
    s = np.arange(128)[:, None]; t = np.arange(128)[None, :]
    band = np.zeros((20, 128, 128), f)
    bandh = np.zeros((8, 128, 128), f)
    for g, w in enumerate((2, 4, 8, 16)):
        inb = ((t - s) >= 0) & ((t - s) < w)
        band[g] = inb * (1.0 / w) - (s == t)
        band[4 + g] = inb * (1.0 / np.minimum(t + 1, w)) - (s == t)
        band[8 + g] = (((t - (s - 128)) >= 0) & ((t - (s - 128)) < w)) * (1.0 / w)
        sb, ss_ = s // 8, s % 8; tb, tt = t // 8, t % 8
        band[16 + g] = ((sb == tb) & ((tt - ss_) >= 0) & ((tt - ss_) < w)) * (1.0 / w) - (s == t)
        r = np.arange(240)[:, None]
        hb, hr = r // 15, r % 15
        m = ((hb == tb) & (hr >= 16 + tt - w)) * (1.0 / w)
        bandh[2 * g] = m[0:128]
        bandh[2 * g + 1, 0:112] = m[128:240]
    mask = (s <= t).astype(f)
    mnew = np.zeros((128, NB, 8, 8), f)
    for b in range(NB):
        for s_ in range(8):
            for t_ in range(8):
                if s_ <= t_:
                    mnew[b * 8 + s_, b, :, t_] = 1.0
    return dict(c_ident=ident, c_cs=cs,
                c_band=np.ascontiguousarray(band.transpose(1, 0, 2).reshape(128, 20 * 128)),
                c_bandh=np.ascontiguousarray(bandh.transpose(1, 0, 2).reshape(128, 8 * 128)),
                c_mask=mask, c_mnew=np.ascontiguousarray(mnew.reshape(128, NB * 64)),
                c_pcol=np.arange(128, dtype=f).reshape(128, 1))


_NC = None


def kernel(x_prompt, x_sample, cache_ckv, cache_krope, state_pool, state_conv, state_lru, page_table,
           norm_mix, w_in_even, g_q_nope, g_q_rope, g_ckv, g_k_rope, g_k_nope, w_uk, w_uv, w_pool, pool_scale,
           w_out_even, w_in_rnn, conv_w, conv_b, w_gate_a, b_gate_a, w_gate_x, b_gate_x, lru_lambda, w_out_rnn,
           norm_ffn, w_up, w_down):
    global _NC
    f = np.float32
    A_ = lambda a: np.ascontiguousarray(np.asarray(a))
    import os
    upto = int(os.environ.get("KDBG_UPTO", "9"))
    small = os.environ.get("KDBG_SMALL", "0") == "1"
    if _NC is None:
        _NC = build(upto, small)
    nc = _NC

    def pk(v):
        return A_(v).reshape(8, 128).T

    vecs = np.concatenate([
        pk(norm_mix[0]), pk(norm_ffn[0]), pk(norm_mix[1]), pk(norm_ffn[1]),
        A_(pool_scale[0]).reshape(4, 128).T,
        np.concatenate([pk(conv_w[0, k]) for k in range(4)], axis=1),
        pk(conv_b[0]), pk(b_gate_a[0]), pk(b_gate_x[0]), pk(lru_lambda[0])], axis=1).astype(f)
    assert vecs.shape == (128, 100)
    rowv = np.concatenate([A_(g_q_nope[0]), A_(g_q_rope[0]), A_(g_ckv[0]), A_(g_k_rope[0]), A_(g_k_nope[0])]).astype(f)
    rowv = np.ascontiguousarray(np.broadcast_to(rowv[None, :], (128, 448)))
    consts = _consts()
    ckv2 = A_(cache_ckv).reshape(10240 * 128, 256)
    krp2 = A_(cache_krope).reshape(10240 * 128, 32)
    if small:
        ckv2 = ckv2[:256]; krp2 = krp2[:256]
    shared = dict(
        ckv=ckv2, krp=krp2, w_in=A_(w_in_even[0]), w_uk=A_(w_uk[0]).reshape(256, 512),
        w_ukT=np.ascontiguousarray(A_(w_uk[0]).transpose(2, 1, 0).reshape(64, 8 * 256)),
        w_uv=A_(w_uv[0]).reshape(256, 512), w_pool=A_(w_pool[0]), w_out=A_(w_out_even[0]),
        w_inr=A_(w_in_rnn[0]), w_ga=A_(w_gate_a[0]), w_gx=A_(w_gate_x[0]), w_outr=A_(w_out_rnn[0]),
        w_up=A_(w_up), w_dn=A_(w_down), vecs=vecs, rowv=rowv, **consts)
    in_maps = []
    for c in range(8):
        b0 = c * NB
        m = dict(shared)
        m["xp"] = A_(x_prompt[c // 2])
        m["xs"] = A_(x_sample[b0:b0 + NB]).reshape(128, 1024)
        m["ptab"] = np.ascontiguousarray(np.broadcast_to(A_(page_table[b0:b0 + NB]).reshape(1, NB * 64), (128, NB * 64))).astype(np.int32)
        m["spool"] = A_(state_pool[0, b0:b0 + NB]).reshape(NB * 15, 512)
        m["sconv"] = A_(state_conv[0, b0:b0 + NB]).reshape(NB * 3, 1024)
        m["slru"] = A_(state_lru[0, b0:b0 + NB])
        in_maps.append(m)
    res = run_bass_kernel_spmd(nc, in_maps, core_ids=list(range(8))).results
    ev = [res[2 * b] for b in range(4)]
    cat = lambda k: np.concatenate([r[k] for r in res], axis=0)
    y_prompt = np.stack([r["yp"] for r in ev]).astype(f)
    y_sample = cat("ys").reshape(128, 8, 1024)
    return (y_prompt, y_sample,
            np.stack([r["o_ckvp"] for r in ev])[None], np.stack([r["o_krp"] for r in ev])[None],
            np.stack([r["o_poolp"] for r in ev])[None], np.stack([r["o_convp"] for r in ev])[None],
            np.stack([r["o_lrup"].reshape(1024) for r in ev])[None],
            cat("o_ckvs").reshape(1, 128, 8, 256), cat("o_krs").reshape(1, 128, 8, 32),
            cat("o_pools")[None], cat("o_convs")[None], cat("o_lrus")[None])
```
